# Optimizing a Trainium2 kernel written in Bass

```python
import math
import jax, jax.numpy as jnp
from jax import lax
import numpy as np

D_MODEL = 1024
BATCH = 2
SEQ = 8192
DEPTH = 1
DEC_BATCH = 16
DEC_SEQ = 16
PAST_LEN = 2048

CHUNK = 64
D_SSM = 512
SSM_GROUP = 16
N_SSM_GROUPS = D_SSM // SSM_GROUP
SSM_STATE = 64
D_POOL = 256
POOL_WINDOWS = (2, 4, 8, 16)
N_POOL_GROUPS = len(POOL_WINDOWS)
POOL_GROUP = D_POOL // N_POOL_GROUPS
POOL_BUF = max(POOL_WINDOWS) - 1
N_MEM = 256
MEM_HEADS = 4
D_MEM = 256
MEM_HEAD_DIM = D_MEM // MEM_HEADS
D_MIX = D_SSM + D_POOL + D_MEM
N_BRANCH = 3
D_FF = 2816
CONV_W = 3
EPS = 1e-6
DT_MIN = 1e-3
DT_MAX = 1e-1

kernel_name = "hybrid_streaming_encoder_step"

F32 = jnp.float32


def rmsnorm(x, g):
    xf = x.astype(F32)
    y = xf * lax.rsqrt(jnp.mean(xf * xf, axis=-1, keepdims=True) + EPS)
    return (y * g.astype(F32)).astype(x.dtype)


def s5_scan(u, h0_re, h0_im, lam_re, lam_im, log_dt, b_re, b_im, c_re, c_im, d_skip):
    bsz, L, _ = u.shape
    uf = u.astype(F32)
    ug = uf.reshape(bsz, L, N_SSM_GROUPS, SSM_GROUP)
    lr = lam_re.astype(F32)
    li = lam_im.astype(F32)
    dt = jnp.exp(log_dt.astype(F32))[:, None]
    mag = jnp.exp(lr * dt)
    ab_re = mag * jnp.cos(li * dt)
    ab_im = mag * jnp.sin(li * dt)
    den = lr * lr + li * li
    nr = ab_re - 1.0
    ni = ab_im
    z_re = (nr * lr + ni * li) / den
    z_im = (ni * lr - nr * li) / den
    br = b_re.astype(F32)
    bi = b_im.astype(F32)
    bb_re = z_re[..., None] * br - z_im[..., None] * bi
    bb_im = z_re[..., None] * bi + z_im[..., None] * br
    bu_re = jnp.einsum('gnh,blgh->blgn', bb_re, ug)
    bu_im = jnp.einsum('gnh,blgh->blgn', bb_im, ug)
    h0r = h0_re.astype(F32)
    h0i = h0_im.astype(F32)
    bu_re = bu_re.at[:, 0].add(ab_re * h0r - ab_im * h0i)
    bu_im = bu_im.at[:, 0].add(ab_re * h0i + ab_im * h0r)
    a_re = jnp.broadcast_to(ab_re, bu_re.shape)
    a_im = jnp.broadcast_to(ab_im, bu_im.shape)

    def combine(e1, e2):
        a1r, a1i, b1r, b1i = e1
        a2r, a2i, b2r, b2i = e2
        return (a2r * a1r - a2i * a1i,
                a2r * a1i + a2i * a1r,
                a2r * b1r - a2i * b1i + b2r,
                a2r * b1i + a2i * b1r + b2i)

    _, _, hr, hi = lax.associative_scan(combine, (a_re, a_im, bu_re, bu_im), axis=1)
    y = (jnp.einsum('ghn,blgn->blgh', c_re.astype(F32), hr)
         - jnp.einsum('ghn,blgn->blgh', c_im.astype(F32), hi))
    y = y.reshape(bsz, L, D_SSM) + d_skip.astype(F32) * uf
    return y.astype(u.dtype), hr[:, -1].astype(h0_re.dtype), hi[:, -1].astype(h0_im.dtype)


def multiscale_pool(u, buf, pos0, w_pool, pool_scale):
    bsz, L, _ = u.shape
    ext = jnp.concatenate([buf.astype(u.dtype), u], axis=1)
    cs = jnp.cumsum(ext.astype(F32), axis=1)
    cs = jnp.concatenate([jnp.zeros_like(cs[:, :1]), cs], axis=1)
    end = cs[:, POOL_BUF + 1:POOL_BUF + 1 + L]
    pos = pos0 + jnp.arange(L)
    means = []
    for gi, w in enumerate(POOL_WINDOWS):
        sl = slice(gi * POOL_GROUP, (gi + 1) * POOL_GROUP)
        start = cs[:, POOL_BUF + 1 - w:POOL_BUF + 1 - w + L, sl]
        cnt = jnp.minimum(pos + 1, w).astype(F32)[None, :, None]
        means.append((end[..., sl] - start) / cnt)
    diff = jnp.concatenate(means, axis=-1) - u.astype(F32)
    diff = diff.astype(u.dtype).reshape(bsz, L, N_POOL_GROUPS, POOL_GROUP)
    y = jnp.einsum('blgc,gcd->blgd', diff, w_pool).reshape(bsz, L, D_POOL) * pool_scale
    return y, ext[:, -POOL_BUF:]


def memory_kv(mem, g_mem, w_mem_k, w_mem_v):
    bsz = mem.shape[0]
    mn = rmsnorm(mem, g_mem)
    k = (mn @ w_mem_k).reshape(bsz, N_MEM, MEM_HEADS, MEM_HEAD_DIM)
    v = (mn @ w_mem_v).reshape(bsz, N_MEM, MEM_HEADS, MEM_HEAD_DIM)
    return k, v


def memory_attention(q, mk, mv):
    bsz, L, _ = q.shape
    qh = q.reshape(bsz, L, MEM_HEADS, MEM_HEAD_DIM)
    s = jnp.einsum('blhd,bmhd->bhlm', qh.astype(F32), mk.astype(F32)) * (MEM_HEAD_DIM ** -0.5)
    p = jax.nn.softmax(s, axis=-1).astype(mv.dtype)
    o = jnp.einsum('bhlm,bmhd->blhd', p, mv)
    return o.reshape(bsz, L, D_MEM).astype(q.dtype)


def conv_ffn(h, buf, w_ffn_up, w_dw, b_dw, w_ffn_down):
    L = h.shape[1]
    up = h @ w_ffn_up
    a, g = jnp.split(up, 2, axis=-1)
    ext = jnp.concatenate([buf.astype(a.dtype), a], axis=1)
    conv = ext[:, 0:L] * w_dw[0]
    for k in range(1, CONV_W):
        conv = conv + ext[:, k:k + L] * w_dw[k]
    conv = conv + b_dw
    act = jax.nn.gelu(conv) * g
    return act @ w_ffn_down, ext[:, -(CONV_W - 1):]


def layer(x, mk, mv, ssm_re, ssm_im, pool_buf, conv_buf, pos0, p):
    bsz, L, _ = x.shape
    h = rmsnorm(x, p['g_pre1'])
    z = h @ p['w_in']
    u_ssm = z[..., :D_SSM]
    u_pool = z[..., D_SSM:D_SSM + D_POOL]
    q_mem = z[..., D_SSM + D_POOL:]
    y_ssm, ssm_re_new, ssm_im_new = s5_scan(u_ssm, ssm_re, ssm_im, p['ssm_lam_re'], p['ssm_lam_im'],
                                            p['ssm_log_dt'], p['ssm_b_re'], p['ssm_b_im'],
                                            p['ssm_c_re'], p['ssm_c_im'], p['ssm_d'])
    y_ssm = jax.nn.gelu(y_ssm)
    y_ssm = y_ssm * jax.nn.sigmoid(y_ssm @ p['w_glu'] + p['b_glu'])
    y_pool, pool_new = multiscale_pool(u_pool, pool_buf, pos0, p['w_pool'], p['pool_scale'])
    y_mem = memory_attention(q_mem, mk, mv)
    gates = jax.nn.sigmoid(h @ p['w_gate'] + p['b_gate']).reshape(bsz, L, N_BRANCH, D_MODEL)
    merged = (gates[:, :, 0] * (y_ssm @ p['w_br_ssm'])
              + gates[:, :, 1] * (y_pool @ p['w_br_pool'])
              + gates[:, :, 2] * (y_mem @ p['w_br_mem']))
    x = x + rmsnorm(merged @ p['w_out'], p['g_post1'])
    h2 = rmsnorm(x, p['g_pre2'])
    f, conv_new = conv_ffn(h2, conv_buf, p['w_ffn_up'], p['w_dw'], p['b_dw'], p['w_ffn_down'])
    x = x + rmsnorm(f, p['g_post2'])
    return x, ssm_re_new, ssm_im_new, pool_new, conv_new


def setup_inputs(seed: int = 0) -> dict:
    key = jax.random.key(seed)
    ks = iter(jax.random.split(key, 48))

    def nrm(shape, scale=1.0):
        return jax.random.normal(next(ks), shape, F32) * scale

    def gain(shape):
        return 1.0 + nrm(shape, 0.02)

    G, N, H = N_SSM_GROUPS, SSM_STATE, SSM_GROUP
    lam_im0 = jnp.pi * jnp.arange(N, dtype=F32)
    inp = {
        'x_prompt': nrm((BATCH, SEQ, D_MODEL)),
        'x_sample': nrm((DEC_BATCH, DEC_SEQ, D_MODEL)),
        'mem_prompt': nrm((BATCH, N_MEM, D_MODEL)),
        'cache_mem_k': nrm((DEPTH, DEC_BATCH, N_MEM, MEM_HEADS, MEM_HEAD_DIM)),
        'cache_mem_v': nrm((DEPTH, DEC_BATCH, N_MEM, MEM_HEADS, MEM_HEAD_DIM)),
        'state_ssm_re': nrm((DEPTH, DEC_BATCH, G, N), 0.5),
        'state_ssm_im': nrm((DEPTH, DEC_BATCH, G, N), 0.5),
        'state_pool': nrm((DEPTH, DEC_BATCH, POOL_BUF, D_POOL)),
        'state_conv': nrm((DEPTH, DEC_BATCH, CONV_W - 1, D_FF)),
        'g_pre1': gain((DEPTH, D_MODEL)),
        'w_in': nrm((DEPTH, D_MODEL, D_MIX), D_MODEL ** -0.5),
        'ssm_lam_re': -0.5 + nrm((DEPTH, G, N), 0.01),
        'ssm_lam_im': lam_im0 + nrm((DEPTH, G, N), 0.01),
        'ssm_log_dt': jax.random.uniform(next(ks), (DEPTH, G), F32, math.log(DT_MIN), math.log(DT_MAX)),
        'ssm_b_re': nrm((DEPTH, G, N, H), (2.0 * H) ** -0.5),
        'ssm_b_im': nrm((DEPTH, G, N, H), (2.0 * H) ** -0.5),
        'ssm_c_re': nrm((DEPTH, G, H, N), (2.0 * N) ** -0.5),
        'ssm_c_im': nrm((DEPTH, G, H, N), (2.0 * N) ** -0.5),
        'ssm_d': nrm((DEPTH, D_SSM)),
        'w_glu': nrm((DEPTH, D_SSM, D_SSM), D_SSM ** -0.5),
        'b_glu': nrm((DEPTH, D_SSM), 0.02),
        'w_pool': nrm((DEPTH, N_POOL_GROUPS, POOL_GROUP, POOL_GROUP), POOL_GROUP ** -0.5),
        'pool_scale': gain((DEPTH, D_POOL)),
        'g_mem': gain((DEPTH, D_MODEL)),
        'w_mem_k': nrm((DEPTH, D_MODEL, D_MEM), D_MODEL ** -0.5),
        'w_mem_v': nrm((DEPTH, D_MODEL, D_MEM), D_MODEL ** -0.5),
        'w_gate': nrm((DEPTH, D_MODEL, N_BRANCH * D_MODEL), D_MODEL ** -0.5),
        'b_gate': nrm((DEPTH, N_BRANCH * D_MODEL), 0.02),
        'w_br_ssm': nrm((DEPTH, D_SSM, D_MODEL), D_SSM ** -0.5),
        'w_br_pool': nrm((DEPTH, D_POOL, D_MODEL), D_POOL ** -0.5),
        'w_br_mem': nrm((DEPTH, D_MEM, D_MODEL), D_MEM ** -0.5),
        'w_out': nrm((DEPTH, D_MODEL, D_MODEL), D_MODEL ** -0.5),
        'g_post1': gain((DEPTH, D_MODEL)),
        'g_pre2': gain((DEPTH, D_MODEL)),
        'w_ffn_up': nrm((DEPTH, D_MODEL, 2 * D_FF), D_MODEL ** -0.5),
        'w_dw': nrm((DEPTH, CONV_W, D_FF), CONV_W ** -0.5),
        'b_dw': nrm((DEPTH, D_FF), 0.02),
        'w_ffn_down': nrm((DEPTH, D_FF, D_MODEL), D_FF ** -0.5),
        'g_post2': gain((DEPTH, D_MODEL)),
    }
    return inp


def reference(x_prompt, x_sample, mem_prompt, cache_mem_k, cache_mem_v, state_ssm_re, state_ssm_im,
              state_pool, state_conv, g_pre1, w_in, ssm_lam_re, ssm_lam_im, ssm_log_dt, ssm_b_re,
              ssm_b_im, ssm_c_re, ssm_c_im, ssm_d, w_glu, b_glu, w_pool, pool_scale, g_mem, w_mem_k,
              w_mem_v, w_gate, b_gate, w_br_ssm, w_br_pool, w_br_mem, w_out, g_post1, g_pre2,
              w_ffn_up, w_dw, b_dw, w_ffn_down, g_post2):
    xp = x_prompt
    xs = x_sample
    mk_l, mv_l = [], []
    srp_l, sip_l, srs_l, sis_l = [], [], [], []
    pp_l, ps_l, cp_l, cs_l = [], [], [], []
    dt = x_prompt.dtype
    for l in range(DEPTH):
        p = dict(g_pre1=g_pre1[l], w_in=w_in[l], ssm_lam_re=ssm_lam_re[l], ssm_lam_im=ssm_lam_im[l],
                 ssm_log_dt=ssm_log_dt[l], ssm_b_re=ssm_b_re[l], ssm_b_im=ssm_b_im[l],
                 ssm_c_re=ssm_c_re[l], ssm_c_im=ssm_c_im[l], ssm_d=ssm_d[l], w_glu=w_glu[l],
                 b_glu=b_glu[l], w_pool=w_pool[l], pool_scale=pool_scale[l], w_gate=w_gate[l],
                 b_gate=b_gate[l], w_br_ssm=w_br_ssm[l], w_br_pool=w_br_pool[l], w_br_mem=w_br_mem[l],
                 w_out=w_out[l], g_post1=g_post1[l], g_pre2=g_pre2[l], w_ffn_up=w_ffn_up[l],
                 w_dw=w_dw[l], b_dw=b_dw[l], w_ffn_down=w_ffn_down[l], g_post2=g_post2[l])
        mk_p, mv_p = memory_kv(mem_prompt, g_mem[l], w_mem_k[l], w_mem_v[l])
        z_ssm = jnp.zeros((BATCH, N_SSM_GROUPS, SSM_STATE), dt)
        z_pool = jnp.zeros((BATCH, POOL_BUF, D_POOL), dt)
        z_conv = jnp.zeros((BATCH, CONV_W - 1, D_FF), dt)
        xp, sr_p, si_p, pb_p, cb_p = layer(xp, mk_p, mv_p, z_ssm, z_ssm, z_pool, z_conv, 0, p)
        xs, sr_s, si_s, pb_s, cb_s = layer(xs, cache_mem_k[l], cache_mem_v[l], state_ssm_re[l],
                                           state_ssm_im[l], state_pool[l], state_conv[l], PAST_LEN, p)
        mk_l.append(mk_p)
        mv_l.append(mv_p)
        srp_l.append(sr_p)
        sip_l.append(si_p)
        srs_l.append(sr_s)
        sis_l.append(si_s)
        pp_l.append(pb_p)
        ps_l.append(pb_s)
        cp_l.append(cb_p)
        cs_l.append(cb_s)
    return (xp, xs, jnp.stack(mk_l), jnp.stack(mv_l), jnp.stack(srp_l), jnp.stack(sip_l),
            jnp.stack(srs_l), jnp.stack(sis_l), jnp.stack(pp_l), jnp.stack(ps_l),
            jnp.stack(cp_l), jnp.stack(cs_l))
```

```python
import math
from contextlib import ExitStack

import numpy as np
import concourse.bass as bass
import concourse.mybir as mybir
from concourse.bass_utils import run_bass_kernel_spmd

F32 = mybir.dt.float32
BF16 = mybir.dt.bfloat16
ALU = mybir.AluOpType
AF = mybir.ActivationFunctionType

D = 1024
DFF = 2816
NFF = 22
N = 256
L = 8
EPS = 1e-6
NCORES = 8
CPS = 4
HP = 15
HC = 2
UCAP = 4096

def unit_table():
    u = []
    u.append(("kv", "w_kv", 128, 8, [(0, 512)], 2))
    u.append(("in0", "w_in", 128, 8, [(0, 512)], 0))
    u.append(("in1", "w_in", 128, 8, [(512, 512)], 0))
    u.append(("glu", "w_glu", 128, 4, [(0, 512)], None))
    for b in range(3):
        for h in range(2):
            u.append((f"gate{b}{h}", "w_gate", 128, 8, [(b * 1024 + h * 512, 512)], 0))
    for h in range(2):
        u.append((f"brs{h}", "w_br_ssm", 128, 4, [(h * 512, 512)], None))
    u.append(("brp", "w_br_pool", 128, 2, [(0, 1024)], None))
    u.append(("brm", "w_br_mem", 64, 4, [(0, 1024)], None))
    for h in range(2):
        u.append((f"out{h}", "w_out", 128, 8, [(h * 512, 512)], None))
    for q in range(11):
        u.append((f"up{q}", "w_ffn_up", 128, 8, [(q * 256, 256), (DFF + q * 256, 256)], 1))
    for o in range(8):
        u.append((f"dn{o}", "w_ffn_down", 128, NFF, [(o * 128, 128)], None))
    return u


UNITS = unit_table()
UIDX = {u[0]: i for i, u in enumerate(UNITS)}
NU = len(UNITS)


def unit_host(u, W):
    name, key, P, KC, cols, gi = u
    w = W[key]
    parts = [w[:, c0:c0 + n] for (c0, n) in cols]
    w = np.concatenate(parts, axis=1) if len(parts) > 1 else parts[0]
    w = w[:P * KC]
    nc_ = w.shape[1]
    a = w.reshape(KC, P, nc_).transpose(1, 0, 2).reshape(P, KC * nc_)
    out = np.zeros((128, UCAP), np.float32)
    out[:P, :KC * nc_] = a
    return out


class Rec:
    ENG = ("pe", "act", "dve", "pool", "sp")

    def __init__(self):
        self.streams = {e: [] for e in self.ENG}
        self.cnt = {e: 0 for e in self.ENG}
        self.lastw = {}
        self.readers = {}
        self.waited = {e: {} for e in self.ENG}
        self.dcnt = {}
        self.sems = set()
        self.enabled = True

    def _need(self, eng, tok):
        if tok is None:
            return
        if tok[0] == "c":
            _, e, k = tok
            if e == eng and eng == "pe":
                return
            sem, val = "c_" + e, k
        else:
            sem, val = "d_" + tok[1], tok[2]
        if self.waited[eng].get(sem, 0) >= val:
            return
        self.waited[eng][sem] = val
        self.streams[eng].append(("w", sem, val))
        self.sems.add(sem)

    def _deps(self, eng, reads, writes):
        for r in reads:
            self._need(eng, self.lastw.get(r))
        for w in writes:
            self._need(eng, self.lastw.get(w))
            for t in self.readers.get(w, {}).values():
                self._need(eng, t)

    def _reg(self, tok, reads, writes):
        rk = (tok[0], tok[1])
        for r in reads:
            self.readers.setdefault(r, {})[rk] = tok
        for w in writes:
            self.lastw[w] = tok
            self.readers[w] = {}

    def op(self, eng, fn, reads=(), writes=(), signal=True):
        if not self.enabled:
            return
        self._deps(eng, reads, writes)
        if signal:
            self.cnt[eng] += 1
            tok = ("c", eng, self.cnt[eng])
            self.streams[eng].append(("o", fn, "c_" + eng, 1))
        else:
            tok = ("c", eng, self.cnt[eng] + 1)
            self.streams[eng].append(("o", fn, None, 0))
        self.sems.add("c_" + eng)
        self._reg(tok, reads, writes)

    def dma(self, eng, fn, semkey, reads=(), writes=()):
        if not self.enabled:
            return
        n = self.dcnt.get(semkey, 0)
        if n:
            self._need(eng, ("d", semkey, 16 * n))
        self._deps(eng, reads, writes)
        self.dcnt[semkey] = n + 1
        tok = ("d", semkey, 16 * (n + 1))
        self.streams[eng].append(("o", fn, "d_" + semkey, 16))
        self.sems.add("d_" + semkey)
        self._reg(tok, reads, writes)

    def final_wait(self, eng):
        for k, n in self.dcnt.items():
            self._need(eng, ("d", k, 16 * n))


class Rot:
    def __init__(self, items):
        self.items = items
        self.i = 0

    def next(self):
        it = self.items[self.i % len(self.items)]
        self.i += 1
        return it


STOP = None
CP_MODE = 1


def build(M, dbg=False):
    OWN = N * M
    NPRE = (CPS - 1) * M
    NS = N * NPRE + N * M + L
    NF = L + 32
    NYP = N * M + L
    NC = N // L

    nc = bass.Bass("TRN2", target_bir_lowering=False)
    R = Rec()
    st = ExitStack()

    def din(name, shape):
        return nc.dram_tensor(name, list(shape), F32, kind="ExternalInput").ap()

    def dout(name, shape):
        return nc.dram_tensor(name, list(shape), F32, kind="ExternalOutput").ap()

    xT_p = din("xT_p", [128, 8, NS])
    xT_s = din("xT_s", [128, 8, 32])
    memT = din("memT", [128, 8, 256])
    kTs = din("kTs", [2, 128, 2, 256])
    vs_in = din("vs", [2, 128, 2, 256])
    sst_in = din("sst", [64, 2, 32, 2])
    ph_in = din("ph", [128, 2, 2, HP])
    ch_in = din("ch", [128, NFF, 2, HC])
    wsrc = din("wsrc", [NU, 128, UCAP])
    wpool_in = din("wpool", [128, 2, 128])
    NPRM = 24 + 4 + 2 + 8 + 8 + 66 + 22 + 32 + 24
    prm_in = din("prm", [128, NPRM])
    s5p_in = din("s5p", [64, 96 + 4 * 512])
    cst_in = din("cst", [128, 128 + 128 + 32 + 1])

    yT_p = dout("yT_p", [128, 8, NYP])
    yT_s = dout("yT_s", [128, 8, 32])
    kout = dout("kout", [128, 2, 256])
    vout = dout("vout", [128, 2, 256])
    sfin = dout("sfin", [64, 3, 32, 2])
    poolo = dout("poolo", [128, 2, 3, HP])
    convo = dout("convo", [128, NFF, 3, HC])
    wbf = nc.dram_tensor("wbf", [NU, 128, UCAP], BF16, kind="Internal").ap()

    def sb(name, shape, dt=F32):
        return st.enter_context(nc.sbuf_tensor("s_" + name, list(shape), dt))

    def ps(name, shape, dt=F32):
        return st.enter_context(nc.psum_tensor("p_" + name, list(shape), dt))

    xT = sb("xT", [128, 8, N])
    hT = sb("hT", [128, 8, N], BF16)
    sq = [sb(f"sq{i}", [128, N], BF16) for i in range(2)]
    rstd = sb("rstd", [128, N])
    rstdB = sb("rstdB", [128, N])
    zs = sb("zs", [128, 4, N], BF16)
    WP = 3 * HP + NF + 8
    WPm = max(HP + N, WP)
    up = sb("up", [128, 2, WPm])
    b2 = sb("b2", [128, 2, WPm])
    b4 = sb("b4", [128, 2, WPm])
    dfT = sb("dfT", [128, 2, N], BF16)
    ypT = sb("ypT", [128, 2, N], BF16)
    qT = sb("qT", [128, 2, N], BF16)
    Zc = sb("Zc", [64, 4096], BF16)
    U = sb("U", [128, 32, NC], BF16)
    Yg = sb("Yg", [128, 32, NC], BF16)
    NSLOT = NC + 4
    Sall = sb("Sall", [64, NSLOT, 32, 2])
    Sbf = sb("Sbf", [64, NSLOT, 32, 2], BF16)
    ct1 = sb("ct1", [64, 32, 2])
    ct2 = sb("ct2", [64, 32, 2])
    ysT = sb("ysT", [128, 4, N], BF16)
    tb = [sb(f"tb{i}", [128, N], BF16) for i in range(3)]
    yss = sb("yss", [128, 4, N], BF16)
    PT = [sb(f"PT{i}", [128, N], BF16) for i in range(4)]
    rden = sb("rden", [64, N])
    ymT = sb("ymT", [64, 4, N], BF16)
    tf = [sb(f"tf{i}", [128, N]) for i in range(2)]
    mgT = sb("mgT", [128, 8, N], BF16)
    gT = sb("gT", [128, 8, N], BF16)
    m2T = sb("m2T", [128, 8, N])
    WA = max(HC + N, 3 * HC + NF)
    abuf = [sb(f"abuf{i}", [128, WA]) for i in range(2)]
    cbuf = [sb(f"cbuf{i}", [128, WA]) for i in range(2)]
    actT = sb("actT", [128, NFF, N], BF16)
    wsl = [sb(f"wsl{i}", [128, UCAP], BF16) for i in range(3)]
    wstage = sb("wstage", [128, UCAP // 2])
    Tm = sb("Tm", [128, 32, 128], BF16)
    Pm = sb("Pm", [64, 2, 32, 128], BF16)
    Qm = sb("Qm", [128, 2, 32, 64], BF16)
    KT = sb("KT", [128, 3, 2, 256], BF16)
    Vb = sb("Vb", [128, 3, 2, 256], BF16)
    wpl = sb("wpl", [128, 2, 128], BF16)
    prm = sb("prm", [128, NPRM])
    cst = sb("cst", [128, 289])
    onesb = sb("onesb", [128, 128], BF16)
    Sst = sb("Sst", [64, 3, 32, 2])
    phist = sb("phist", [128, 2, 3, HP])
    ahist = sb("ahist", [128, NFF, 3, HC])
    A1 = sb("A1", [64, 32, 2])
    A2n = sb("A2n", [64, 32])
    A2p = sb("A2p", [64, 32])
    A32 = sb("A32", [64, 3, 32])
    accA = sb("accA", [64, 32, 2])
    accB = sb("accB", [64, 32, 2])
    s5p = sb("s5p", [64, 96 + 2048])
    tmpT = sb("tmpT", [128, 4, 128])
    stg = sb("stg", [128, 2, 256])
    sm = Sall[:].rearrange("p a b c -> p (a b c)")[:, 0:1536].rearrange("p (a b) -> p a b", a=48)
    pws = wstage[0:64, 0:1536].rearrange("p (a b c) -> p a b c", a=6, b=32)
    bb0 = stg[0:64].rearrange("p a b -> p (a b)")
    bb1 = wstage[0:64, 1536:2048]

    pm = [ps(f"pm{i}", [128, 512]) for i in range(5)]
    pt = [ps(f"pt{i}", [128, 1024], BF16) for i in range(2)]
    pst = ps("pst", [128, 512])
    pmR = Rot([(pm[i], f"pm{i}") for i in range(5)])
    ptR = Rot([(pt[i], f"pt{i}") for i in range(2)])
    sqR = Rot([(sq[i], f"sq{i}") for i in range(2)])
    tbR = Rot([(tb[i], f"tb{i}") for i in range(3)])
    tfR = Rot([(tf[i], f"tf{i}") for i in range(2)])
    PTR = Rot([(PT[i], f"PT{i}") for i in range(4)])
    abR = Rot([(abuf[i], cbuf[i], f"ab{i}", f"cb{i}") for i in range(2)])
    evR = Rot(["act", "dve"])
    ewR = Rot(["dve", "pool"])

    o = 0
    P_BGATE = o; o += 24
    P_BGLU = o; o += 4
    P_PSC = o; o += 2
    P_GP1 = o; o += 8
    P_GP2 = o; o += 8
    P_WDW = o; o += 66
    P_BDW = o; o += 22
    P_DD = o; o += 32
    P_GV = o; o += 24
    ident = cst[:, 0:128]
    mask = cst[:, 128:256]
    invc = cst[:, 256:288]
    hflag = cst[:, 288:289]

    def E(eng, reads, writes, fn, signal=True):
        if eng != "pe":
            writes = list(writes) + [k for k in reads if k.startswith("pm") or k.startswith("pt") or k == "pst"]
        R.op(eng, fn, reads, writes, signal)

    def tt(eng, out, a, b, op, reads, writes):
        E(eng, reads, writes, lambda e, out=out, a=a, b=b, op=op: e.tensor_tensor(out=out, in0=a, in1=b, op=op))

    def ts(eng, out, a, s1, s2, op0, op1, reads, writes):
        if s2 is None:
            E(eng, reads, writes, lambda e, out=out, a=a, s1=s1, op0=op0:
              e.tensor_single_scalar(out=out, in_=a, scalar=s1, op=op0))
        else:
            E(eng, reads, writes, lambda e, out=out, a=a, s1=s1, s2=s2, op0=op0, op1=op1:
              e.tensor_scalar(out=out, in0=a, scalar1=s1, scalar2=s2, op0=op0, op1=op1))

    def stt(eng, out, a, s, b, op0, op1, reads, writes):
        E(eng, reads, writes, lambda e, out=out, a=a, s=s, b=b, op0=op0, op1=op1:
          e.scalar_tensor_tensor(out=out, in0=a, scalar=s, in1=b, op0=op0, op1=op1))

    def cp(eng, out, a, reads, writes):
        if eng == "act":
            E(eng, reads, writes, lambda e, out=out, a=a: e.copy(out=out, in_=a))
        elif eng == "dve" and CP_MODE == 1:
            E(eng, reads, writes, lambda e, out=out, a=a: e.tensor_single_scalar(out=out, in_=a, scalar=1.0, op=ALU.mult))
        else:
            E(eng, reads, writes, lambda e, out=out, a=a: e.tensor_copy(out=out, in_=a))

    def act(out, a, func, reads, writes, bias=None, scale=None):
        kw = {}
        if bias is not None:
            kw["bias"] = bias
        if scale is not None:
            kw["scale"] = scale
        E("act", reads, writes, lambda e, out=out, a=a, func=func, kw=kw: e.activation(out=out, in_=a, func=func, **kw))

    def mm(out, lhsT, rhs, start, stop, reads, writes, sig=True):
        E("pe", reads, writes, lambda e, out=out, lhsT=lhsT, rhs=rhs, start=start, stop=stop:
          e.matmul(out, lhsT, rhs, start=start, stop=stop), signal=sig)

    def tr(out, in_, idn, reads, writes):
        E("pe", reads, writes, lambda e, out=out, in_=in_, idn=idn: e.transpose(out, in_, idn))

    def dma(out, in_, semkey, reads, writes, eng="sp"):
        R.dma(eng, lambda e, out=out, in_=in_: e.dma_start(out=out, in_=in_), semkey, reads, writes)

    def memset(eng, ap, val, writes):
        E(eng, [], writes, lambda e, ap=ap, val=val: e.memset(ap, val))

    def stage(k):
        if STOP is not None and k >= STOP:
            R.enabled = False

    dma(prm[:], prm_in, "c0", [], ["prm"])
    dma(cst[:], cst_in, "c1", [], ["cst"])
    dma(s5p[:], s5p_in, "c2", [], ["s5p"])
    dma(wstage[:, 0:256], wpool_in.rearrange("p a b -> p (a b)"), "c3", [], ["wstage"])
    cp("dve", wpl[:].rearrange("p a b -> p (a b)"), wstage[:, 0:256], ["wstage"], ["wpl"])
    memset("dve", onesb[:], 1.0, ["onesb"])
    memset("dve", Sst[:, 0], 0.0, ["Sst"])
    memset("pool", phist[:, :, 0], 0.0, ["phist"])
    memset("pool", ahist[:, :, 0], 0.0, ["ahist"])
    dma(Sst[:, 1:3], sst_in, "c4", [], ["Sst"])
    dma(phist[:, :, 1:3], ph_in, "c5", [], ["phist"])
    dma(ahist[:, :, 1:3], ch_in, "c6", [], ["ahist"])

    first_use = ["kv", "in0", "in1", "glu", "gate00", "gate01", "brs0", "brs1", "gate10", "gate11", "brp", "gate20", "gate21",
                 "brm", "out0", "out1"] + [f"up{q}" for q in range(11)] + [f"dn{o}" for o in range(8)]
    assert sorted(first_use) == sorted(UIDX)
    for nm in first_use:
        ui = UIDX[nm]
        u = UNITS[ui]
        tot = u[3] * sum(n for _, n in u[4])
        dma(wbf[ui, :, 0:tot], wsrc[ui, :, 0:tot], f"cv{ui}", [], [f"wbf{ui}"], eng="pool")

    stage(1)
    smi = [0]

    def smn():
        i = smi[0]
        smi[0] += 1
        return sm[:, i, :]

    K_ = ["Sall"]
    lr = s5p[:, 0:32]
    li = s5p[:, 32:64]
    ldt = s5p[:, 64:96]
    dt_ = smn()
    act(dt_, ldt, AF.Exp, ["s5p"], K_)
    x1 = smn(); ang = smn(); q_ = smn(); mag = smn()
    tt("dve", x1, lr, dt_, ALU.mult, ["s5p"] + K_, K_)
    tt("dve", ang, li, dt_, ALU.mult, ["s5p"] + K_, K_)
    ts("dve", q_, x1, 1.0 / 120, None, ALU.mult, None, K_, K_)
    for c in (1.0 / 24, 1.0 / 6, 0.5, 1.0):
        stt("dve", q_, q_, c, x1, ALU.add, ALU.mult, K_, K_)
    ts("dve", mag, q_, 1.0, None, ALU.add, None, K_, K_)
    yq = smn(); u_ = smn(); s_ = smn(); c_ = smn(); t_ = smn()
    MAGIC = 12582912.0
    ts("dve", t_, ang, 1.0 / (2 * math.pi), MAGIC, ALU.mult, ALU.add, K_, K_)
    ts("dve", t_, t_, -MAGIC, None, ALU.add, None, K_, K_)
    stt("dve", yq, t_, -2 * math.pi, ang, ALU.mult, ALU.add, K_, K_)
    ts("dve", t_, yq, math.pi, None, ALU.is_gt, None, K_, K_)
    stt("dve", yq, t_, -2 * math.pi, yq, ALU.mult, ALU.add, K_, K_)
    ts("dve", t_, yq, -math.pi, None, ALU.is_lt, None, K_, K_)
    stt("dve", yq, t_, 2 * math.pi, yq, ALU.mult, ALU.add, K_, K_)
    ts("dve", yq, yq, 0.25, None, ALU.mult, None, K_, K_)
    tt("dve", u_, yq, yq, ALU.mult, K_, K_)
    ts("dve", q_, u_, 1.0 / 362880, None, ALU.mult, None, K_, K_)
    for c in (-1.0 / 5040, 1.0 / 120, -1.0 / 6):
        stt("dve", q_, q_, c, u_, ALU.add, ALU.mult, K_, K_)
    stt("dve", s_, q_, 1.0, yq, ALU.add, ALU.mult, K_, K_)
    ts("dve", q_, u_, -1.0 / 3628800, None, ALU.mult, None, K_, K_)
    for c in (1.0 / 40320, -1.0 / 720, 1.0 / 24, -0.5):
        stt("dve", q_, q_, c, u_, ALU.add, ALU.mult, K_, K_)
    ts("dve", c_, q_, 1.0, None, ALU.add, None, K_, K_)
    for _ in range(2):
        tt("dve", t_, s_, c_, ALU.mult, K_, K_)
        tt("dve", q_, s_, s_, ALU.mult, K_, K_)
        ts("dve", s_, t_, 2.0, None, ALU.mult, None, K_, K_)
        ts("dve", c_, q_, -2.0, 1.0, ALU.mult, ALU.add, K_, K_)
    pwr = [smn() for _ in range(9)]
    pwi = [smn() for _ in range(9)]
    memset("dve", pwr[0], 1.0, K_)
    memset("dve", pwi[0], 0.0, K_)
    tt("dve", pwr[1], mag, c_, ALU.mult, K_, K_)
    tt("dve", pwi[1], mag, s_, ALU.mult, K_, K_)

    def cmul(or_, oi_, ar, ai, br, bi, t1, t2, keys, eng="dve", wkeys=None):
        wk = keys if wkeys is None else wkeys
        tt(eng, t1, ar, br, ALU.mult, keys, wk)
        tt(eng, t2, ai, bi, ALU.mult, keys, wk)
        tt(eng, or_, t1, t2, ALU.subtract, keys + wk, wk)
        tt(eng, t1, ar, bi, ALU.mult, keys, wk)
        tt(eng, t2, ai, br, ALU.mult, keys, wk)
        tt(eng, oi_, t1, t2, ALU.add, keys + wk, wk)

    ta = smn(); tb_ = smn()
    for k in range(2, 9):
        cmul(pwr[k], pwi[k], pwr[k - 1], pwi[k - 1], pwr[1], pwi[1], ta, tb_, K_)
    den = smn(); nr = smn(); zr = smn(); zi = smn()
    tt("dve", den, lr, lr, ALU.mult, ["s5p"] + K_, K_)
    tt("dve", ta, li, li, ALU.mult, ["s5p"] + K_, K_)
    tt("dve", den, den, ta, ALU.add, K_, K_)
    E("dve", K_, K_, lambda e, den=den: e.reciprocal(out=den, in_=den))
    ts("dve", nr, pwr[1], -1.0, None, ALU.add, None, K_, K_)
    tt("dve", ta, nr, lr, ALU.mult, ["s5p"] + K_, K_)
    tt("dve", tb_, pwi[1], li, ALU.mult, ["s5p"] + K_, K_)
    tt("dve", ta, ta, tb_, ALU.add, K_, K_)
    tt("dve", zr, ta, den, ALU.mult, K_, K_)
    tt("dve", ta, pwi[1], lr, ALU.mult, ["s5p"] + K_, K_)
    tt("dve", tb_, nr, li, ALU.mult, ["s5p"] + K_, K_)
    tt("dve", ta, ta, tb_, ALU.subtract, K_, K_)
    tt("dve", zi, ta, den, ALU.mult, K_, K_)
    i7r = smn(); i7i = smn()
    tt("dve", ta, pwr[7], pwr[7], ALU.mult, K_, K_)
    tt("dve", tb_, pwi[7], pwi[7], ALU.mult, K_, K_)
    tt("dve", ta, ta, tb_, ALU.add, K_, K_)
    E("dve", K_, K_, lambda e, ta=ta: e.reciprocal(out=ta, in_=ta))
    tt("dve", i7r, pwr[7], ta, ALU.mult, K_, K_)
    tt("dve", i7i, pwi[7], ta, ALU.mult, K_, K_)
    ts("dve", i7i, i7i, -1.0, None, ALU.mult, None, K_, K_)
    KA = ["Achain"]
    cp("dve", A1[:, :, 0], pwr[8], K_, KA)
    cp("dve", A1[:, :, 1], pwr[8], K_, KA)
    cp("dve", A2p[:], pwi[8], K_, KA)
    ts("dve", A2n[:], pwi[8], -1.0, None, ALU.mult, None, K_, KA)
    KP = ["wstage"]
    zpr = smn(); zpi = smn()
    for i in range(8):
        cp("dve", pws[:, 0, :, i], pwr[7 - i], K_, KP)
        cp("dve", pws[:, 1, :, i], pwi[7 - i], K_, KP)
        cmul(zpr, zpi, pwr[i], pwi[i], i7r, i7i, ta, tb_, K_)
        cp("dve", pws[:, 2, :, i], zpr, K_, KP)
        cp("dve", pws[:, 3, :, i], zpi, K_, KP)
        cp("dve", pws[:, 4, :, i], pwr[i + 1], K_, KP)
        cp("dve", pws[:, 5, :, i], pwi[i + 1], K_, KP)
        bre = s5p[:, 96:608].rearrange("p (g h) -> p g h", h=16)
    bim = s5p[:, 608:1120].rearrange("p (g h) -> p g h", h=16)
    cre = s5p[:, 1120:1632].rearrange("p (g h) -> p g h", h=16)
    cim = s5p[:, 1632:2144].rearrange("p (g h) -> p g h", h=16)
    BBr = bb0.rearrange("p (g h) -> p g h", h=16)
    BBi = bb1.rearrange("p (g h) -> p g h", h=16)
    zrb = zr.unsqueeze(2).broadcast_to([64, 32, 16])
    zib = zi.unsqueeze(2).broadcast_to([64, 32, 16])
    m2f = m2T[0:64].rearrange("p a b -> p (a b)").rearrange("p (a b) -> p a b", a=4)
    xTf = xT[0:64].rearrange("p a b -> p (a b)").rearrange("p (a b) -> p a b", a=4)

    def bgv(i):
        return m2f[:, i, :] if i < 4 else xTf[:, i - 4, :]

    t512a = bgv(6).rearrange("p (g h) -> p g h", h=16)
    t512b = bgv(7).rearrange("p (g h) -> p g h", h=16)
    cmul(BBr, BBi, zrb, zib, bre, bim, t512a, t512b, ["s5p", "stg", "wstage", "m2T", "xT", "Sall"])

    def v4(ap2):
        return ap2.rearrange("p (g i h) -> p g i h", g=4, i=8)

    KG = ["m2T", "xT", "wstage", "stg", "s5p"]
    for e8 in range(8):
        g0 = 4 * e8
        XR, XI, ZR, ZI, PR, PI, TA, TB = [v4(bgv(i)) for i in range(8)]

        def pwb(idx):
            return pws[:, idx, g0:g0 + 4, :].unsqueeze(3).broadcast_to([64, 4, 8, 16])

        def vb(ap3):
            return ap3[:, g0:g0 + 4, :].unsqueeze(2).broadcast_to([64, 4, 8, 16])

        cmul(XR, XI, pwb(0), pwb(1), vb(BBr), vb(BBi), TA, TB, KG)
        cmul(ZR, ZI, pwb(2), pwb(3), vb(cre), vb(cim), TA, TB, KG)
        cmul(PR, PI, pwb(4), pwb(5), vb(cre), vb(cim), TA, TB, KG)
        cp("dve", Pm[:, 0, g0:g0 + 4, :], bgv(4).rearrange("p (g x) -> p g x", g=4), KG, ["Pm"])
        ts("dve", Pm[:, 1, g0:g0 + 4, :], bgv(5).rearrange("p (g x) -> p g x", g=4), -1.0, None,
           ALU.mult, None, KG, ["Pm"])
        ts("dve", bgv(3), bgv(3), -1.0, None, ALU.mult, None, KG, KG)
        pq, pqk = pmR.next()
        for gl in range(4):
            for ri in range(2):
                src = bgv(ri)[:, gl * 128:(gl + 1) * 128]
                tr(pq[:, (gl * 2 + ri) * 64:(gl * 2 + ri + 1) * 64], src, ident[0:64, 0:64], KG + ["cst"], [pqk])
        cp("act", Qm[:, :, g0:g0 + 4, :].rearrange("p r g n -> p g r n"),
           pq[:, :].rearrange("p (g r n) -> p g r n", g=4, r=2), [pqk], ["Qm"])
        pT_, pTk = pmR.next()
        for gl in range(4):
            sl = slice(gl * 128, (gl + 1) * 128)
            mm(pT_[:, sl], bgv(0)[:, sl], bgv(2)[:, sl], True, False, KG, [pTk])
            mm(pT_[:, sl], bgv(1)[:, sl], bgv(3)[:, sl], False, True, KG, [pTk])
        tt("dve", tmpT[:], pT_[:, :].rearrange("p (g x) -> p g x", g=4),
           mask.unsqueeze(1).broadcast_to([128, 4, 128]), ALU.mult, [pTk, "cst"], ["tmpT"])
        for gl in range(4):
            g = g0 + gl
            stt("dve", Tm[:, g, :], ident, prm[:, P_DD + g:P_DD + g + 1], tmpT[:, gl, :], ALU.mult, ALU.add,
                ["cst", "prm", "tmpT"], ["Tm"])

    Pre = s5p[:, 0:1024].rearrange("p (c g) -> p c g", c=32)
    Pim = s5p[:, 1024:2048].rearrange("p (c g) -> p c g", c=32)
    cur_r = smn(); cur_i = smn(); nx_r = smn(); nx_i = smn()
    KT_ = ["Sall", "s5p"]
    memset("pool", cur_r, 1.0, K_)
    memset("pool", cur_i, 0.0, K_)
    ta2 = smn(); tb2 = smn()
    for k in range(32):
        cp("pool", Pre[:, 31 - k, :], cur_r, K_, ["s5p"])
        cp("pool", Pim[:, 31 - k, :], cur_i, K_, ["s5p"])
        cmul(nx_r, nx_i, cur_r, cur_i, pwr[8], pwi[8], ta2, tb2, K_, eng="pool")
        cur_r, cur_i, nx_r, nx_i = nx_r, nx_i, cur_r, cur_i
    cp("pool", A32[:, 0], cur_r, K_, ["A32"])
    cp("pool", A32[:, 1], cur_i, K_, ["A32"])
    ts("pool", A32[:, 2], cur_i, -1.0, None, ALU.mult, None, K_, ["A32"])
    stage(2)
    stage(3)
    wuse = []
    wstate = {"next_load": 0, "slot_of": {}}

    def plan_pass(kind, last_prefix=False):
        if kind == "prefix":
            return ["in0"] + (["in1"] if last_prefix else [])
        lst = ["in0", "in1", "gate10", "gate11", "brp", "gate20", "gate21", "brm", "glu", "gate00", "gate01", "brs0", "brs1"]
        lst += ["out0", "out1"] + [f"up{q}" for q in range(11)] + [f"dn{o}" for o in range(8)]
        return lst

    def issue_loads(upto):
        while wstate["next_load"] < min(upto, len(wuse)):
            i = wstate["next_load"]
            name = wuse[i]
            ui = UIDX[name]
            u = UNITS[ui]
            tot = u[3] * sum(n for _, n in u[4])
            s = i % 3
            dma(wsl[s][:, 0:tot], wbf[ui, :, 0:tot], f"wld{s}", [f"wbf{ui}"], [f"wsl{s}"])
            wstate["next_load"] += 1

    wptr = [0]

    PLAN = [False]

    def wget(name):
        if PLAN[0]:
            wuse.append(name)
            return wsl[0], "wsl0"
        i = wptr[0]
        assert wuse[i] == name, (wuse[i], name, i)
        issue_loads(i + 3)
        wptr[0] += 1
        return wsl[i % 3], f"wsl{i % 3}"

    def norm_stats(src, srck, n, rb=None, rk="rstd"):
        rb = rstd if rb is None else rb
        for kc in range(8):
            s_, sk = sqR.next()
            act(s_[:, 0:n], src[:, kc, 0:n], AF.Square, [srck], [sk])
            mm(pst[:, 0:n], onesb[:], s_[:, 0:n], kc == 0, kc == 7, ["onesb", sk], ["pst"])
        ts("dve", rb[:, 0:n], pst[:, 0:n], 1.0 / D, EPS, ALU.mult, ALU.add, ["pst"], [rk])
        act(rb[:, 0:n], rb[:, 0:n], AF.Sqrt, [rk], [rk])
        E("dve", [rk], [rk], lambda e, n=n, rb=rb: e.reciprocal(out=rb[:, 0:n], in_=rb[:, 0:n]))

    def make_h_act(src, srck, n, gi):
        hT, hk_ = HB["h"], HB["hk"]
        for kc in range(8):
            gcol = prm[:, P_GV + gi * 8 + kc:P_GV + gi * 8 + kc + 1]
            act(hT[:, kc, 0:n], src[:, kc, 0:n], AF.Copy, [srck, "prm"], [hk_], scale=gcol)

    HB = {"h": hT, "hk": "hT", "g": gT, "gk": "gT"}

    def use_h(par):
        if par == 0:
            HB.update(h=hT, hk="hT", g=gT, gk="gT")
        else:
            HB.update(h=gT, hk="gT", g=hT, gk="hT")

    def make_h(src, srck, n, gi):
        hT, hk_ = HB["h"], HB["hk"]
        for kc in range(8):
            gcol = prm[:, P_GV + gi * 8 + kc:P_GV + gi * 8 + kc + 1]
            stt("dve", hT[:, kc, 0:n], src[:, kc, 0:n], gcol, rstd[:, 0:n], ALU.mult, ALU.mult, [srck, "rstd", "prm"], [hk_])

    def proj(wt, wk, kcs, colsel, rhs_fn, rhs_keys, n, part=128):
        p_, pk = pmR.next()
        for i, kc in enumerate(kcs):
            mm(p_[:, 0:n], wt(kc, colsel), rhs_fn(kc), i == 0, i == len(kcs) - 1, [wk] + rhs_keys, [pk],
               sig=(i == len(kcs) - 1))
        return p_, pk

    def wview(wt, kcn, ncol, P_=128):
        v = wt[0:P_, 0:kcn * ncol].rearrange("p (k c) -> p k c", k=kcn)
        return lambda kc, cs: v[:, kc, cs]

    CH = "pool"

    def s5_front(segs, n, prefix=False):
        ncn = n // L
        evn = (lambda: "act") if prefix else evR.next
        for i0 in range(0, 8, 2):
            p_, pk = ptR.next()
            for ii in range(2):
                for f in range(4):
                    src = zs[:, f, 0:n].rearrange("p (c i) -> p i c", i=8)[:, i0 + ii, :]
                    tr(p_[0:ncn, (ii * 4 + f) * 128:(ii * 4 + f + 1) * 128], src, identb[:, :], ["zs", "identb"], [pk])
            cp(evn(), Zc[0:ncn, :].rearrange("p (g i h) -> p i g h", g=32, i=8)[:, i0:i0 + 2, :, :],
               p_[0:ncn, :].rearrange("p (i g h) -> p i g h", i=2, g=32), [pk], ["Zc"])
        for gh in range(2):
            p_, pk = ptR.next()
            for gl in range(16):
                g = gh * 16 + gl
                src = Zc[0:ncn, g * 128:(g + 1) * 128]
                tr(p_[:, gl * 64:gl * 64 + ncn], src, identb[0:ncn, 0:ncn], ["Zc", "identb"], [pk])
            cp(evn(), U[:, gh * 16:(gh + 1) * 16, 0:ncn],
               p_[:, :].rearrange("p (g c) -> p g c", g=16)[:, :, 0:ncn], [pk], ["U"])
        for g0 in range(0, 32, 4):
            p_, pk = pmR.next()
            for gl in range(4):
                for ri in range(2):
                    mm(p_[0:64, (gl * 2 + ri) * 64:(gl * 2 + ri) * 64 + ncn], Qm[:, ri, g0 + gl, :], U[:, g0 + gl, 0:ncn],
                       True, True, ["Qm", "U"], [pk])
            pv = p_[0:64, :].rearrange("p (g r c) -> p c g r", g=4, r=2)
            for sg in segs:
                c0 = sg["c0"] // L
                cn = sg["n"] // L
                cp(evn(), Sall[:, sg["slot"] + 1:sg["slot"] + 1 + cn, g0:g0 + 4, :], pv[:, c0:c0 + cn, :, :],
                   [pk, ], ["Sall"])
        if prefix:
            return
        for sg in segs:
            b = sg["slot"]
            cn = sg["n"] // L
            cp(CH, Sall[:, b], Sst[:, sg["sid"]], ["Sst"], ["Sall"])
            for c in range(cn):
                s0 = Sall[:, b + c]
                s1 = Sall[:, b + c + 1]
                tt(CH, ct1[:], s0, A1[:], ALU.mult, ["Sall", "Achain"], ["ct1"])
                tt(CH, ct2[:, :, 0], s0[:, :, 1], A2n[:], ALU.mult, ["Sall", "Achain"], ["ct2"])
                tt(CH, ct2[:, :, 1], s0[:, :, 0], A2p[:], ALU.mult, ["Sall", "Achain"], ["ct2"])
                tt(CH, s1, s1, ct1[:], ALU.add, ["Sall", "ct1"], ["Sall"])
                tt(CH, s1, s1, ct2[:], ALU.add, ["Sall", "ct2"], ["Sall"])
            cp(CH, Sst[:, sg["sid"]], Sall[:, b + cn], ["Sall"], ["Sst"])

    def prefix_wsum():
        Wv = Sall[:, 1:33]
        t1 = m2T[0:64].rearrange("p a b -> p (a b)").rearrange("p (c g r) -> p c g r", c=32, g=32)
        S0 = Sst[:, 0]
        for tab, acc in ((Pre, accA), (Pim, accB)):
            tt("dve", t1, Wv, tab.unsqueeze(3).broadcast_to([64, 32, 32, 2]), ALU.mult, ["Sall", "s5p"], ["m2T"])
            E("dve", ["m2T"], ["acc"], lambda e, acc=acc, t1=t1: e.tensor_reduce(
                out=acc[:], in_=t1.rearrange("p c g r -> p g r c"), axis=mybir.AxisListType.X, op=ALU.add))
        tt("dve", ct1[:], S0, A32[:, 0].unsqueeze(2).broadcast_to([64, 32, 2]), ALU.mult, ["Sst", "A32"], ["ct1"])
        tt("dve", ct2[:, :, 0], S0[:, :, 1], A32[:, 2], ALU.mult, ["Sst", "A32"], ["ct2"])
        tt("dve", ct2[:, :, 1], S0[:, :, 0], A32[:, 1], ALU.mult, ["Sst", "A32"], ["ct2"])
        tt("dve", S0, ct1[:], ct2[:], ALU.add, ["ct1", "ct2"], ["Sst"])
        tt("dve", S0, S0, accA[:], ALU.add, ["Sst", "acc"], ["Sst"])
        tt("dve", S0[:, :, 0], S0[:, :, 0], accB[:, :, 1], ALU.subtract, ["Sst", "acc"], ["Sst"])
        tt("dve", S0[:, :, 1], S0[:, :, 1], accB[:, :, 0], ALU.add, ["Sst", "acc"], ["Sst"])

    def s5_back(segs, n, wglu, wgk):
        ncn = n // L
        nslot = segs[-1]["slot"] + segs[-1]["n"] // L + 1
        cp("act", Sbf[:, 0:nslot], Sall[:, 0:nslot], ["Sall"], ["Sbf"])
        for g0 in range(0, 32, 8):
            p_, pk = pmR.next()
            for gl in range(8):
                g = g0 + gl
                mm(p_[:, gl * 64:gl * 64 + ncn], Tm[:, g, :], U[:, g, 0:ncn], True, False, ["Tm", "U"], [pk])
                for si, sg in enumerate(segs):
                    c0 = sg["c0"] // L
                    cn = sg["n"] // L
                    for ri in range(2):
                        last = (si == len(segs) - 1) and ri == 1
                        mm(p_[:, gl * 64 + c0:gl * 64 + c0 + cn], Pm[:, ri, g, :],
                           Sbf[:, sg["slot"]:sg["slot"] + cn, g, ri], False, last, ["Pm", "Sbf"], [pk])
            act(Yg[:, g0:g0 + 8, 0:ncn], p_[:, :].rearrange("p (g c) -> p g c", g=8)[:, :, 0:ncn], AF.Gelu, [pk], ["Yg"])
        for g0 in range(0, 32, 8):
            p_, pk = ptR.next()
            for gl in range(8):
                tr(p_[0:ncn, gl * 128:(gl + 1) * 128], Yg[:, g0 + gl, 0:ncn], identb[:, :], ["Yg", "identb"], [pk])
            cp(evR.next(), Zc[0:ncn, :].rearrange("p (j g h) -> p g j h", j=8, g=32)[:, g0:g0 + 8, :, :],
               p_[0:ncn, :].rearrange("p (g j h) -> p g j h", g=8, j=8), [pk], ["Zc"])
        for j0 in range(0, 8, 2):
            p_, pk = ptR.next()
            for jj in range(2):
                for f in range(4):
                    tr(p_[:, (jj * 4 + f) * 64:(jj * 4 + f) * 64 + ncn], Zc[0:ncn, (j0 + jj) * 512 + f * 128:(j0 + jj) * 512 + (f + 1) * 128],
                       identb[0:ncn, 0:ncn], ["Zc", "identb"], [pk])
            src = p_[:, 0:512].rearrange("p (j f c) -> p f c j", j=2, f=4)[:, :, 0:ncn, :]
            dst = ysT[:, :, 0:n].rearrange("p f (c j) -> p f c j", j=8)[:, :, :, j0:j0 + 2]
            cp(evR.next(), dst, src, [pk], ["ysT"])
        wv = wview(wglu, 4, 512)
        for ot in range(4):
            p_, pk = proj(wv, wgk, range(4), slice(ot * 128, (ot + 1) * 128), lambda kc: ysT[:, kc, 0:n], ["ysT"], n)
            t_, tk = tbR.next()
            act(t_[:, 0:n], p_[:, 0:n], AF.Sigmoid, [pk, "prm"], [tk], bias=prm[:, P_BGLU + ot:P_BGLU + ot + 1])
            tt("dve", yss[:, ot, 0:n], ysT[:, ot, 0:n], t_[:, 0:n], ALU.mult, ["ysT", tk], ["yss"])

    def pooling(segs, n, wtot, first_main):
        W = wtot
        for sg in segs:
            cp("pool", up[:, :, sg["po"] - HP:sg["po"]], phist[:, :, sg["sid"], :], ["phist"], ["up"])
        tt("dve", b2[:, :, 1:W], up[:, :, 1:W], up[:, :, 0:W - 1], ALU.add, ["up"], ["b2"])
        tt("pool", b4[:, :, 3:W], b2[:, :, 3:W], b2[:, :, 1:W - 2], ALU.add, ["b2"], ["b4"])
        tt("dve", b2[:, 1, 7:W], b4[:, 1, 7:W], b4[:, 1, 3:W - 4], ALU.add, ["b4"], ["b2"])
        tt("pool", b4[:, 1, 15:W], b2[:, 1, 15:W], b2[:, 1, 7:W - 8], ALU.add, ["b2"], ["b4"])
        srcs = [(b2, 0, 0, 64, 2), (b4, 0, 64, 128, 4), (b2, 1, 0, 64, 8), (b4, 1, 64, 128, 16)]
        for sg in segs:
            po, c0, sn = sg["po"], sg["c0"], sg["n"]
            for (bt, tl, p0, p1, w) in srcs:
                stt("dve", dfT[p0:p1, tl, c0:c0 + sn], bt[p0:p1, tl, po:po + sn], 1.0 / w, up[p0:p1, tl, po:po + sn],
                    ALU.mult, ALU.subtract, ["b2", "b4", "up"], ["dfT"])
            if first_main and sg["sid"] == 0:
                for (bt, tl, p0, p1, w) in srcs:
                    t_, tk = tfR.next()
                    tt("dve", t_[p0:p1, 0:16], bt[p0:p1, tl, po + 8:po + 24], invc[p0:p1, tl * 16:(tl + 1) * 16], ALU.mult,
                       ["b2", "b4", "cst"], [tk])
                    tt("dve", dfT[p0:p1, tl, c0 + 8:c0 + 24], t_[p0:p1, 0:16], up[p0:p1, tl, po + 8:po + 24], ALU.subtract,
                       [tk, "up"], ["dfT"])
            cp("pool", phist[:, :, sg["sid"], :], up[:, :, po + sn - HP:po + sn], ["up"], ["phist"])
        for tl in range(2):
            p_, pk = pmR.next()
            mm(p_[:, 0:n], wpl[:, tl, :], dfT[:, tl, 0:n], True, True, ["wpl", "dfT"], [pk])
            ts("dve", ypT[:, tl, 0:n], p_[:, 0:n], prm[:, P_PSC + tl:P_PSC + tl + 1], None, ALU.mult, None, [pk, "prm"], ["ypT"])

    def attention(segs, n):
        def s1(sg, h):
            c0, sn, kv = sg["c0"], sg["n"], sg["kv"]
            hq, r0 = h // 2, (h % 2) * 64
            pts = []
            for mc in range(2):
                p_, pk = pmR.next()
                mm(p_[:, 0:sn], KT[r0:r0 + 64, kv, hq, mc * 128:(mc + 1) * 128], qT[r0:r0 + 64, hq, c0:c0 + sn], True, True,
                   ["KT", "qT"], [pk])
                t_, tk = PTR.next()
                act(t_[:, 0:sn], p_[:, 0:sn], AF.Exp, [pk], [tk], scale=0.125)
                pts.append((t_, tk))
            return pts

        def s2(sg, h, pts):
            c0, sn, kv = sg["c0"], sg["n"], sg["kv"]
            po_, pok = pmR.next()
            pd_, pdk = pmR.next()
            for mc in range(2):
                mm(po_[0:64, 0:sn], Vb[:, kv, mc, h * 64:(h + 1) * 64], pts[mc][0][:, 0:sn], mc == 0, mc == 1,
                   ["Vb", pts[mc][1]], [pok])
            for mc in range(2):
                mm(pd_[0:64, 0:sn], onesb[:, 0:64], pts[mc][0][:, 0:sn], mc == 0, mc == 1, ["onesb", pts[mc][1]], [pdk])
            E("dve", [pdk], ["rden"], lambda e, pd_=pd_, sn=sn: e.reciprocal(out=rden[:, 0:sn], in_=pd_[0:64, 0:sn]))
            tt("dve", ymT[:, h, c0:c0 + sn], po_[0:64, 0:sn], rden[:, 0:sn], ALU.mult, [pok, "rden"], ["ymT"])

        items = [(sg, h) for sg in segs for h in range(4)]
        pend = None
        for (sg, h) in items:
            pts = s1(sg, h)
            if pend is not None:
                s2(*pend)
            pend = (sg, h, pts)
        s2(*pend)

    def merge_branch(b, role, n):
        brdefs = {
            0: (["brs0", "brs1"], 4, 512, 128, lambda kc: yss[:, kc, 0:n], ["yss"]),
            1: (["brp"], 2, 1024, 128, lambda kc: ypT[:, kc, 0:n], ["ypT"]),
            2: (["brm"], 4, 1024, 64, lambda kc: ymT[:, kc, 0:n], ["ymT"]),
        }
        brn, kcn, bcols, bp, rfn, rkeys = brdefs[b]
        hT, hk_, gT, gk_ = HB["h"], HB["hk"], HB["g"], HB["gk"]
        for hf in range(2):
            gwt, gwk = wget(f"gate{b}{hf}")
            gv = wview(gwt, 8, 512)
            for o4 in range(4):
                ot = hf * 4 + o4
                pg, pgk = proj(gv, gwk, range(8), slice(o4 * 128, (o4 + 1) * 128), lambda kc: hT[:, kc, 0:n], [hk_], n)
                act(gT[:, ot, 0:n], pg[:, 0:n], AF.Sigmoid, [pgk, "prm"], [gk_],
                    bias=prm[:, P_BGATE + b * 8 + ot:P_BGATE + b * 8 + ot + 1])
        for bi, nm in enumerate(brn):
            bwt, bwk = wget(nm)
            bv = wview(bwt, kcn, bcols, bp)
            ots = range(bi * 4, bi * 4 + 4) if len(brn) == 2 else range(8)
            for ot in ots:
                csel = slice((ot % 4) * 128, (ot % 4 + 1) * 128) if len(brn) == 2 else slice(ot * 128, (ot + 1) * 128)
                pb, pbk = proj(bv, bwk, range(kcn), csel, rfn, rkeys, n)
                if role == "first":
                    tt("dve", m2T[:, ot, 0:n], pb[:, 0:n], gT[:, ot, 0:n], ALU.mult, [pbk, gk_], ["m2T"])
                else:
                    f_, fk = tfR.next()
                    tt("dve", f_[:, 0:n], pb[:, 0:n], gT[:, ot, 0:n], ALU.mult, [pbk, gk_], [fk])
                    if role == "mid":
                        tt("dve", m2T[:, ot, 0:n], m2T[:, ot, 0:n], f_[:, 0:n], ALU.add, ["m2T", fk], ["m2T"])
                    else:
                        tt("dve", mgT[:, ot, 0:n], m2T[:, ot, 0:n], f_[:, 0:n], ALU.add, ["m2T", fk], ["mgT"])

    def merge_out(n):
        for hf in range(2):
            wt_, wk_ = wget(f"out{hf}")
            wv = wview(wt_, 8, 512)
            for o4 in range(4):
                ot = hf * 4 + o4
                p_, pk = proj(wv, wk_, range(8), slice(o4 * 128, (o4 + 1) * 128), lambda kc: mgT[:, kc, 0:n], ["mgT"], n)
                cp("act", m2T[:, ot, 0:n], p_[:, 0:n], [pk], ["m2T"])

    def resid_norm(n, gofs, xb, xk):
        norm_stats(m2T, "m2T", n)
        for ot in range(8):
            f_, fk = tfR.next()
            stt("dve", f_[:, 0:n], m2T[:, ot, 0:n], prm[:, gofs + ot:gofs + ot + 1], rstd[:, 0:n], ALU.mult, ALU.mult,
                ["m2T", "prm", "rstd"], [fk])
            tt("dve", xb[:, ot, 0:n], xb[:, ot, 0:n], f_[:, 0:n], ALU.add, [xk, fk], [xk])

    def ffn_up(segs, n, first_main, xb, xk):
        norm_stats(xb, xk, n)
        make_h(xb, xk, n, 1)
        hT, hk_ = HB["h"], HB["hk"]
        pend = []

        def stage_b(j, pg, pgk, cb, cbk):
            t_, tk = tbR.next()
            for sg in segs:
                ao, c0, sn = sg["ao"], sg["c0"], sg["n"]
                act(t_[:, c0:c0 + sn], cb[:, ao:ao + sn], AF.Gelu, [cbk], [tk])
            tt("dve", actT[:, j, 0:n], pg[:, 0:n], t_[:, 0:n], ALU.mult, [pgk, tk], ["actT"])

        for q in range(11):
            wt_, wk_ = wget(f"up{q}")
            wv = wview(wt_, 8, 512)
            for jj in range(2):
                j = 2 * q + jj
                pa, pak = proj(wv, wk_, range(8), slice(jj * 128, (jj + 1) * 128), lambda kc: hT[:, kc, 0:n], [hk_], n)
                pg, pgk = proj(wv, wk_, range(8), slice(256 + jj * 128, 256 + (jj + 1) * 128), lambda kc: hT[:, kc, 0:n], [hk_], n)
                ab, cb, abk, cbk = abR.next()
                wtot = segs[-1]["ao"] + segs[-1]["n"]
                for sg in segs:
                    ao, c0, sn = sg["ao"], sg["c0"], sg["n"]
                    cp("act", ab[:, ao:ao + sn], pa[:, c0:c0 + sn], [pak], [abk])
                    cp("act", ab[:, ao - HC:ao], ahist[:, j, sg["sid"], :], ["ahist"], [abk])
                    if first_main and sg["sid"] == 0:
                        act(ab[:, ao:ao + 8], ab[:, ao:ao + 8], AF.Copy, [abk, "cst"], [abk], scale=hflag)
                    cp("act", ahist[:, j, sg["sid"], :], ab[:, ao + sn - HC:ao + sn], [abk], ["ahist"])
                w0 = prm[:, P_WDW + j * 3 + 0:P_WDW + j * 3 + 1]
                w1 = prm[:, P_WDW + j * 3 + 1:P_WDW + j * 3 + 2]
                w2 = prm[:, P_WDW + j * 3 + 2:P_WDW + j * 3 + 3]
                bd = prm[:, P_BDW + j:P_BDW + j + 1]
                act(cb[:, HC:wtot], ab[:, HC:wtot], AF.Identity, [abk, "prm"], [cbk], bias=bd, scale=w2)
                stt("dve", cb[:, HC:wtot], ab[:, HC - 1:wtot - 1], w1, cb[:, HC:wtot], ALU.mult, ALU.add, [abk, cbk, "prm"], [cbk])
                stt("dve", cb[:, HC:wtot], ab[:, HC - 2:wtot - 2], w0, cb[:, HC:wtot], ALU.mult, ALU.add, [abk, cbk, "prm"], [cbk])
                if pend:
                    stage_b(*pend.pop())
                pend.append((j, pg, pgk, cb, cbk))
        stage_b(*pend.pop())

    def ffn_down(n, xb, xk):
        for ot in range(8):
            wt_, wk_ = wget(f"dn{ot}")
            wv = wview(wt_, NFF, 128)
            p_, pk = proj(wv, wk_, range(NFF), slice(0, 128), lambda kc: actT[:, kc, 0:n], ["actT"], n)
            cp("act", m2T[:, ot, 0:n], p_[:, 0:n], [pk], ["m2T"])
        resid_norm(n, P_GP2, xb, xk)

    xbufs = [(xT, "xT"), (wstage[:, :].rearrange("p (a b) -> p a b", a=8), "wstage")]
    loaded = set()

    def load_x(d):
        if d["id"] in loaded:
            return
        loaded.add(d["id"])
        xb, xk = xbufs[d["par"]]
        for (src_ap, c0, sn) in d["xsrc"]:
            dma(xb[:, :, c0:c0 + sn], src_ap, "xld", [], [xk])

    def front(d, nxt=None, part="all"):
        kind, segs, n = d["kind"], d["segs"], d["n"]
        xb, xk = xbufs[d["par"]]
        use_h(d["par"])
        hT, hk_ = HB["h"], HB["hk"]
        if part in ("all", "A1"):
            load_x(d)
            if nxt is not None and kind == "prefix":
                load_x(nxt)
            if kind == "prefix":
                rbp, rkp = [(rstd, "rstd"), (rstdB, "rstdB")][d["par"]]
                make_h_act(xb, xk, n, 0)
                norm_stats(xb, xk, n, rbp, rkp)
            else:
                norm_stats(xb, xk, n)
                make_h(xb, xk, n, 0)
            if part == "A1":
                return
        wt_, wk_ = wget("in0")
        wv = wview(wt_, 8, 512)
        for ot in range(4):
            p_, pk = proj(wv, wk_, range(8), slice(ot * 128, (ot + 1) * 128), lambda kc: hT[:, kc, 0:n], [hk_], n)
            if kind == "prefix":
                rbp, rkp = [(rstd, "rstd"), (rstdB, "rstdB")][d["par"]]
                tt("dve", zs[:, ot, 0:n], p_[:, 0:n], rbp[:, 0:n], ALU.mult, [pk, rkp], ["zs"])
            else:
                cp("act", zs[:, ot, 0:n], p_[:, 0:n], [pk], ["zs"])
        if kind != "prefix" or d["last_prefix"]:
            wt_, wk_ = wget("in1")
            wv = wview(wt_, 8, 512)
            for ot in range(4):
                if kind == "prefix" and ot >= 2:
                    break
                p_, pk = proj(wv, wk_, range(8), slice(ot * 128, (ot + 1) * 128), lambda kc: hT[:, kc, 0:n], [hk_], n)
                if ot < 2:
                    for sg in segs:
                        if kind == "prefix":
                            rbp, rkp = [(rstd, "rstd"), (rstdB, "rstdB")][d["par"]]
                            tt("dve", up[:, ot, sg["po"]:sg["po"] + sg["n"]], p_[:, sg["c0"]:sg["c0"] + sg["n"]],
                               rbp[:, sg["c0"]:sg["c0"] + sg["n"]], ALU.mult, [pk, rkp], ["up"])
                        else:
                            cp("dve", up[:, ot, sg["po"]:sg["po"] + sg["n"]], p_[:, sg["c0"]:sg["c0"] + sg["n"]], [pk], ["up"])
                else:
                    cp("dve", qT[:, ot - 2, 0:n], p_[:, 0:n], [pk], ["qT"])
        if kind != "prefix":
            pooling(segs, n, segs[-1]["po"] + segs[-1]["n"], d["first_main"])
        s5_front(segs, n, prefix=(kind == "prefix"))
        if kind == "prefix" and d["last_prefix"]:
            sg = segs[0]
            cp("pool", phist[:, :, 0, :], up[:, :, sg["po"] + n - HP:sg["po"] + n], ["up"], ["phist"])

    def back_a1(d, nxt=None):
        segs, n = d["segs"], d["n"]
        use_h(d["par"])
        if nxt is not None:
            load_x(nxt)
        attention(segs, n)
        merge_branch(1, "first", n)
        merge_branch(2, "mid", n)
        wg_, wgk_ = wget("glu")
        s5_back(segs, n, wg_, wgk_)
        merge_branch(0, "last", n)
        merge_out(n)

    def back_a2(d):
        segs, n = d["segs"], d["n"]
        xb, xk = xbufs[d["par"]]
        use_h(d["par"])
        resid_norm(n, P_GP1, xb, xk)
        ffn_up(segs, n, d["first_main"], xb, xk)

    def back_b(d):
        n = d["n"]
        xb, xk = xbufs[d["par"]]
        ffn_down(n, xb, xk)
        for (dst_ap, c0, sn) in d["ydst"]:
            dma(dst_ap, xb[:, :, c0:c0 + sn], "yst", [xk], [])

    def drive(descs):
        loaded.clear()
        pre = [d for d in descs if d["kind"] == "prefix"]
        mains = [d for d in descs if d["kind"] != "prefix"]
        npre = len(pre)

        def a1(i):
            front(pre[i], pre[i + 1] if i + 1 < npre else mains[0], part="A1")

        if npre:
            a1(0)
            if npre > 1:
                a1(1)
            front(pre[0], part="A2")
            for i in range(1, npre):
                if i + 1 < npre:
                    a1(i + 1)
                prefix_wsum()
                front(pre[i], part="A2")
            prefix_wsum()
        front(mains[0])
        for j, d in enumerate(mains):
            nxt = mains[j + 1] if j + 1 < len(mains) else None
            back_a1(d, nxt)
            if nxt is not None:
                front(nxt)
            back_a2(d)
            back_b(d)

    identb = sb("identb", [128, 128], BF16)
    cp("dve", identb[:], ident, ["cst"], ["identb"])
    descs = []
    col = 0
    ycol = 0
    for p in range(NPRE + M):
        kind = "prefix" if p < NPRE else "main"
        descs.append(dict(id=p, kind=kind, par=p % 2, n=N, last_prefix=(p == NPRE - 1), first_main=(p == NPRE),
                          segs=[dict(sid=0, kv=0, c0=0, n=N, slot=0, po=HP, ao=HC)],
                          xsrc=[(xT_p[:, :, col:col + N], 0, N)],
                          ydst=[(yT_p[:, :, ycol:ycol + N], 0, N)] if kind == "main" else []))
        col += N
        if kind == "main":
            ycol += N
    fsegs = []
    c0 = slot = po = ao = 0
    for sid, sn in ((0, L), (1, 16), (2, 16)):
        po += HP
        ao += HC
        fsegs.append(dict(sid=sid, kv=sid, c0=c0, n=sn, slot=slot, po=po, ao=ao))
        c0 += sn
        slot += sn // L + 1
        po += sn
        ao += sn
    descs.append(dict(id=NPRE + M, kind="main", par=(NPRE + M) % 2, n=NF, last_prefix=False, first_main=False, segs=fsegs,
                      xsrc=[(xT_p[:, :, col:col + L], 0, L), (xT_s, L, 32)],
                      ydst=[(yT_p[:, :, ycol:ycol + L], 0, L), (yT_s, L, 32)]))
    wuse.append("kv")
    PLAN[0] = True
    R.enabled = False
    drive(descs)
    R.enabled = True
    PLAN[0] = False


    stage(4)
    dma(xT[:, :, 0:256], memT, "xld", [], ["xT"])
    norm_stats(xT, "xT", 256)
    stage(4.1)
    make_h(xT, "xT", 256, 2)
    stage(4.2)
    wt_, wk_ = wget("kv")
    stage(4.3)
    wv = wview(wt_, 8, 512)
    for hq in range(2):
        p_, pk = proj(wv, wk_, range(8), slice(hq * 128, (hq + 1) * 128), lambda kc: hT[:, kc, 0:256], ["hT"], 256)
        stage(4.31)
        cp("act", KT[:, 0, hq, :], p_[:, 0:256], [pk], ["KT"])
        stage(4.32)
        cp("dve", stg[:, hq, :], p_[:, 0:256], [pk], ["stg"])
        stage(4.33)
    dma(kout, stg[:], "o_k", ["stg"], [])
    stage(4.4)
    for mc in range(2):
        p_, pk = pmR.next()
        for kc in range(8):
            mm(p_[:, 0:256], hT[:, kc, mc * 128:(mc + 1) * 128], wv(kc, slice(256, 512)), kc == 0, kc == 7, ["hT", wk_], [pk])
        cp("act", Vb[:, 0, mc, :], p_[:, 0:256], [pk], ["Vb"])
        cp("dve", stg[:, mc, :], p_[:, 0:256], [pk], ["stg"])
    dma(vout, stg[:], "o_v", ["stg"], [])
    stage(4.5)
    for s in range(2):
        dma(stg[:], kTs[s], "c7", [], ["stg"])
        cp("dve", KT[:, 1 + s].rearrange("p a b -> p (a b)"), stg[:].rearrange("p a b -> p (a b)"), ["stg"], ["KT"])
        dma(stg[:], vs_in[s], "c7", [], ["stg"])
        cp("dve", Vb[:, 1 + s].rearrange("p a b -> p (a b)"), stg[:].rearrange("p a b -> p (a b)"), ["stg"], ["Vb"])

    stage(5)
    drive(descs)
    R.enabled = True
    dma(sfin, Sst[:], "o_s", ["Sst"], [])
    dma(poolo, phist[:], "o_p", ["phist"], [])
    dma(convo, ahist[:], "o_c", ["ahist"], [])
    R.final_wait("sp")

    sems = {name: st.enter_context(nc.semaphore(name)) for name in sorted(R.sems)}
    with nc.Block() as block:
        def mk(stream):
            def body(eng):
                for it in stream:
                    if it[0] == "w":
                        eng.wait_ge(sems[it[1]], it[2])
                    else:
                        ins = it[1](eng)
                        if it[2] is not None:
                            ins.then_inc(sems[it[2]], it[3])
            return body

        block.tensor(mk(R.streams["pe"]))
        block.scalar(mk(R.streams["act"]))
        block.vector(mk(R.streams["dve"]))
        block.gpsimd(mk(R.streams["pool"]))
        block.sync(mk(R.streams["sp"]))
    st.close()
    return nc


def fm(a):
    T, F = a.shape
    return np.ascontiguousarray(a.T.reshape(F // 128, 128, T).transpose(1, 0, 2))


def unfm(a):
    P, KC, T = a.shape
    return np.ascontiguousarray(a.transpose(1, 0, 2).reshape(KC * P, T).T)


def prep_inputs(inp, M):
    OWN = N * M
    SEQ = OWN * CPS
    f32 = np.float32
    g = lambda k: np.asarray(inp[k], f32)[0]
    W = {k: g(k) for k in ("w_in", "w_glu", "w_gate", "w_br_ssm", "w_br_pool", "w_br_mem", "w_out", "w_ffn_up", "w_ffn_down")}
    W["w_kv"] = np.concatenate([g("w_mem_k"), g("w_mem_v")], axis=1)
    wsrc = np.stack([unit_host(u, W) for u in UNITS])
    wp = g("w_pool")
    wpool = np.zeros((128, 2, 128), f32)
    for gi in range(4):
        tl, hf = gi // 2, gi % 2
        wpool[hf * 64:(hf + 1) * 64, tl, hf * 64:(hf + 1) * 64] = wp[gi]

    def pp(v):
        return v.reshape(-1, 128).T

    prm = np.concatenate([
        pp(g("b_gate")), pp(g("b_glu")), pp(g("pool_scale")), pp(g("g_post1")), pp(g("g_post2")),
        g("w_dw").T.reshape(NFF, 128, 3).transpose(1, 0, 2).reshape(128, 66), pp(g("b_dw")),
        np.tile(g("ssm_d").reshape(32, 16).T, (8, 1)),
        pp(g("g_pre1")), pp(g("g_pre2")), pp(g("g_mem")),
    ], axis=1).astype(f32)
    s5p = np.concatenate([
        g("ssm_lam_re").T, g("ssm_lam_im").T, np.tile(g("ssm_log_dt")[None, :], (64, 1)),
        g("ssm_b_re").transpose(1, 0, 2).reshape(64, 512), g("ssm_b_im").transpose(1, 0, 2).reshape(64, 512),
        g("ssm_c_re").transpose(2, 0, 1).reshape(64, 512), g("ssm_c_im").transpose(2, 0, 1).reshape(64, 512),
    ], axis=1).astype(f32)
    ident = np.eye(128, dtype=f32)
    ii = np.arange(128) // 16
    mask = (ii[None, :] >= ii[:, None]).astype(f32)
    xp = np.asarray(inp["x_prompt"], f32)
    xs = np.asarray(inp["x_sample"], f32)
    memp = np.asarray(inp["mem_prompt"], f32)
    ck = np.asarray(inp["cache_mem_k"], f32)[0]
    cv = np.asarray(inp["cache_mem_v"], f32)[0]
    sre = np.asarray(inp["state_ssm_re"], f32)[0]
    sim = np.asarray(inp["state_ssm_im"], f32)[0]
    spool = np.asarray(inp["state_pool"], f32)[0]
    sconv = np.asarray(inp["state_conv"], f32)[0]
    NPREC = (CPS - 1) * OWN
    maps = []
    for c in range(NCORES):
        b, k = c // CPS, c % CPS
        t0 = k * OWN
        idx = np.arange(t0 - L - NPREC, t0 + OWN)
        stream = np.zeros((idx.size, D), f32)
        v = idx >= 0
        stream[v] = xp[b, idx[v]]
        wins = np.array([2, 4, 8, 16], f32)
        pos = t0 + np.arange(16)
        invc = np.zeros((128, 2, 16), f32)
        for gi in range(4):
            tl, hf = gi // 2, gi % 2
            invc[hf * 64:(hf + 1) * 64, tl, :] = 1.0 / np.minimum(pos + 1, wins[gi])
        cst = np.concatenate([ident, mask, invc.reshape(128, 32), np.full((128, 1), 0.0 if k == 0 else 1.0, f32)], axis=1)
        ss = [2 * c, 2 * c + 1]
        m = {
            "xT_p": fm(stream),
            "xT_s": fm(xs[ss].reshape(32, D)),
            "memT": fm(memp[b]),
            "kTs": np.stack([ck[s].reshape(256, 256).T.reshape(2, 128, 256).transpose(1, 0, 2) for s in ss]),
            "vs": np.stack([cv[s].reshape(2, 128, 256).transpose(1, 0, 2) for s in ss]),
            "sst": np.stack([np.stack([sre[s].T, sim[s].T], axis=-1) for s in ss], axis=1),
            "ph": np.stack([spool[s].T.reshape(2, 128, HP).transpose(1, 0, 2) for s in ss], axis=2),
            "ch": np.stack([sconv[s].T.reshape(NFF, 128, HC).transpose(1, 0, 2) for s in ss], axis=2),
            "wsrc": wsrc, "wpool": wpool, "prm": prm, "s5p": s5p, "cst": cst.astype(f32),
        }
        maps.append({kk: np.ascontiguousarray(vv, dtype=f32) for kk, vv in m.items()})
    return maps


def assemble(res, M):
    OWN = N * M
    SEQ = OWN * CPS
    f32 = np.float32
    yp = np.zeros((2, SEQ, D), f32)
    ys = np.zeros((16, 16, D), f32)
    mk = np.zeros((1, 2, 256, 4, 64), f32)
    mv = np.zeros((1, 2, 256, 4, 64), f32)
    srp = np.zeros((1, 2, 32, 64), f32); sip = np.zeros((1, 2, 32, 64), f32)
    srs = np.zeros((1, 16, 32, 64), f32); sis = np.zeros((1, 16, 32, 64), f32)
    pp_ = np.zeros((1, 2, HP, 256), f32); ps_ = np.zeros((1, 16, HP, 256), f32)
    cp_ = np.zeros((1, 2, HC, DFF), f32); cs_ = np.zeros((1, 16, HC, DFF), f32)
    for c in range(NCORES):
        r = res[c]
        b, k = c // CPS, c % CPS
        y = unfm(np.asarray(r["yT_p"], f32))
        yp[b, k * OWN:(k + 1) * OWN] = y[L:L + OWN]
        ysm = unfm(np.asarray(r["yT_s"], f32))
        ys[2 * c] = ysm[0:16]
        ys[2 * c + 1] = ysm[16:32]
        sf = np.asarray(r["sfin"], f32)
        po = np.asarray(r["poolo"], f32)
        co = np.asarray(r["convo"], f32)

        def pool_of(sid):
            return po[:, :, sid, :].transpose(1, 0, 2).reshape(256, HP).T

        def conv_of(sid):
            return co[:, :, sid, :].transpose(1, 0, 2).reshape(DFF, HC).T

        if k == 0:
            kt = np.asarray(r["kout"], f32)
            mk[0, b] = kt.transpose(1, 0, 2).reshape(256, 256).T.reshape(256, 4, 64)
            vt = np.asarray(r["vout"], f32)
            mv[0, b] = vt.transpose(1, 0, 2).reshape(256, 4, 64)
        if k == CPS - 1:
            srp[0, b] = sf[:, 0, :, 0].T
            sip[0, b] = sf[:, 0, :, 1].T
            pp_[0, b] = pool_of(0)
            cp_[0, b] = conv_of(0)
        for j, s in enumerate((2 * c, 2 * c + 1)):
            srs[0, s] = sf[:, 1 + j, :, 0].T
            sis[0, s] = sf[:, 1 + j, :, 1].T
            ps_[0, s] = pool_of(1 + j)
            cs_[0, s] = conv_of(1 + j)
    return (yp, ys, mk, mv, srp, sip, srs, sis, pp_, ps_, cp_, cs_)


_CACHE = {}


def kernel(**inputs):
    M = 8
    maps = prep_inputs(inputs, M)
    if M not in _CACHE:
        _CACHE[M] = build(M)
    res = run_bass_kernel_spmd(_CACHE[M], maps, core_ids=list(range(NCORES)))
    return assemble(res.results, M)
```

```python
import math
from contextlib import ExitStack

import numpy as np
import concourse.bass as bass
import concourse.mybir as mybir
from concourse.bass_utils import run_bass_kernel_spmd

F32 = mybir.dt.float32
BF16 = mybir.dt.bfloat16
ALU = mybir.AluOpType
AF = mybir.ActivationFunctionType

D = 1024
DFF = 2816
NFF = 22
N = 256
L = 8
EPS = 1e-6
NCORES = 8
CPS = 4
HP = 15
HC = 2
UCAP = 4096

def unit_table():
    u = []
    u.append(("kv", "w_kv", 128, 8, [(0, 512)], 2))
    u.append(("in0", "w_in", 128, 8, [(0, 512)], 0))
    u.append(("in1", "w_in", 128, 8, [(512, 512)], 0))
    u.append(("glu", "w_glu", 128, 4, [(0, 512)], None))
    for b in range(3):
        for h in range(2):
            u.append((f"gate{b}{h}", "w_gate", 128, 8, [(b * 1024 + h * 512, 512)], 0))
    for h in range(2):
        u.append((f"brs{h}", "w_br_ssm", 128, 4, [(h * 512, 512)], None))
    u.append(("brp", "w_br_pool", 128, 2, [(0, 1024)], None))
    u.append(("brm", "w_br_mem", 64, 4, [(0, 1024)], None))
    for h in range(2):
        u.append((f"out{h}", "w_out", 128, 8, [(h * 512, 512)], None))
    for q in range(11):
        u.append((f"up{q}", "w_ffn_up", 128, 8, [(q * 256, 256), (DFF + q * 256, 256)], 1))
    for o in range(8):
        u.append((f"dn{o}", "w_ffn_down", 128, NFF, [(o * 128, 128)], None))
    return u


UNITS = unit_table()
UIDX = {u[0]: i for i, u in enumerate(UNITS)}
NU = len(UNITS)


def unit_host(u, W):
    name, key, P, KC, cols, gi = u
    w = W[key]
    parts = [w[:, c0:c0 + n] for (c0, n) in cols]
    w = np.concatenate(parts, axis=1) if len(parts) > 1 else parts[0]
    w = w[:P * KC]
    nc_ = w.shape[1]
    a = w.reshape(KC, P, nc_).transpose(1, 0, 2).reshape(P, KC * nc_)
    out = np.zeros((128, UCAP), np.float32)
    out[:P, :KC * nc_] = a
    return out


class Rec:
    ENG = ("pe", "act", "dve", "pool", "sp")

    def __init__(self):
        self.streams = {e: [] for e in self.ENG}
        self.cnt = {e: 0 for e in self.ENG}
        self.lastw = {}
        self.readers = {}
        self.waited = {e: {} for e in self.ENG}
        self.dcnt = {}
        self.sems = set()
        self.enabled = True

    def _need(self, eng, tok):
        if tok is None:
            return
        if tok[0] == "c":
            _, e, k = tok
            if e == eng and eng == "pe":
                return
            sem, val = "c_" + e, k
        else:
            sem, val = "d_" + tok[1], tok[2]
        if self.waited[eng].get(sem, 0) >= val:
            return
        self.waited[eng][sem] = val
        self.streams[eng].append(("w", sem, val))
        self.sems.add(sem)

    def _deps(self, eng, reads, writes):
        for r in reads:
            self._need(eng, self.lastw.get(r))
        for w in writes:
            self._need(eng, self.lastw.get(w))
            for t in self.readers.get(w, {}).values():
                self._need(eng, t)

    def _reg(self, tok, reads, writes):
        rk = (tok[0], tok[1])
        for r in reads:
            self.readers.setdefault(r, {})[rk] = tok
        for w in writes:
            self.lastw[w] = tok
            self.readers[w] = {}

    def op(self, eng, fn, reads=(), writes=(), signal=True):
        if not self.enabled:
            return
        self._deps(eng, reads, writes)
        if signal:
            self.cnt[eng] += 1
            tok = ("c", eng, self.cnt[eng])
            self.streams[eng].append(("o", fn, "c_" + eng, 1))
        else:
            tok = ("c", eng, self.cnt[eng] + 1)
            self.streams[eng].append(("o", fn, None, 0))
        self.sems.add("c_" + eng)
        self._reg(tok, reads, writes)

    def dma(self, eng, fn, semkey, reads=(), writes=()):
        if not self.enabled:
            return
        n = self.dcnt.get(semkey, 0)
        if n:
            self._need(eng, ("d", semkey, 16 * n))
        self._deps(eng, reads, writes)
        self.dcnt[semkey] = n + 1
        tok = ("d", semkey, 16 * (n + 1))
        self.streams[eng].append(("o", fn, "d_" + semkey, 16))
        self.sems.add("d_" + semkey)
        self._reg(tok, reads, writes)

    def final_wait(self, eng):
        for k, n in self.dcnt.items():
            self._need(eng, ("d", k, 16 * n))


class Rot:
    def __init__(self, items):
        self.items = items
        self.i = 0

    def next(self):
        it = self.items[self.i % len(self.items)]
        self.i += 1
        return it


STOP = None
CP_MODE = 1


def build(M, dbg=False):
    OWN = N * M
    NPRE = (CPS - 1) * M
    NS = N * NPRE + N * M + L
    NF = L + 32
    NYP = N * M + L
    NC = N // L

    nc = bass.Bass("TRN2", target_bir_lowering=False)
    R = Rec()
    st = ExitStack()

    def din(name, shape):
        return nc.dram_tensor(name, list(shape), F32, kind="ExternalInput").ap()

    def dout(name, shape):
        return nc.dram_tensor(name, list(shape), F32, kind="ExternalOutput").ap()

    xT_p = din("xT_p", [128, 8, NS])
    xT_s = din("xT_s", [128, 8, 32])
    memT = din("memT", [128, 8, 256])
    kTs = din("kTs", [2, 128, 2, 256])
    vs_in = din("vs", [2, 128, 2, 256])
    sst_in = din("sst", [64, 2, 32, 2])
    ph_in = din("ph", [128, 2, 2, HP])
    ch_in = din("ch", [128, NFF, 2, HC])
    wsrc = din("wsrc", [NU, 128, UCAP])
    wpool_in = din("wpool", [128, 2, 128])
    NPRM = 24 + 4 + 2 + 8 + 8 + 66 + 22 + 32 + 24
    prm_in = din("prm", [128, NPRM])
    s5p_in = din("s5p", [64, 96 + 4 * 512])
    cst_in = din("cst", [128, 128 + 128 + 32 + 1])

    yT_p = dout("yT_p", [128, 8, NYP])
    yT_s = dout("yT_s", [128, 8, 32])
    kout = dout("kout", [128, 2, 256])
    vout = dout("vout", [128, 2, 256])
    sfin = dout("sfin", [64, 3, 32, 2])
    poolo = dout("poolo", [128, 2, 3, HP])
    convo = dout("convo", [128, NFF, 3, HC])
    wbf = nc.dram_tensor("wbf", [NU, 128, UCAP], BF16, kind="Internal").ap()

    def sb(name, shape, dt=F32):
        return st.enter_context(nc.sbuf_tensor("s_" + name, list(shape), dt))

    def ps(name, shape, dt=F32):
        return st.enter_context(nc.psum_tensor("p_" + name, list(shape), dt))

    xT = sb("xT", [128, 8, N])
    hT = sb("hT", [128, 8, N], BF16)
    sq = [sb(f"sq{i}", [128, N], BF16) for i in range(2)]
    rstd = sb("rstd", [128, N])
    zs = sb("zs", [128, 4, N], BF16)
    WP = 3 * HP + NF + 8
    WPm = max(HP + N, WP)
    up = sb("up", [128, 2, WPm])
    b2 = sb("b2", [128, 2, WPm])
    b4 = sb("b4", [128, 2, WPm])
    dfT = sb("dfT", [128, 2, N], BF16)
    ypT = sb("ypT", [128, 2, N], BF16)
    qT = sb("qT", [128, 2, N], BF16)
    Zc = sb("Zc", [64, 4096], BF16)
    U = sb("U", [128, 32, NC], BF16)
    Yg = sb("Yg", [128, 32, NC], BF16)
    NSLOT = NC + 4
    Sall = sb("Sall", [64, NSLOT, 32, 2])
    Sbf = sb("Sbf", [64, NSLOT, 32, 2], BF16)
    ct1 = sb("ct1", [64, 32, 2])
    ct2 = sb("ct2", [64, 32, 2])
    ysT = sb("ysT", [128, 4, N], BF16)
    tb = [sb(f"tb{i}", [128, N], BF16) for i in range(3)]
    yss = sb("yss", [128, 4, N], BF16)
    PT = [sb(f"PT{i}", [128, N], BF16) for i in range(4)]
    rden = sb("rden", [64, N])
    ymT = sb("ymT", [64, 4, N], BF16)
    tf = [sb(f"tf{i}", [128, N]) for i in range(2)]
    mgT = sb("mgT", [128, 8, N], BF16)
    gT = sb("gT", [128, 8, N], BF16)
    m2T = sb("m2T", [128, 8, N])
    WA = max(HC + N, 3 * HC + NF)
    abuf = [sb(f"abuf{i}", [128, WA]) for i in range(2)]
    cbuf = [sb(f"cbuf{i}", [128, WA]) for i in range(2)]
    actT = sb("actT", [128, NFF, N], BF16)
    wsl = [sb(f"wsl{i}", [128, UCAP], BF16) for i in range(3)]
    wstage = sb("wstage", [128, UCAP // 2])
    Tm = sb("Tm", [128, 32, 128], BF16)
    Pm = sb("Pm", [64, 2, 32, 128], BF16)
    Qm = sb("Qm", [128, 2, 32, 64], BF16)
    KT = sb("KT", [128, 3, 2, 256], BF16)
    Vb = sb("Vb", [128, 3, 2, 256], BF16)
    wpl = sb("wpl", [128, 2, 128], BF16)
    prm = sb("prm", [128, NPRM])
    cst = sb("cst", [128, 289])
    onesb = sb("onesb", [128, 128], BF16)
    epsc_t = sb("epsc", [128, 1])
    epsc = epsc_t[:, 0:1]
    Sst = sb("Sst", [64, 3, 32, 2])
    phist = sb("phist", [128, 2, 3, HP])
    ahist = sb("ahist", [128, NFF, 3, HC])
    A1 = sb("A1", [64, 32, 2])
    A2n = sb("A2n", [64, 32])
    A2p = sb("A2p", [64, 32])
    A32 = sb("A32", [64, 3, 32])
    accA = sb("accA", [64, 32, 2])
    accB = sb("accB", [64, 32, 2])
    s5p = sb("s5p", [64, 96 + 2048])
    tmpT = sb("tmpT", [128, 4, 128])
    stg = sb("stg", [128, 2, 256])
    sm = Sall[:].rearrange("p a b c -> p (a b c)")[:, 0:1536].rearrange("p (a b) -> p a b", a=48)
    pws = wstage[0:64, 0:1536].rearrange("p (a b c) -> p a b c", a=6, b=32)
    bb0 = stg[0:64].rearrange("p a b -> p (a b)")
    bb1 = wstage[0:64, 1536:2048]

    pm = [ps(f"pm{i}", [128, 512]) for i in range(5)]
    pt = [ps(f"pt{i}", [128, 1024], BF16) for i in range(2)]
    pst = ps("pst", [128, 512])
    pmR = Rot([(pm[i], f"pm{i}") for i in range(5)])
    ptR = Rot([(pt[i], f"pt{i}") for i in range(2)])
    sqR = Rot([(sq[i], f"sq{i}") for i in range(2)])
    tbR = Rot([(tb[i], f"tb{i}") for i in range(3)])
    tfR = Rot([(tf[i], f"tf{i}") for i in range(2)])
    PTR = Rot([(PT[i], f"PT{i}") for i in range(4)])
    abR = Rot([(abuf[i], cbuf[i], f"ab{i}", f"cb{i}") for i in range(2)])
    evR = Rot(["act", "dve"])
    ewR = Rot(["dve", "pool"])

    o = 0
    P_BGATE = o; o += 24
    P_BGLU = o; o += 4
    P_PSC = o; o += 2
    P_GP1 = o; o += 8
    P_GP2 = o; o += 8
    P_WDW = o; o += 66
    P_BDW = o; o += 22
    P_DD = o; o += 32
    P_GV = o; o += 24
    ident = cst[:, 0:128]
    mask = cst[:, 128:256]
    invc = cst[:, 256:288]
    hflag = cst[:, 288:289]

    def E(eng, reads, writes, fn, signal=True):
        if eng != "pe":
            writes = list(writes) + [k for k in reads if k.startswith("pm") or k.startswith("pt") or k == "pst"]
        R.op(eng, fn, reads, writes, signal)

    def tt(eng, out, a, b, op, reads, writes):
        E(eng, reads, writes, lambda e, out=out, a=a, b=b, op=op: e.tensor_tensor(out=out, in0=a, in1=b, op=op))

    def ts(eng, out, a, s1, s2, op0, op1, reads, writes):
        if s2 is None:
            E(eng, reads, writes, lambda e, out=out, a=a, s1=s1, op0=op0:
              e.tensor_single_scalar(out=out, in_=a, scalar=s1, op=op0))
        else:
            E(eng, reads, writes, lambda e, out=out, a=a, s1=s1, s2=s2, op0=op0, op1=op1:
              e.tensor_scalar(out=out, in0=a, scalar1=s1, scalar2=s2, op0=op0, op1=op1))

    def stt(eng, out, a, s, b, op0, op1, reads, writes):
        E(eng, reads, writes, lambda e, out=out, a=a, s=s, b=b, op0=op0, op1=op1:
          e.scalar_tensor_tensor(out=out, in0=a, scalar=s, in1=b, op0=op0, op1=op1))

    def cp(eng, out, a, reads, writes):
        if eng == "act":
            E(eng, reads, writes, lambda e, out=out, a=a: e.copy(out=out, in_=a))
        elif eng == "dve" and CP_MODE == 1:
            E(eng, reads, writes, lambda e, out=out, a=a: e.tensor_single_scalar(out=out, in_=a, scalar=1.0, op=ALU.mult))
        else:
            E(eng, reads, writes, lambda e, out=out, a=a: e.tensor_copy(out=out, in_=a))

    def act(out, a, func, reads, writes, bias=None, scale=None):
        kw = {}
        if bias is not None:
            kw["bias"] = bias
        if scale is not None:
            kw["scale"] = scale
        E("act", reads, writes, lambda e, out=out, a=a, func=func, kw=kw: e.activation(out=out, in_=a, func=func, **kw))

    def mm(out, lhsT, rhs, start, stop, reads, writes, sig=True):
        E("pe", reads, writes, lambda e, out=out, lhsT=lhsT, rhs=rhs, start=start, stop=stop:
          e.matmul(out, lhsT, rhs, start=start, stop=stop), signal=sig)

    def tr(out, in_, idn, reads, writes):
        E("pe", reads, writes, lambda e, out=out, in_=in_, idn=idn: e.transpose(out, in_, idn))

    def dma(out, in_, semkey, reads, writes, eng="sp"):
        R.dma(eng, lambda e, out=out, in_=in_: e.dma_start(out=out, in_=in_), semkey, reads, writes)

    def memset(eng, ap, val, writes):
        E(eng, [], writes, lambda e, ap=ap, val=val: e.memset(ap, val))

    def stage(k):
        if STOP is not None and k >= STOP:
            R.enabled = False

    dma(prm[:], prm_in, "c0", [], ["prm"])
    dma(cst[:], cst_in, "c1", [], ["cst"])
    dma(s5p[:], s5p_in, "c2", [], ["s5p"])
    dma(wstage[:, 0:256], wpool_in.rearrange("p a b -> p (a b)"), "c3", [], ["wstage"])
    cp("dve", wpl[:].rearrange("p a b -> p (a b)"), wstage[:, 0:256], ["wstage"], ["wpl"])
    memset("dve", onesb[:], 1.0, ["onesb"])
    memset("dve", epsc_t[:], EPS, ["epsc"])
    memset("dve", Sst[:, 0], 0.0, ["Sst"])
    memset("pool", phist[:, :, 0], 0.0, ["phist"])
    memset("pool", ahist[:, :, 0], 0.0, ["ahist"])
    dma(Sst[:, 1:3], sst_in, "c4", [], ["Sst"])
    dma(phist[:, :, 1:3], ph_in, "c5", [], ["phist"])
    dma(ahist[:, :, 1:3], ch_in, "c6", [], ["ahist"])

    first_use = ["kv", "in0", "in1", "glu", "gate00", "gate01", "brs0", "brs1", "gate10", "gate11", "brp", "gate20", "gate21",
                 "brm", "out0", "out1"] + [f"up{q}" for q in range(11)] + [f"dn{o}" for o in range(8)]
    assert sorted(first_use) == sorted(UIDX)
    for nm in first_use:
        ui = UIDX[nm]
        u = UNITS[ui]
        tot = u[3] * sum(n for _, n in u[4])
        dma(wbf[ui, :, 0:tot], wsrc[ui, :, 0:tot], f"cv{ui}", [], [f"wbf{ui}"], eng="pool")

    stage(1)
    smi = [0]

    def smn():
        i = smi[0]
        smi[0] += 1
        return sm[:, i, :]

    K_ = ["Sall"]
    lr = s5p[:, 0:32]
    li = s5p[:, 32:64]
    ldt = s5p[:, 64:96]
    dt_ = smn()
    act(dt_, ldt, AF.Exp, ["s5p"], K_)
    x1 = smn(); ang = smn(); q_ = smn(); mag = smn()
    tt("dve", x1, lr, dt_, ALU.mult, ["s5p"] + K_, K_)
    tt("dve", ang, li, dt_, ALU.mult, ["s5p"] + K_, K_)
    ts("dve", q_, x1, 1.0 / 120, None, ALU.mult, None, K_, K_)
    for c in (1.0 / 24, 1.0 / 6, 0.5, 1.0):
        stt("dve", q_, q_, c, x1, ALU.add, ALU.mult, K_, K_)
    ts("dve", mag, q_, 1.0, None, ALU.add, None, K_, K_)
    yq = smn(); u_ = smn(); s_ = smn(); c_ = smn(); t_ = smn()
    MAGIC = 12582912.0
    ts("dve", t_, ang, 1.0 / (2 * math.pi), MAGIC, ALU.mult, ALU.add, K_, K_)
    ts("dve", t_, t_, -MAGIC, None, ALU.add, None, K_, K_)
    stt("dve", yq, t_, -2 * math.pi, ang, ALU.mult, ALU.add, K_, K_)
    ts("dve", t_, yq, math.pi, None, ALU.is_gt, None, K_, K_)
    stt("dve", yq, t_, -2 * math.pi, yq, ALU.mult, ALU.add, K_, K_)
    ts("dve", t_, yq, -math.pi, None, ALU.is_lt, None, K_, K_)
    stt("dve", yq, t_, 2 * math.pi, yq, ALU.mult, ALU.add, K_, K_)
    ts("dve", yq, yq, 0.25, None, ALU.mult, None, K_, K_)
    tt("dve", u_, yq, yq, ALU.mult, K_, K_)
    ts("dve", q_, u_, 1.0 / 362880, None, ALU.mult, None, K_, K_)
    for c in (-1.0 / 5040, 1.0 / 120, -1.0 / 6):
        stt("dve", q_, q_, c, u_, ALU.add, ALU.mult, K_, K_)
    stt("dve", s_, q_, 1.0, yq, ALU.add, ALU.mult, K_, K_)
    ts("dve", q_, u_, -1.0 / 3628800, None, ALU.mult, None, K_, K_)
    for c in (1.0 / 40320, -1.0 / 720, 1.0 / 24, -0.5):
        stt("dve", q_, q_, c, u_, ALU.add, ALU.mult, K_, K_)
    ts("dve", c_, q_, 1.0, None, ALU.add, None, K_, K_)
    for _ in range(2):
        tt("dve", t_, s_, c_, ALU.mult, K_, K_)
        tt("dve", q_, s_, s_, ALU.mult, K_, K_)
        ts("dve", s_, t_, 2.0, None, ALU.mult, None, K_, K_)
        ts("dve", c_, q_, -2.0, 1.0, ALU.mult, ALU.add, K_, K_)
    pwr = [smn() for _ in range(9)]
    pwi = [smn() for _ in range(9)]
    memset("dve", pwr[0], 1.0, K_)
    memset("dve", pwi[0], 0.0, K_)
    tt("dve", pwr[1], mag, c_, ALU.mult, K_, K_)
    tt("dve", pwi[1], mag, s_, ALU.mult, K_, K_)

    def cmul(or_, oi_, ar, ai, br, bi, t1, t2, keys, eng="dve", wkeys=None):
        wk = keys if wkeys is None else wkeys
        tt(eng, t1, ar, br, ALU.mult, keys, wk)
        tt(eng, t2, ai, bi, ALU.mult, keys, wk)
        tt(eng, or_, t1, t2, ALU.subtract, keys + wk, wk)
        tt(eng, t1, ar, bi, ALU.mult, keys, wk)
        tt(eng, t2, ai, br, ALU.mult, keys, wk)
        tt(eng, oi_, t1, t2, ALU.add, keys + wk, wk)

    ta = smn(); tb_ = smn()
    for k in range(2, 9):
        cmul(pwr[k], pwi[k], pwr[k - 1], pwi[k - 1], pwr[1], pwi[1], ta, tb_, K_)
    den = smn(); nr = smn(); zr = smn(); zi = smn()
    tt("dve", den, lr, lr, ALU.mult, ["s5p"] + K_, K_)
    tt("dve", ta, li, li, ALU.mult, ["s5p"] + K_, K_)
    tt("dve", den, den, ta, ALU.add, K_, K_)
    E("dve", K_, K_, lambda e, den=den: e.reciprocal(out=den, in_=den))
    ts("dve", nr, pwr[1], -1.0, None, ALU.add, None, K_, K_)
    tt("dve", ta, nr, lr, ALU.mult, ["s5p"] + K_, K_)
    tt("dve", tb_, pwi[1], li, ALU.mult, ["s5p"] + K_, K_)
    tt("dve", ta, ta, tb_, ALU.add, K_, K_)
    tt("dve", zr, ta, den, ALU.mult, K_, K_)
    tt("dve", ta, pwi[1], lr, ALU.mult, ["s5p"] + K_, K_)
    tt("dve", tb_, nr, li, ALU.mult, ["s5p"] + K_, K_)
    tt("dve", ta, ta, tb_, ALU.subtract, K_, K_)
    tt("dve", zi, ta, den, ALU.mult, K_, K_)
    i7r = smn(); i7i = smn()
    tt("dve", ta, pwr[7], pwr[7], ALU.mult, K_, K_)
    tt("dve", tb_, pwi[7], pwi[7], ALU.mult, K_, K_)
    tt("dve", ta, ta, tb_, ALU.add, K_, K_)
    E("dve", K_, K_, lambda e, ta=ta: e.reciprocal(out=ta, in_=ta))
    tt("dve", i7r, pwr[7], ta, ALU.mult, K_, K_)
    tt("dve", i7i, pwi[7], ta, ALU.mult, K_, K_)
    ts("dve", i7i, i7i, -1.0, None, ALU.mult, None, K_, K_)
    KA = ["Achain"]
    cp("dve", A1[:, :, 0], pwr[8], K_, KA)
    cp("dve", A1[:, :, 1], pwr[8], K_, KA)
    cp("dve", A2p[:], pwi[8], K_, KA)
    ts("dve", A2n[:], pwi[8], -1.0, None, ALU.mult, None, K_, KA)
    KP = ["wstage"]
    zpr = smn(); zpi = smn()
    for i in range(8):
        cp("dve", pws[:, 0, :, i], pwr[7 - i], K_, KP)
        cp("dve", pws[:, 1, :, i], pwi[7 - i], K_, KP)
        cmul(zpr, zpi, pwr[i], pwi[i], i7r, i7i, ta, tb_, K_)
        cp("dve", pws[:, 2, :, i], zpr, K_, KP)
        cp("dve", pws[:, 3, :, i], zpi, K_, KP)
        cp("dve", pws[:, 4, :, i], pwr[i + 1], K_, KP)
        cp("dve", pws[:, 5, :, i], pwi[i + 1], K_, KP)
        bre = s5p[:, 96:608].rearrange("p (g h) -> p g h", h=16)
    bim = s5p[:, 608:1120].rearrange("p (g h) -> p g h", h=16)
    cre = s5p[:, 1120:1632].rearrange("p (g h) -> p g h", h=16)
    cim = s5p[:, 1632:2144].rearrange("p (g h) -> p g h", h=16)
    BBr = bb0.rearrange("p (g h) -> p g h", h=16)
    BBi = bb1.rearrange("p (g h) -> p g h", h=16)
    zrb = zr.unsqueeze(2).broadcast_to([64, 32, 16])
    zib = zi.unsqueeze(2).broadcast_to([64, 32, 16])
    m2f = m2T[0:64].rearrange("p a b -> p (a b)").rearrange("p (a b) -> p a b", a=4)
    xTf = xT[0:64].rearrange("p a b -> p (a b)").rearrange("p (a b) -> p a b", a=4)

    def bgv(i):
        return m2f[:, i, :] if i < 4 else xTf[:, i - 4, :]

    t512a = bgv(6).rearrange("p (g h) -> p g h", h=16)
    t512b = bgv(7).rearrange("p (g h) -> p g h", h=16)
    cmul(BBr, BBi, zrb, zib, bre, bim, t512a, t512b, ["s5p", "stg", "wstage", "m2T", "xT", "Sall"])

    def v4(ap2):
        return ap2.rearrange("p (g i h) -> p g i h", g=4, i=8)

    KG = ["m2T", "xT", "wstage", "stg", "s5p"]
    for e8 in range(8):
        g0 = 4 * e8
        XR, XI, ZR, ZI, PR, PI, TA, TB = [v4(bgv(i)) for i in range(8)]

        def pwb(idx):
            return pws[:, idx, g0:g0 + 4, :].unsqueeze(3).broadcast_to([64, 4, 8, 16])

        def vb(ap3):
            return ap3[:, g0:g0 + 4, :].unsqueeze(2).broadcast_to([64, 4, 8, 16])

        cmul(XR, XI, pwb(0), pwb(1), vb(BBr), vb(BBi), TA, TB, KG)
        cmul(ZR, ZI, pwb(2), pwb(3), vb(cre), vb(cim), TA, TB, KG)
        cmul(PR, PI, pwb(4), pwb(5), vb(cre), vb(cim), TA, TB, KG)
        cp("dve", Pm[:, 0, g0:g0 + 4, :], bgv(4).rearrange("p (g x) -> p g x", g=4), KG, ["Pm"])
        ts("dve", Pm[:, 1, g0:g0 + 4, :], bgv(5).rearrange("p (g x) -> p g x", g=4), -1.0, None,
           ALU.mult, None, KG, ["Pm"])
        ts("dve", bgv(3), bgv(3), -1.0, None, ALU.mult, None, KG, KG)
        pq, pqk = pmR.next()
        for gl in range(4):
            for ri in range(2):
                src = bgv(ri)[:, gl * 128:(gl + 1) * 128]
                tr(pq[:, (gl * 2 + ri) * 64:(gl * 2 + ri + 1) * 64], src, ident[0:64, 0:64], KG + ["cst"], [pqk])
        cp("act", Qm[:, :, g0:g0 + 4, :].rearrange("p r g n -> p g r n"),
           pq[:, :].rearrange("p (g r n) -> p g r n", g=4, r=2), [pqk], ["Qm"])
        pT_, pTk = pmR.next()
        for gl in range(4):
            sl = slice(gl * 128, (gl + 1) * 128)
            mm(pT_[:, sl], bgv(0)[:, sl], bgv(2)[:, sl], True, False, KG, [pTk])
            mm(pT_[:, sl], bgv(1)[:, sl], bgv(3)[:, sl], False, True, KG, [pTk])
        tt("dve", tmpT[:], pT_[:, :].rearrange("p (g x) -> p g x", g=4),
           mask.unsqueeze(1).broadcast_to([128, 4, 128]), ALU.mult, [pTk, "cst"], ["tmpT"])
        for gl in range(4):
            g = g0 + gl
            stt("dve", Tm[:, g, :], ident, prm[:, P_DD + g:P_DD + g + 1], tmpT[:, gl, :], ALU.mult, ALU.add,
                ["cst", "prm", "tmpT"], ["Tm"])

    Pre = s5p[:, 0:1024].rearrange("p (c g) -> p c g", c=32)
    Pim = s5p[:, 1024:2048].rearrange("p (c g) -> p c g", c=32)
    cur_r = smn(); cur_i = smn(); nx_r = smn(); nx_i = smn()
    KT_ = ["Sall", "s5p"]
    memset("pool", cur_r, 1.0, K_)
    memset("pool", cur_i, 0.0, K_)
    ta2 = smn(); tb2 = smn()
    for k in range(32):
        cp("pool", Pre[:, 31 - k, :], cur_r, K_, ["s5p"])
        cp("pool", Pim[:, 31 - k, :], cur_i, K_, ["s5p"])
        cmul(nx_r, nx_i, cur_r, cur_i, pwr[8], pwi[8], ta2, tb2, K_, eng="pool")
        cur_r, cur_i, nx_r, nx_i = nx_r, nx_i, cur_r, cur_i
    cp("pool", A32[:, 0], cur_r, K_, ["A32"])
    cp("pool", A32[:, 1], cur_i, K_, ["A32"])
    ts("pool", A32[:, 2], cur_i, -1.0, None, ALU.mult, None, K_, ["A32"])
    stage(2)
    stage(3)
    wuse = []
    wstate = {"next_load": 0, "slot_of": {}}

    def plan_pass(kind, last_prefix=False):
        if kind == "prefix":
            return ["in0"] + (["in1"] if last_prefix else [])
        lst = ["in0", "in1", "gate10", "gate11", "brp", "gate20", "gate21", "brm", "glu", "gate00", "gate01", "brs0", "brs1"]
        lst += ["out0", "out1"] + [f"up{q}" for q in range(11)] + [f"dn{o}" for o in range(8)]
        return lst

    def issue_loads(upto):
        while wstate["next_load"] < min(upto, len(wuse)):
            i = wstate["next_load"]
            name = wuse[i]
            ui = UIDX[name]
            u = UNITS[ui]
            tot = u[3] * sum(n for _, n in u[4])
            s = i % 3
            dma(wsl[s][:, 0:tot], wbf[ui, :, 0:tot], f"wld{s}", [f"wbf{ui}"], [f"wsl{s}"])
            wstate["next_load"] += 1

    wptr = [0]

    PLAN = [False]

    def wget(name):
        if PLAN[0]:
            wuse.append(name)
            return wsl[0], "wsl0"
        i = wptr[0]
        assert wuse[i] == name, (wuse[i], name, i)
        issue_loads(i + 3)
        wptr[0] += 1
        return wsl[i % 3], f"wsl{i % 3}"

    def norm_stats(src, srck, n):
        for kc in range(8):
            s_, sk = sqR.next()
            act(s_[:, 0:n], src[:, kc, 0:n], AF.Square, [srck], [sk])
            mm(pst[:, 0:n], onesb[:], s_[:, 0:n], kc == 0, kc == 7, ["onesb", sk], ["pst"])
        act(rstd[:, 0:n], pst[:, 0:n], AF.Sqrt, ["pst", "epsc"], ["rstd"], bias=epsc, scale=1.0 / D)
        E("dve", ["rstd"], ["rstd"], lambda e, n=n: e.reciprocal(out=rstd[:, 0:n], in_=rstd[:, 0:n]))

    HB = {"h": hT, "hk": "hT", "g": gT, "gk": "gT"}

    def use_h(par):
        if par == 0:
            HB.update(h=hT, hk="hT", g=gT, gk="gT")
        else:
            HB.update(h=gT, hk="gT", g=hT, gk="hT")

    def make_h(src, srck, n, gi):
        hT, hk_ = HB["h"], HB["hk"]
        for kc in range(8):
            gcol = prm[:, P_GV + gi * 8 + kc:P_GV + gi * 8 + kc + 1]
            stt("dve", hT[:, kc, 0:n], src[:, kc, 0:n], gcol, rstd[:, 0:n], ALU.mult, ALU.mult, [srck, "rstd", "prm"], [hk_])

    def proj(wt, wk, kcs, colsel, rhs_fn, rhs_keys, n, part=128):
        p_, pk = pmR.next()
        for i, kc in enumerate(kcs):
            mm(p_[:, 0:n], wt(kc, colsel), rhs_fn(kc), i == 0, i == len(kcs) - 1, [wk] + rhs_keys, [pk],
               sig=(i == len(kcs) - 1))
        return p_, pk

    def wview(wt, kcn, ncol, P_=128):
        v = wt[0:P_, 0:kcn * ncol].rearrange("p (k c) -> p k c", k=kcn)
        return lambda kc, cs: v[:, kc, cs]

    CH = "pool"

    def s5_front(segs, n, prefix=False):
        ncn = n // L
        evn = (lambda: "act") if prefix else evR.next
        for i0 in range(0, 8, 2):
            p_, pk = ptR.next()
            for ii in range(2):
                for f in range(4):
                    src = zs[:, f, 0:n].rearrange("p (c i) -> p i c", i=8)[:, i0 + ii, :]
                    tr(p_[0:ncn, (ii * 4 + f) * 128:(ii * 4 + f + 1) * 128], src, identb[:, :], ["zs", "identb"], [pk])
            cp(evn(), Zc[0:ncn, :].rearrange("p (g i h) -> p i g h", g=32, i=8)[:, i0:i0 + 2, :, :],
               p_[0:ncn, :].rearrange("p (i g h) -> p i g h", i=2, g=32), [pk], ["Zc"])
        for gh in range(2):
            p_, pk = ptR.next()
            for gl in range(16):
                g = gh * 16 + gl
                src = Zc[0:ncn, g * 128:(g + 1) * 128]
                tr(p_[:, gl * 64:gl * 64 + ncn], src, identb[0:ncn, 0:ncn], ["Zc", "identb"], [pk])
            cp(evn(), U[:, gh * 16:(gh + 1) * 16, 0:ncn],
               p_[:, :].rearrange("p (g c) -> p g c", g=16)[:, :, 0:ncn], [pk], ["U"])
        for g0 in range(0, 32, 4):
            p_, pk = pmR.next()
            for gl in range(4):
                for ri in range(2):
                    mm(p_[0:64, (gl * 2 + ri) * 64:(gl * 2 + ri) * 64 + ncn], Qm[:, ri, g0 + gl, :], U[:, g0 + gl, 0:ncn],
                       True, True, ["Qm", "U"], [pk])
            pv = p_[0:64, :].rearrange("p (g r c) -> p c g r", g=4, r=2)
            for sg in segs:
                c0 = sg["c0"] // L
                cn = sg["n"] // L
                cp(evn(), Sall[:, sg["slot"] + 1:sg["slot"] + 1 + cn, g0:g0 + 4, :], pv[:, c0:c0 + cn, :, :],
                   [pk, ], ["Sall"])
        if prefix:
            return
        for sg in segs:
            b = sg["slot"]
            cn = sg["n"] // L
            cp(CH, Sall[:, b], Sst[:, sg["sid"]], ["Sst"], ["Sall"])
            for c in range(cn):
                s0 = Sall[:, b + c]
                s1 = Sall[:, b + c + 1]
                tt(CH, ct1[:], s0, A1[:], ALU.mult, ["Sall", "Achain"], ["ct1"])
                tt(CH, ct2[:, :, 0], s0[:, :, 1], A2n[:], ALU.mult, ["Sall", "Achain"], ["ct2"])
                tt(CH, ct2[:, :, 1], s0[:, :, 0], A2p[:], ALU.mult, ["Sall", "Achain"], ["ct2"])
                tt(CH, s1, s1, ct1[:], ALU.add, ["Sall", "ct1"], ["Sall"])
                tt(CH, s1, s1, ct2[:], ALU.add, ["Sall", "ct2"], ["Sall"])
            cp(CH, Sst[:, sg["sid"]], Sall[:, b + cn], ["Sall"], ["Sst"])

    def prefix_wsum():
        Wv = Sall[:, 1:33]
        t1 = m2T[0:64].rearrange("p a b -> p (a b)").rearrange("p (c g r) -> p c g r", c=32, g=32)
        S0 = Sst[:, 0]
        for tab, acc in ((Pre, accA), (Pim, accB)):
            tt("dve", t1, Wv, tab.unsqueeze(3).broadcast_to([64, 32, 32, 2]), ALU.mult, ["Sall", "s5p"], ["m2T"])
            E("dve", ["m2T"], ["acc"], lambda e, acc=acc, t1=t1: e.tensor_reduce(
                out=acc[:], in_=t1.rearrange("p c g r -> p g r c"), axis=mybir.AxisListType.X, op=ALU.add))
        tt("dve", ct1[:], S0, A32[:, 0].unsqueeze(2).broadcast_to([64, 32, 2]), ALU.mult, ["Sst", "A32"], ["ct1"])
        tt("dve", ct2[:, :, 0], S0[:, :, 1], A32[:, 2], ALU.mult, ["Sst", "A32"], ["ct2"])
        tt("dve", ct2[:, :, 1], S0[:, :, 0], A32[:, 1], ALU.mult, ["Sst", "A32"], ["ct2"])
        tt("dve", S0, ct1[:], ct2[:], ALU.add, ["ct1", "ct2"], ["Sst"])
        tt("dve", S0, S0, accA[:], ALU.add, ["Sst", "acc"], ["Sst"])
        tt("dve", S0[:, :, 0], S0[:, :, 0], accB[:, :, 1], ALU.subtract, ["Sst", "acc"], ["Sst"])
        tt("dve", S0[:, :, 1], S0[:, :, 1], accB[:, :, 0], ALU.add, ["Sst", "acc"], ["Sst"])

    def s5_back(segs, n, wglu, wgk):
        ncn = n // L
        nslot = segs[-1]["slot"] + segs[-1]["n"] // L + 1
        cp("act", Sbf[:, 0:nslot], Sall[:, 0:nslot], ["Sall"], ["Sbf"])
        for g0 in range(0, 32, 8):
            p_, pk = pmR.next()
            for gl in range(8):
                g = g0 + gl
                mm(p_[:, gl * 64:gl * 64 + ncn], Tm[:, g, :], U[:, g, 0:ncn], True, False, ["Tm", "U"], [pk])
                for si, sg in enumerate(segs):
                    c0 = sg["c0"] // L
                    cn = sg["n"] // L
                    for ri in range(2):
                        last = (si == len(segs) - 1) and ri == 1
                        mm(p_[:, gl * 64 + c0:gl * 64 + c0 + cn], Pm[:, ri, g, :],
                           Sbf[:, sg["slot"]:sg["slot"] + cn, g, ri], False, last, ["Pm", "Sbf"], [pk])
            act(Yg[:, g0:g0 + 8, 0:ncn], p_[:, :].rearrange("p (g c) -> p g c", g=8)[:, :, 0:ncn], AF.Gelu, [pk], ["Yg"])
        for g0 in range(0, 32, 8):
            p_, pk = ptR.next()
            for gl in range(8):
                tr(p_[0:ncn, gl * 128:(gl + 1) * 128], Yg[:, g0 + gl, 0:ncn], identb[:, :], ["Yg", "identb"], [pk])
            cp(evR.next(), Zc[0:ncn, :].rearrange("p (j g h) -> p g j h", j=8, g=32)[:, g0:g0 + 8, :, :],
               p_[0:ncn, :].rearrange("p (g j h) -> p g j h", g=8, j=8), [pk], ["Zc"])
        for j0 in range(0, 8, 2):
            p_, pk = ptR.next()
            for jj in range(2):
                for f in range(4):
                    tr(p_[:, (jj * 4 + f) * 64:(jj * 4 + f) * 64 + ncn], Zc[0:ncn, (j0 + jj) * 512 + f * 128:(j0 + jj) * 512 + (f + 1) * 128],
                       identb[0:ncn, 0:ncn], ["Zc", "identb"], [pk])
            src = p_[:, 0:512].rearrange("p (j f c) -> p f c j", j=2, f=4)[:, :, 0:ncn, :]
            dst = ysT[:, :, 0:n].rearrange("p f (c j) -> p f c j", j=8)[:, :, :, j0:j0 + 2]
            cp(evR.next(), dst, src, [pk], ["ysT"])
        wv = wview(wglu, 4, 512)
        for ot in range(4):
            p_, pk = proj(wv, wgk, range(4), slice(ot * 128, (ot + 1) * 128), lambda kc: ysT[:, kc, 0:n], ["ysT"], n)
            t_, tk = tbR.next()
            act(t_[:, 0:n], p_[:, 0:n], AF.Sigmoid, [pk, "prm"], [tk], bias=prm[:, P_BGLU + ot:P_BGLU + ot + 1])
            tt("dve", yss[:, ot, 0:n], ysT[:, ot, 0:n], t_[:, 0:n], ALU.mult, ["ysT", tk], ["yss"])

    def pooling(segs, n, wtot, first_main):
        W = wtot
        for sg in segs:
            cp("pool", up[:, :, sg["po"] - HP:sg["po"]], phist[:, :, sg["sid"], :], ["phist"], ["up"])
        tt("dve", b2[:, :, 1:W], up[:, :, 1:W], up[:, :, 0:W - 1], ALU.add, ["up"], ["b2"])
        tt("pool", b4[:, :, 3:W], b2[:, :, 3:W], b2[:, :, 1:W - 2], ALU.add, ["b2"], ["b4"])
        tt("dve", b2[:, 1, 7:W], b4[:, 1, 7:W], b4[:, 1, 3:W - 4], ALU.add, ["b4"], ["b2"])
        tt("pool", b4[:, 1, 15:W], b2[:, 1, 15:W], b2[:, 1, 7:W - 8], ALU.add, ["b2"], ["b4"])
        srcs = [(b2, 0, 0, 64, 2), (b4, 0, 64, 128, 4), (b2, 1, 0, 64, 8), (b4, 1, 64, 128, 16)]
        for sg in segs:
            po, c0, sn = sg["po"], sg["c0"], sg["n"]
            for (bt, tl, p0, p1, w) in srcs:
                stt("dve", dfT[p0:p1, tl, c0:c0 + sn], bt[p0:p1, tl, po:po + sn], 1.0 / w, up[p0:p1, tl, po:po + sn],
                    ALU.mult, ALU.subtract, ["b2", "b4", "up"], ["dfT"])
            if first_main and sg["sid"] == 0:
                for (bt, tl, p0, p1, w) in srcs:
                    t_, tk = tfR.next()
                    tt("dve", t_[p0:p1, 0:16], bt[p0:p1, tl, po + 8:po + 24], invc[p0:p1, tl * 16:(tl + 1) * 16], ALU.mult,
                       ["b2", "b4", "cst"], [tk])
                    tt("dve", dfT[p0:p1, tl, c0 + 8:c0 + 24], t_[p0:p1, 0:16], up[p0:p1, tl, po + 8:po + 24], ALU.subtract,
                       [tk, "up"], ["dfT"])
            cp("pool", phist[:, :, sg["sid"], :], up[:, :, po + sn - HP:po + sn], ["up"], ["phist"])
        for tl in range(2):
            p_, pk = pmR.next()
            mm(p_[:, 0:n], wpl[:, tl, :], dfT[:, tl, 0:n], True, True, ["wpl", "dfT"], [pk])
            ts("dve", ypT[:, tl, 0:n], p_[:, 0:n], prm[:, P_PSC + tl:P_PSC + tl + 1], None, ALU.mult, None, [pk, "prm"], ["ypT"])

    def attention(segs, n):
        def s1(sg, h):
            c0, sn, kv = sg["c0"], sg["n"], sg["kv"]
            hq, r0 = h // 2, (h % 2) * 64
            pts = []
            for mc in range(2):
                p_, pk = pmR.next()
                mm(p_[:, 0:sn], KT[r0:r0 + 64, kv, hq, mc * 128:(mc + 1) * 128], qT[r0:r0 + 64, hq, c0:c0 + sn], True, True,
                   ["KT", "qT"], [pk])
                t_, tk = PTR.next()
                act(t_[:, 0:sn], p_[:, 0:sn], AF.Exp, [pk], [tk], scale=0.125)
                pts.append((t_, tk))
            return pts

        def s2(sg, h, pts):
            c0, sn, kv = sg["c0"], sg["n"], sg["kv"]
            po_, pok = pmR.next()
            pd_, pdk = pmR.next()
            for mc in range(2):
                mm(po_[0:64, 0:sn], Vb[:, kv, mc, h * 64:(h + 1) * 64], pts[mc][0][:, 0:sn], mc == 0, mc == 1,
                   ["Vb", pts[mc][1]], [pok])
            for mc in range(2):
                mm(pd_[0:64, 0:sn], onesb[:, 0:64], pts[mc][0][:, 0:sn], mc == 0, mc == 1, ["onesb", pts[mc][1]], [pdk])
            E("dve", [pdk], ["rden"], lambda e, pd_=pd_, sn=sn: e.reciprocal(out=rden[:, 0:sn], in_=pd_[0:64, 0:sn]))
            tt("dve", ymT[:, h, c0:c0 + sn], po_[0:64, 0:sn], rden[:, 0:sn], ALU.mult, [pok, "rden"], ["ymT"])

        items = [(sg, h) for sg in segs for h in range(4)]
        pend = None
        for (sg, h) in items:
            pts = s1(sg, h)
            if pend is not None:
                s2(*pend)
            pend = (sg, h, pts)
        s2(*pend)

    def merge_branch(b, role, n):
        brdefs = {
            0: (["brs0", "brs1"], 4, 512, 128, lambda kc: yss[:, kc, 0:n], ["yss"]),
            1: (["brp"], 2, 1024, 128, lambda kc: ypT[:, kc, 0:n], ["ypT"]),
            2: (["brm"], 4, 1024, 64, lambda kc: ymT[:, kc, 0:n], ["ymT"]),
        }
        brn, kcn, bcols, bp, rfn, rkeys = brdefs[b]
        hT, hk_, gT, gk_ = HB["h"], HB["hk"], HB["g"], HB["gk"]
        for hf in range(2):
            gwt, gwk = wget(f"gate{b}{hf}")
            gv = wview(gwt, 8, 512)
            for o4 in range(4):
                ot = hf * 4 + o4
                pg, pgk = proj(gv, gwk, range(8), slice(o4 * 128, (o4 + 1) * 128), lambda kc: hT[:, kc, 0:n], [hk_], n)
                act(gT[:, ot, 0:n], pg[:, 0:n], AF.Sigmoid, [pgk, "prm"], [gk_],
                    bias=prm[:, P_BGATE + b * 8 + ot:P_BGATE + b * 8 + ot + 1])
        for bi, nm in enumerate(brn):
            bwt, bwk = wget(nm)
            bv = wview(bwt, kcn, bcols, bp)
            ots = range(bi * 4, bi * 4 + 4) if len(brn) == 2 else range(8)
            for ot in ots:
                csel = slice((ot % 4) * 128, (ot % 4 + 1) * 128) if len(brn) == 2 else slice(ot * 128, (ot + 1) * 128)
                pb, pbk = proj(bv, bwk, range(kcn), csel, rfn, rkeys, n)
                if role == "first":
                    tt("dve", m2T[:, ot, 0:n], pb[:, 0:n], gT[:, ot, 0:n], ALU.mult, [pbk, gk_], ["m2T"])
                else:
                    f_, fk = tfR.next()
                    tt("dve", f_[:, 0:n], pb[:, 0:n], gT[:, ot, 0:n], ALU.mult, [pbk, gk_], [fk])
                    if role == "mid":
                        tt("dve", m2T[:, ot, 0:n], m2T[:, ot, 0:n], f_[:, 0:n], ALU.add, ["m2T", fk], ["m2T"])
                    else:
                        tt("dve", mgT[:, ot, 0:n], m2T[:, ot, 0:n], f_[:, 0:n], ALU.add, ["m2T", fk], ["mgT"])

    def merge_out(n):
        for hf in range(2):
            wt_, wk_ = wget(f"out{hf}")
            wv = wview(wt_, 8, 512)
            for o4 in range(4):
                ot = hf * 4 + o4
                p_, pk = proj(wv, wk_, range(8), slice(o4 * 128, (o4 + 1) * 128), lambda kc: mgT[:, kc, 0:n], ["mgT"], n)
                cp("act", m2T[:, ot, 0:n], p_[:, 0:n], [pk], ["m2T"])

    def resid_norm(n, gofs, xb, xk):
        norm_stats(m2T, "m2T", n)
        for ot in range(8):
            f_, fk = tfR.next()
            stt("dve", f_[:, 0:n], m2T[:, ot, 0:n], prm[:, gofs + ot:gofs + ot + 1], rstd[:, 0:n], ALU.mult, ALU.mult,
                ["m2T", "prm", "rstd"], [fk])
            tt("dve", xb[:, ot, 0:n], xb[:, ot, 0:n], f_[:, 0:n], ALU.add, [xk, fk], [xk])

    def ffn_up(segs, n, first_main, xb, xk):
        norm_stats(xb, xk, n)
        make_h(xb, xk, n, 1)
        hT, hk_ = HB["h"], HB["hk"]
        pend = []

        def stage_b(j, pg, pgk, cb, cbk):
            t_, tk = tbR.next()
            for sg in segs:
                ao, c0, sn = sg["ao"], sg["c0"], sg["n"]
                act(t_[:, c0:c0 + sn], cb[:, ao:ao + sn], AF.Gelu, [cbk], [tk])
            tt("dve", actT[:, j, 0:n], pg[:, 0:n], t_[:, 0:n], ALU.mult, [pgk, tk], ["actT"])

        for q in range(11):
            wt_, wk_ = wget(f"up{q}")
            wv = wview(wt_, 8, 512)
            for jj in range(2):
                j = 2 * q + jj
                pa, pak = proj(wv, wk_, range(8), slice(jj * 128, (jj + 1) * 128), lambda kc: hT[:, kc, 0:n], [hk_], n)
                pg, pgk = proj(wv, wk_, range(8), slice(256 + jj * 128, 256 + (jj + 1) * 128), lambda kc: hT[:, kc, 0:n], [hk_], n)
                ab, cb, abk, cbk = abR.next()
                wtot = segs[-1]["ao"] + segs[-1]["n"]
                for sg in segs:
                    ao, c0, sn = sg["ao"], sg["c0"], sg["n"]
                    cp("act", ab[:, ao:ao + sn], pa[:, c0:c0 + sn], [pak], [abk])
                    cp("act", ab[:, ao - HC:ao], ahist[:, j, sg["sid"], :], ["ahist"], [abk])
                    if first_main and sg["sid"] == 0:
                        act(ab[:, ao:ao + 8], ab[:, ao:ao + 8], AF.Copy, [abk, "cst"], [abk], scale=hflag)
                    cp("act", ahist[:, j, sg["sid"], :], ab[:, ao + sn - HC:ao + sn], [abk], ["ahist"])
                w0 = prm[:, P_WDW + j * 3 + 0:P_WDW + j * 3 + 1]
                w1 = prm[:, P_WDW + j * 3 + 1:P_WDW + j * 3 + 2]
                w2 = prm[:, P_WDW + j * 3 + 2:P_WDW + j * 3 + 3]
                bd = prm[:, P_BDW + j:P_BDW + j + 1]
                act(cb[:, HC:wtot], ab[:, HC:wtot], AF.Identity, [abk, "prm"], [cbk], bias=bd, scale=w2)
                stt("dve", cb[:, HC:wtot], ab[:, HC - 1:wtot - 1], w1, cb[:, HC:wtot], ALU.mult, ALU.add, [abk, cbk, "prm"], [cbk])
                stt("dve", cb[:, HC:wtot], ab[:, HC - 2:wtot - 2], w0, cb[:, HC:wtot], ALU.mult, ALU.add, [abk, cbk, "prm"], [cbk])
                if pend:
                    stage_b(*pend.pop())
                pend.append((j, pg, pgk, cb, cbk))
        stage_b(*pend.pop())

    def ffn_down(n, xb, xk):
        for ot in range(8):
            wt_, wk_ = wget(f"dn{ot}")
            wv = wview(wt_, NFF, 128)
            p_, pk = proj(wv, wk_, range(NFF), slice(0, 128), lambda kc: actT[:, kc, 0:n], ["actT"], n)
            cp("act", m2T[:, ot, 0:n], p_[:, 0:n], [pk], ["m2T"])
        resid_norm(n, P_GP2, xb, xk)

    xbufs = [(xT, "xT"), (wstage[:, :].rearrange("p (a b) -> p a b", a=8), "wstage")]
    loaded = set()

    def load_x(d):
        if d["id"] in loaded:
            return
        loaded.add(d["id"])
        xb, xk = xbufs[d["par"]]
        for (src_ap, c0, sn) in d["xsrc"]:
            dma(xb[:, :, c0:c0 + sn], src_ap, "xld", [], [xk])

    def front(d, nxt=None, part="all"):
        kind, segs, n = d["kind"], d["segs"], d["n"]
        xb, xk = xbufs[d["par"]]
        use_h(d["par"])
        hT, hk_ = HB["h"], HB["hk"]
        if part in ("all", "A1"):
            load_x(d)
            if nxt is not None and kind == "prefix":
                load_x(nxt)
            norm_stats(xb, xk, n)
            make_h(xb, xk, n, 0)
            if part == "A1":
                return
        wt_, wk_ = wget("in0")
        wv = wview(wt_, 8, 512)
        for ot in range(4):
            p_, pk = proj(wv, wk_, range(8), slice(ot * 128, (ot + 1) * 128), lambda kc: hT[:, kc, 0:n], [hk_], n)
            cp("act", zs[:, ot, 0:n], p_[:, 0:n], [pk], ["zs"])
        if kind != "prefix" or d["last_prefix"]:
            wt_, wk_ = wget("in1")
            wv = wview(wt_, 8, 512)
            for ot in range(4):
                if kind == "prefix" and ot >= 2:
                    break
                p_, pk = proj(wv, wk_, range(8), slice(ot * 128, (ot + 1) * 128), lambda kc: hT[:, kc, 0:n], [hk_], n)
                if ot < 2:
                    for sg in segs:
                        cp("dve", up[:, ot, sg["po"]:sg["po"] + sg["n"]], p_[:, sg["c0"]:sg["c0"] + sg["n"]], [pk], ["up"])
                else:
                    cp("dve", qT[:, ot - 2, 0:n], p_[:, 0:n], [pk], ["qT"])
        if kind != "prefix":
            pooling(segs, n, segs[-1]["po"] + segs[-1]["n"], d["first_main"])
        s5_front(segs, n, prefix=(kind == "prefix"))
        if kind == "prefix" and d["last_prefix"]:
            sg = segs[0]
            cp("pool", phist[:, :, 0, :], up[:, :, sg["po"] + n - HP:sg["po"] + n], ["up"], ["phist"])

    def back_a1(d, nxt=None):
        segs, n = d["segs"], d["n"]
        use_h(d["par"])
        if nxt is not None:
            load_x(nxt)
        attention(segs, n)
        merge_branch(1, "first", n)
        merge_branch(2, "mid", n)
        wg_, wgk_ = wget("glu")
        s5_back(segs, n, wg_, wgk_)
        merge_branch(0, "last", n)
        merge_out(n)

    def back_a2(d):
        segs, n = d["segs"], d["n"]
        xb, xk = xbufs[d["par"]]
        use_h(d["par"])
        resid_norm(n, P_GP1, xb, xk)
        ffn_up(segs, n, d["first_main"], xb, xk)

    def back_b(d):
        n = d["n"]
        xb, xk = xbufs[d["par"]]
        ffn_down(n, xb, xk)
        for (dst_ap, c0, sn) in d["ydst"]:
            dma(dst_ap, xb[:, :, c0:c0 + sn], "yst", [xk], [])

    def drive(descs):
        loaded.clear()
        pre = [d for d in descs if d["kind"] == "prefix"]
        mains = [d for d in descs if d["kind"] != "prefix"]
        npre = len(pre)

        def a1(i):
            front(pre[i], pre[i + 1] if i + 1 < npre else mains[0], part="A1")

        if npre:
            a1(0)
            if npre > 1:
                a1(1)
            front(pre[0], part="A2")
            for i in range(1, npre):
                if i + 1 < npre:
                    a1(i + 1)
                prefix_wsum()
                front(pre[i], part="A2")
            prefix_wsum()
        front(mains[0])
        for j, d in enumerate(mains):
            nxt = mains[j + 1] if j + 1 < len(mains) else None
            back_a1(d, nxt)
            if nxt is not None:
                front(nxt)
            back_a2(d)
            back_b(d)

    identb = sb("identb", [128, 128], BF16)
    cp("dve", identb[:], ident, ["cst"], ["identb"])
    descs = []
    col = 0
    ycol = 0
    for p in range(NPRE + M):
        kind = "prefix" if p < NPRE else "main"
        descs.append(dict(id=p, kind=kind, par=p % 2, n=N, last_prefix=(p == NPRE - 1), first_main=(p == NPRE),
                          segs=[dict(sid=0, kv=0, c0=0, n=N, slot=0, po=HP, ao=HC)],
                          xsrc=[(xT_p[:, :, col:col + N], 0, N)],
                          ydst=[(yT_p[:, :, ycol:ycol + N], 0, N)] if kind == "main" else []))
        col += N
        if kind == "main":
            ycol += N
    fsegs = []
    c0 = slot = po = ao = 0
    for sid, sn in ((0, L), (1, 16), (2, 16)):
        po += HP
        ao += HC
        fsegs.append(dict(sid=sid, kv=sid, c0=c0, n=sn, slot=slot, po=po, ao=ao))
        c0 += sn
        slot += sn // L + 1
        po += sn
        ao += sn
    descs.append(dict(id=NPRE + M, kind="main", par=(NPRE + M) % 2, n=NF, last_prefix=False, first_main=False, segs=fsegs,
                      xsrc=[(xT_p[:, :, col:col + L], 0, L), (xT_s, L, 32)],
                      ydst=[(yT_p[:, :, ycol:ycol + L], 0, L), (yT_s, L, 32)]))
    wuse.append("kv")
    PLAN[0] = True
    R.enabled = False
    drive(descs)
    R.enabled = True
    PLAN[0] = False


    stage(4)
    dma(xT[:, :, 0:256], memT, "xld", [], ["xT"])
    norm_stats(xT, "xT", 256)
    stage(4.1)
    make_h(xT, "xT", 256, 2)
    stage(4.2)
    wt_, wk_ = wget("kv")
    stage(4.3)
    wv = wview(wt_, 8, 512)
    for hq in range(2):
        p_, pk = proj(wv, wk_, range(8), slice(hq * 128, (hq + 1) * 128), lambda kc: hT[:, kc, 0:256], ["hT"], 256)
        stage(4.31)
        cp("act", KT[:, 0, hq, :], p_[:, 0:256], [pk], ["KT"])
        stage(4.32)
        cp("dve", stg[:, hq, :], p_[:, 0:256], [pk], ["stg"])
        stage(4.33)
    dma(kout, stg[:], "o_k", ["stg"], [])
    stage(4.4)
    for mc in range(2):
        p_, pk = pmR.next()
        for kc in range(8):
            mm(p_[:, 0:256], hT[:, kc, mc * 128:(mc + 1) * 128], wv(kc, slice(256, 512)), kc == 0, kc == 7, ["hT", wk_], [pk])
        cp("act", Vb[:, 0, mc, :], p_[:, 0:256], [pk], ["Vb"])
        cp("dve", stg[:, mc, :], p_[:, 0:256], [pk], ["stg"])
    dma(vout, stg[:], "o_v", ["stg"], [])
    stage(4.5)
    for s in range(2):
        dma(stg[:], kTs[s], "c7", [], ["stg"])
        cp("dve", KT[:, 1 + s].rearrange("p a b -> p (a b)"), stg[:].rearrange("p a b -> p (a b)"), ["stg"], ["KT"])
        dma(stg[:], vs_in[s], "c7", [], ["stg"])
        cp("dve", Vb[:, 1 + s].rearrange("p a b -> p (a b)"), stg[:].rearrange("p a b -> p (a b)"), ["stg"], ["Vb"])

    stage(5)
    drive(descs)
    R.enabled = True
    dma(sfin, Sst[:], "o_s", ["Sst"], [])
    dma(poolo, phist[:], "o_p", ["phist"], [])
    dma(convo, ahist[:], "o_c", ["ahist"], [])
    R.final_wait("sp")

    sems = {name: st.enter_context(nc.semaphore(name)) for name in sorted(R.sems)}
    with nc.Block() as block:
        def mk(stream):
            def body(eng):
                for it in stream:
                    if it[0] == "w":
                        eng.wait_ge(sems[it[1]], it[2])
                    else:
                        ins = it[1](eng)
                        if it[2] is not None:
                            ins.then_inc(sems[it[2]], it[3])
            return body

        block.tensor(mk(R.streams["pe"]))
        block.scalar(mk(R.streams["act"]))
        block.vector(mk(R.streams["dve"]))
        block.gpsimd(mk(R.streams["pool"]))
        block.sync(mk(R.streams["sp"]))
    st.close()
    return nc


def fm(a):
    T, F = a.shape
    return np.ascontiguousarray(a.T.reshape(F // 128, 128, T).transpose(1, 0, 2))


def unfm(a):
    P, KC, T = a.shape
    return np.ascontiguousarray(a.transpose(1, 0, 2).reshape(KC * P, T).T)


def prep_inputs(inp, M):
    OWN = N * M
    SEQ = OWN * CPS
    f32 = np.float32
    g = lambda k: np.asarray(inp[k], f32)[0]
    W = {k: g(k) for k in ("w_in", "w_glu", "w_gate", "w_br_ssm", "w_br_pool", "w_br_mem", "w_out", "w_ffn_up", "w_ffn_down")}
    W["w_kv"] = np.concatenate([g("w_mem_k"), g("w_mem_v")], axis=1)
    wsrc = np.stack([unit_host(u, W) for u in UNITS])
    wp = g("w_pool")
    wpool = np.zeros((128, 2, 128), f32)
    for gi in range(4):
        tl, hf = gi // 2, gi % 2
        wpool[hf * 64:(hf + 1) * 64, tl, hf * 64:(hf + 1) * 64] = wp[gi]

    def pp(v):
        return v.reshape(-1, 128).T

    prm = np.concatenate([
        pp(g("b_gate")), pp(g("b_glu")), pp(g("pool_scale")), pp(g("g_post1")), pp(g("g_post2")),
        g("w_dw").T.reshape(NFF, 128, 3).transpose(1, 0, 2).reshape(128, 66), pp(g("b_dw")),
        np.tile(g("ssm_d").reshape(32, 16).T, (8, 1)),
        pp(g("g_pre1")), pp(g("g_pre2")), pp(g("g_mem")),
    ], axis=1).astype(f32)
    s5p = np.concatenate([
        g("ssm_lam_re").T, g("ssm_lam_im").T, np.tile(g("ssm_log_dt")[None, :], (64, 1)),
        g("ssm_b_re").transpose(1, 0, 2).reshape(64, 512), g("ssm_b_im").transpose(1, 0, 2).reshape(64, 512),
        g("ssm_c_re").transpose(2, 0, 1).reshape(64, 512), g("ssm_c_im").transpose(2, 0, 1).reshape(64, 512),
    ], axis=1).astype(f32)
    ident = np.eye(128, dtype=f32)
    ii = np.arange(128) // 16
    mask = (ii[None, :] >= ii[:, None]).astype(f32)
    xp = np.asarray(inp["x_prompt"], f32)
    xs = np.asarray(inp["x_sample"], f32)
    memp = np.asarray(inp["mem_prompt"], f32)
    ck = np.asarray(inp["cache_mem_k"], f32)[0]
    cv = np.asarray(inp["cache_mem_v"], f32)[0]
    sre = np.asarray(inp["state_ssm_re"], f32)[0]
    sim = np.asarray(inp["state_ssm_im"], f32)[0]
    spool = np.asarray(inp["state_pool"], f32)[0]
    sconv = np.asarray(inp["state_conv"], f32)[0]
    NPREC = (CPS - 1) * OWN
    maps = []
    for c in range(NCORES):
        b, k = c // CPS, c % CPS
        t0 = k * OWN
        idx = np.arange(t0 - L - NPREC, t0 + OWN)
        stream = np.zeros((idx.size, D), f32)
        v = idx >= 0
        stream[v] = xp[b, idx[v]]
        wins = np.array([2, 4, 8, 16], f32)
        pos = t0 + np.arange(16)
        invc = np.zeros((128, 2, 16), f32)
        for gi in range(4):
            tl, hf = gi // 2, gi % 2
            invc[hf * 64:(hf + 1) * 64, tl, :] = 1.0 / np.minimum(pos + 1, wins[gi])
        cst = np.concatenate([ident, mask, invc.reshape(128, 32), np.full((128, 1), 0.0 if k == 0 else 1.0, f32)], axis=1)
        ss = [2 * c, 2 * c + 1]
        m = {
            "xT_p": fm(stream),
            "xT_s": fm(xs[ss].reshape(32, D)),
            "memT": fm(memp[b]),
            "kTs": np.stack([ck[s].reshape(256, 256).T.reshape(2, 128, 256).transpose(1, 0, 2) for s in ss]),
            "vs": np.stack([cv[s].reshape(2, 128, 256).transpose(1, 0, 2) for s in ss]),
            "sst": np.stack([np.stack([sre[s].T, sim[s].T], axis=-1) for s in ss], axis=1),
            "ph": np.stack([spool[s].T.reshape(2, 128, HP).transpose(1, 0, 2) for s in ss], axis=2),
            "ch": np.stack([sconv[s].T.reshape(NFF, 128, HC).transpose(1, 0, 2) for s in ss], axis=2),
            "wsrc": wsrc, "wpool": wpool, "prm": prm, "s5p": s5p, "cst": cst.astype(f32),
        }
        maps.append({kk: np.ascontiguousarray(vv, dtype=f32) for kk, vv in m.items()})
    return maps


def assemble(res, M):
    OWN = N * M
    SEQ = OWN * CPS
    f32 = np.float32
    yp = np.zeros((2, SEQ, D), f32)
    ys = np.zeros((16, 16, D), f32)
    mk = np.zeros((1, 2, 256, 4, 64), f32)
    mv = np.zeros((1, 2, 256, 4, 64), f32)
    srp = np.zeros((1, 2, 32, 64), f32); sip = np.zeros((1, 2, 32, 64), f32)
    srs = np.zeros((1, 16, 32, 64), f32); sis = np.zeros((1, 16, 32, 64), f32)
    pp_ = np.zeros((1, 2, HP, 256), f32); ps_ = np.zeros((1, 16, HP, 256), f32)
    cp_ = np.zeros((1, 2, HC, DFF), f32); cs_ = np.zeros((1, 16, HC, DFF), f32)
    for c in range(NCORES):
        r = res[c]
        b, k = c // CPS, c % CPS
        y = unfm(np.asarray(r["yT_p"], f32))
        yp[b, k * OWN:(k + 1) * OWN] = y[L:L + OWN]
        ysm = unfm(np.asarray(r["yT_s"], f32))
        ys[2 * c] = ysm[0:16]
        ys[2 * c + 1] = ysm[16:32]
        sf = np.asarray(r["sfin"], f32)
        po = np.asarray(r["poolo"], f32)
        co = np.asarray(r["convo"], f32)

        def pool_of(sid):
            return po[:, :, sid, :].transpose(1, 0, 2).reshape(256, HP).T

        def conv_of(sid):
            return co[:, :, sid, :].transpose(1, 0, 2).reshape(DFF, HC).T

        if k == 0:
            kt = np.asarray(r["kout"], f32)
            mk[0, b] = kt.transpose(1, 0, 2).reshape(256, 256).T.reshape(256, 4, 64)
            vt = np.asarray(r["vout"], f32)
            mv[0, b] = vt.transpose(1, 0, 2).reshape(256, 4, 64)
        if k == CPS - 1:
            srp[0, b] = sf[:, 0, :, 0].T
            sip[0, b] = sf[:, 0, :, 1].T
            pp_[0, b] = pool_of(0)
            cp_[0, b] = conv_of(0)
        for j, s in enumerate((2 * c, 2 * c + 1)):
            srs[0, s] = sf[:, 1 + j, :, 0].T
            sis[0, s] = sf[:, 1 + j, :, 1].T
            ps_[0, s] = pool_of(1 + j)
            cs_[0, s] = conv_of(1 + j)
    return (yp, ys, mk, mv, srp, sip, srs, sis, pp_, ps_, cp_, cs_)


_CACHE = {}


def kernel(**inputs):
    M = 8
    maps = prep_inputs(inputs, M)
    if M not in _CACHE:
        _CACHE[M] = build(M)
    res = run_bass_kernel_spmd(_CACHE[M], maps, core_ids=list(range(NCORES)))
    return assemble(res.results, M)
```

```python
import math
from contextlib import ExitStack

import numpy as np
import concourse.bass as bass
import concourse.mybir as mybir
from concourse.bass_utils import run_bass_kernel_spmd

F32 = mybir.dt.float32
BF16 = mybir.dt.bfloat16
ALU = mybir.AluOpType
AF = mybir.ActivationFunctionType

D = 1024
DFF = 2816
NFF = 22
N = 256
L = 8
EPS = 1e-6
NCORES = 8
CPS = 4
HP = 15
HC = 2
UCAP = 4096

def unit_table():
    u = []
    u.append(("kv", "w_kv", 128, 8, [(0, 512)], 2))
    u.append(("in0", "w_in", 128, 8, [(0, 512)], 0))
    u.append(("in1", "w_in", 128, 8, [(512, 512)], 0))
    u.append(("glu", "w_glu", 128, 4, [(0, 512)], None))
    for b in range(3):
        for h in range(2):
            u.append((f"gate{b}{h}", "w_gate", 128, 8, [(b * 1024 + h * 512, 512)], 0))
    for h in range(2):
        u.append((f"brs{h}", "w_br_ssm", 128, 4, [(h * 512, 512)], None))
    u.append(("brp", "w_br_pool", 128, 2, [(0, 1024)], None))
    u.append(("brm", "w_br_mem", 64, 4, [(0, 1024)], None))
    for h in range(2):
        u.append((f"out{h}", "w_out", 128, 8, [(h * 512, 512)], None))
    for q in range(11):
        u.append((f"up{q}", "w_ffn_up", 128, 8, [(q * 256, 256), (DFF + q * 256, 256)], 1))
    for o in range(8):
        u.append((f"dn{o}", "w_ffn_down", 128, NFF, [(o * 128, 128)], None))
    return u


UNITS = unit_table()
UIDX = {u[0]: i for i, u in enumerate(UNITS)}
NU = len(UNITS)


def unit_host(u, W):
    name, key, P, KC, cols, gi = u
    w = W[key]
    parts = [w[:, c0:c0 + n] for (c0, n) in cols]
    w = np.concatenate(parts, axis=1) if len(parts) > 1 else parts[0]
    w = w[:P * KC]
    nc_ = w.shape[1]
    a = w.reshape(KC, P, nc_).transpose(1, 0, 2).reshape(P, KC * nc_)
    out = np.zeros((128, UCAP), np.float32)
    out[:P, :KC * nc_] = a
    return out


class Rec:
    ENG = ("pe", "act", "dve", "pool", "sp")

    def __init__(self):
        self.streams = {e: [] for e in self.ENG}
        self.cnt = {e: 0 for e in self.ENG}
        self.lastw = {}
        self.readers = {}
        self.waited = {e: {} for e in self.ENG}
        self.dcnt = {}
        self.sems = set()
        self.enabled = True

    def _need(self, eng, tok):
        if tok is None:
            return
        if tok[0] == "c":
            _, e, k = tok
            if e == eng and eng == "pe":
                return
            sem, val = "c_" + e, k
        else:
            sem, val = "d_" + tok[1], tok[2]
        if self.waited[eng].get(sem, 0) >= val:
            return
        self.waited[eng][sem] = val
        self.streams[eng].append(("w", sem, val))
        self.sems.add(sem)

    def _deps(self, eng, reads, writes):
        for r in reads:
            self._need(eng, self.lastw.get(r))
        for w in writes:
            self._need(eng, self.lastw.get(w))
            for t in self.readers.get(w, {}).values():
                self._need(eng, t)

    def _reg(self, tok, reads, writes):
        rk = (tok[0], tok[1])
        for r in reads:
            self.readers.setdefault(r, {})[rk] = tok
        for w in writes:
            self.lastw[w] = tok
            self.readers[w] = {}

    def op(self, eng, fn, reads=(), writes=(), signal=True):
        if not self.enabled:
            return
        self._deps(eng, reads, writes)
        if signal:
            self.cnt[eng] += 1
            tok = ("c", eng, self.cnt[eng])
            self.streams[eng].append(("o", fn, "c_" + eng, 1))
        else:
            tok = ("c", eng, self.cnt[eng] + 1)
            self.streams[eng].append(("o", fn, None, 0))
        self.sems.add("c_" + eng)
        self._reg(tok, reads, writes)

    def dma(self, eng, fn, semkey, reads=(), writes=()):
        if not self.enabled:
            return
        n = self.dcnt.get(semkey, 0)
        if n:
            self._need(eng, ("d", semkey, 16 * n))
        self._deps(eng, reads, writes)
        self.dcnt[semkey] = n + 1
        tok = ("d", semkey, 16 * (n + 1))
        self.streams[eng].append(("o", fn, "d_" + semkey, 16))
        self.sems.add("d_" + semkey)
        self._reg(tok, reads, writes)

    def final_wait(self, eng):
        for k, n in self.dcnt.items():
            self._need(eng, ("d", k, 16 * n))


class Rot:
    def __init__(self, items):
        self.items = items
        self.i = 0

    def next(self):
        it = self.items[self.i % len(self.items)]
        self.i += 1
        return it


STOP = None
CP_MODE = 1


def build(M, dbg=False):
    OWN = N * M
    NPRE = (CPS - 1) * M
    NS = N * NPRE + N * M + L
    NF = L + 32
    NYP = N * M + L
    NC = N // L

    nc = bass.Bass("TRN2", target_bir_lowering=False)
    R = Rec()
    st = ExitStack()

    def din(name, shape):
        return nc.dram_tensor(name, list(shape), F32, kind="ExternalInput").ap()

    def dout(name, shape):
        return nc.dram_tensor(name, list(shape), F32, kind="ExternalOutput").ap()

    xT_p = din("xT_p", [128, 8, NS])
    xT_s = din("xT_s", [128, 8, 32])
    memT = din("memT", [128, 8, 256])
    kTs = din("kTs", [2, 128, 2, 256])
    vs_in = din("vs", [2, 128, 2, 256])
    sst_in = din("sst", [64, 2, 32, 2])
    ph_in = din("ph", [128, 2, 2, HP])
    ch_in = din("ch", [128, NFF, 2, HC])
    wsrc = din("wsrc", [NU, 128, UCAP])
    wpool_in = din("wpool", [128, 2, 128])
    NPRM = 24 + 4 + 2 + 8 + 8 + 66 + 22 + 32 + 24
    prm_in = din("prm", [128, NPRM])
    s5p_in = din("s5p", [64, 96 + 4 * 512])
    cst_in = din("cst", [128, 128 + 128 + 32 + 1])

    yT_p = dout("yT_p", [128, 8, NYP])
    yT_s = dout("yT_s", [128, 8, 32])
    kout = dout("kout", [128, 2, 256])
    vout = dout("vout", [128, 2, 256])
    sfin = dout("sfin", [64, 3, 32, 2])
    poolo = dout("poolo", [128, 2, 3, HP])
    convo = dout("convo", [128, NFF, 3, HC])
    wbf = nc.dram_tensor("wbf", [NU, 128, UCAP], BF16, kind="Internal").ap()

    def sb(name, shape, dt=F32):
        return st.enter_context(nc.sbuf_tensor("s_" + name, list(shape), dt))

    def ps(name, shape, dt=F32):
        return st.enter_context(nc.psum_tensor("p_" + name, list(shape), dt))

    xT = sb("xT", [128, 8, N])
    hT = sb("hT", [128, 8, N], BF16)
    sq = [sb(f"sq{i}", [128, N], BF16) for i in range(2)]
    rstd = sb("rstd", [128, N])
    zs = sb("zs", [128, 4, N], BF16)
    WP = 3 * HP + NF + 8
    WPm = max(HP + N, WP)
    up = sb("up", [128, 2, WPm])
    b2 = sb("b2", [128, 2, WPm])
    b4 = sb("b4", [128, 2, WPm])
    dfT = sb("dfT", [128, 2, N], BF16)
    ypT = sb("ypT", [128, 2, N], BF16)
    qT = sb("qT", [128, 2, N], BF16)
    Zc = sb("Zc", [64, 4096], BF16)
    U = sb("U", [128, 32, NC], BF16)
    Yg = sb("Yg", [128, 32, NC], BF16)
    NSLOT = NC + 4
    Sall = sb("Sall", [64, NSLOT, 32, 2])
    Sbf = sb("Sbf", [64, NSLOT, 32, 2], BF16)
    ct1 = sb("ct1", [64, 32, 2])
    ct2 = sb("ct2", [64, 32, 2])
    ysT = sb("ysT", [128, 4, N], BF16)
    tb = [sb(f"tb{i}", [128, N], BF16) for i in range(3)]
    yss = sb("yss", [128, 4, N], BF16)
    PT = [sb(f"PT{i}", [128, N], BF16) for i in range(4)]
    rden = sb("rden", [64, N])
    ymT = sb("ymT", [64, 4, N], BF16)
    tf = [sb(f"tf{i}", [128, N]) for i in range(2)]
    mgT = sb("mgT", [128, 8, N], BF16)
    gT = sb("gT", [128, 8, N], BF16)
    m2T = sb("m2T", [128, 8, N])
    WA = max(HC + N, 3 * HC + NF)
    abuf = [sb(f"abuf{i}", [128, WA]) for i in range(2)]
    cbuf = [sb(f"cbuf{i}", [128, WA]) for i in range(2)]
    actT = sb("actT", [128, NFF, N], BF16)
    wsl = [sb(f"wsl{i}", [128, UCAP], BF16) for i in range(3)]
    wstage = sb("wstage", [128, UCAP // 2])
    Tm = sb("Tm", [128, 32, 128], BF16)
    Pm = sb("Pm", [64, 2, 32, 128], BF16)
    Qm = sb("Qm", [128, 2, 32, 64], BF16)
    KT = sb("KT", [128, 3, 2, 256], BF16)
    Vb = sb("Vb", [128, 3, 2, 256], BF16)
    wpl = sb("wpl", [128, 2, 128], BF16)
    prm = sb("prm", [128, NPRM])
    cst = sb("cst", [128, 289])
    onesb = sb("onesb", [128, 128], BF16)
    epsc_t = sb("epsc", [128, 1])
    epsc = epsc_t[:, 0:1]
    Sst = sb("Sst", [64, 3, 32, 2])
    phist = sb("phist", [128, 2, 3, HP])
    ahist = sb("ahist", [128, NFF, 3, HC])
    A1 = sb("A1", [64, 32, 2])
    A2n = sb("A2n", [64, 32])
    A2p = sb("A2p", [64, 32])
    A32 = sb("A32", [64, 3, 32])
    accA = sb("accA", [64, 32, 2])
    accB = sb("accB", [64, 32, 2])
    s5p = sb("s5p", [64, 96 + 2048])
    tmpT = sb("tmpT", [128, 4, 128])
    stg = sb("stg", [128, 2, 256])
    sm = Sall[:].rearrange("p a b c -> p (a b c)")[:, 0:1536].rearrange("p (a b) -> p a b", a=48)
    pws = wstage[0:64, 0:1536].rearrange("p (a b c) -> p a b c", a=6, b=32)
    bb0 = stg[0:64].rearrange("p a b -> p (a b)")
    bb1 = wstage[0:64, 1536:2048]

    pm = [ps(f"pm{i}", [128, 512]) for i in range(5)]
    pt = [ps(f"pt{i}", [128, 1024], BF16) for i in range(2)]
    pst = ps("pst", [128, 512])
    pmR = Rot([(pm[i], f"pm{i}") for i in range(5)])
    ptR = Rot([(pt[i], f"pt{i}") for i in range(2)])
    sqR = Rot([(sq[i], f"sq{i}") for i in range(2)])
    tbR = Rot([(tb[i], f"tb{i}") for i in range(3)])
    tfR = Rot([(tf[i], f"tf{i}") for i in range(2)])
    PTR = Rot([(PT[i], f"PT{i}") for i in range(4)])
    abR = Rot([(abuf[i], cbuf[i], f"ab{i}", f"cb{i}") for i in range(2)])
    evR = Rot(["act", "dve"])
    ewR = Rot(["dve", "pool"])

    o = 0
    P_BGATE = o; o += 24
    P_BGLU = o; o += 4
    P_PSC = o; o += 2
    P_GP1 = o; o += 8
    P_GP2 = o; o += 8
    P_WDW = o; o += 66
    P_BDW = o; o += 22
    P_DD = o; o += 32
    P_GV = o; o += 24
    ident = cst[:, 0:128]
    mask = cst[:, 128:256]
    invc = cst[:, 256:288]
    hflag = cst[:, 288:289]

    def E(eng, reads, writes, fn, signal=True):
        if eng != "pe":
            writes = list(writes) + [k for k in reads if k.startswith("pm") or k.startswith("pt") or k == "pst"]
        R.op(eng, fn, reads, writes, signal)

    def tt(eng, out, a, b, op, reads, writes):
        E(eng, reads, writes, lambda e, out=out, a=a, b=b, op=op: e.tensor_tensor(out=out, in0=a, in1=b, op=op))

    def ts(eng, out, a, s1, s2, op0, op1, reads, writes):
        if s2 is None:
            E(eng, reads, writes, lambda e, out=out, a=a, s1=s1, op0=op0:
              e.tensor_single_scalar(out=out, in_=a, scalar=s1, op=op0))
        else:
            E(eng, reads, writes, lambda e, out=out, a=a, s1=s1, s2=s2, op0=op0, op1=op1:
              e.tensor_scalar(out=out, in0=a, scalar1=s1, scalar2=s2, op0=op0, op1=op1))

    def stt(eng, out, a, s, b, op0, op1, reads, writes):
        E(eng, reads, writes, lambda e, out=out, a=a, s=s, b=b, op0=op0, op1=op1:
          e.scalar_tensor_tensor(out=out, in0=a, scalar=s, in1=b, op0=op0, op1=op1))

    def cp(eng, out, a, reads, writes):
        if eng == "act":
            E(eng, reads, writes, lambda e, out=out, a=a: e.copy(out=out, in_=a))
        elif eng == "dve" and CP_MODE == 1:
            E(eng, reads, writes, lambda e, out=out, a=a: e.tensor_single_scalar(out=out, in_=a, scalar=1.0, op=ALU.mult))
        else:
            E(eng, reads, writes, lambda e, out=out, a=a: e.tensor_copy(out=out, in_=a))

    def act(out, a, func, reads, writes, bias=None, scale=None):
        kw = {}
        if bias is not None:
            kw["bias"] = bias
        if scale is not None:
            kw["scale"] = scale
        E("act", reads, writes, lambda e, out=out, a=a, func=func, kw=kw: e.activation(out=out, in_=a, func=func, **kw))

    def mm(out, lhsT, rhs, start, stop, reads, writes, sig=True):
        E("pe", reads, writes, lambda e, out=out, lhsT=lhsT, rhs=rhs, start=start, stop=stop:
          e.matmul(out, lhsT, rhs, start=start, stop=stop), signal=sig)

    def tr(out, in_, idn, reads, writes):
        E("pe", reads, writes, lambda e, out=out, in_=in_, idn=idn: e.transpose(out, in_, idn))

    def dma(out, in_, semkey, reads, writes, eng="sp"):
        R.dma(eng, lambda e, out=out, in_=in_: e.dma_start(out=out, in_=in_), semkey, reads, writes)

    def memset(eng, ap, val, writes):
        E(eng, [], writes, lambda e, ap=ap, val=val: e.memset(ap, val))

    def stage(k):
        if STOP is not None and k >= STOP:
            R.enabled = False

    dma(prm[:], prm_in, "c0", [], ["prm"])
    dma(cst[:], cst_in, "c1", [], ["cst"])
    dma(s5p[:], s5p_in, "c2", [], ["s5p"])
    dma(wstage[:, 0:256], wpool_in.rearrange("p a b -> p (a b)"), "c3", [], ["wstage"])
    cp("dve", wpl[:].rearrange("p a b -> p (a b)"), wstage[:, 0:256], ["wstage"], ["wpl"])
    memset("dve", onesb[:], 1.0, ["onesb"])
    memset("dve", epsc_t[:], EPS, ["epsc"])
    memset("dve", Sst[:, 0], 0.0, ["Sst"])
    memset("pool", phist[:, :, 0], 0.0, ["phist"])
    memset("pool", ahist[:, :, 0], 0.0, ["ahist"])
    dma(Sst[:, 1:3], sst_in, "c4", [], ["Sst"])
    dma(phist[:, :, 1:3], ph_in, "c5", [], ["phist"])
    dma(ahist[:, :, 1:3], ch_in, "c6", [], ["ahist"])

    first_use = ["kv", "in0", "in1", "glu", "gate00", "gate01", "brs0", "brs1", "gate10", "gate11", "brp", "gate20", "gate21",
                 "brm", "out0", "out1"] + [f"up{q}" for q in range(11)] + [f"dn{o}" for o in range(8)]
    assert sorted(first_use) == sorted(UIDX)
    for nm in first_use:
        ui = UIDX[nm]
        u = UNITS[ui]
        tot = u[3] * sum(n for _, n in u[4])
        dma(wbf[ui, :, 0:tot], wsrc[ui, :, 0:tot], f"cv{ui}", [], [f"wbf{ui}"], eng="pool")

    stage(1)
    smi = [0]

    def smn():
        i = smi[0]
        smi[0] += 1
        return sm[:, i, :]

    K_ = ["Sall"]
    lr = s5p[:, 0:32]
    li = s5p[:, 32:64]
    ldt = s5p[:, 64:96]
    dt_ = smn()
    act(dt_, ldt, AF.Exp, ["s5p"], K_)
    x1 = smn(); ang = smn(); q_ = smn(); mag = smn()
    tt("dve", x1, lr, dt_, ALU.mult, ["s5p"] + K_, K_)
    tt("dve", ang, li, dt_, ALU.mult, ["s5p"] + K_, K_)
    ts("dve", q_, x1, 1.0 / 120, None, ALU.mult, None, K_, K_)
    for c in (1.0 / 24, 1.0 / 6, 0.5, 1.0):
        stt("dve", q_, q_, c, x1, ALU.add, ALU.mult, K_, K_)
    ts("dve", mag, q_, 1.0, None, ALU.add, None, K_, K_)
    yq = smn(); u_ = smn(); s_ = smn(); c_ = smn(); t_ = smn()
    MAGIC = 12582912.0
    ts("dve", t_, ang, 1.0 / (2 * math.pi), MAGIC, ALU.mult, ALU.add, K_, K_)
    ts("dve", t_, t_, -MAGIC, None, ALU.add, None, K_, K_)
    stt("dve", yq, t_, -2 * math.pi, ang, ALU.mult, ALU.add, K_, K_)
    ts("dve", t_, yq, math.pi, None, ALU.is_gt, None, K_, K_)
    stt("dve", yq, t_, -2 * math.pi, yq, ALU.mult, ALU.add, K_, K_)
    ts("dve", t_, yq, -math.pi, None, ALU.is_lt, None, K_, K_)
    stt("dve", yq, t_, 2 * math.pi, yq, ALU.mult, ALU.add, K_, K_)
    ts("dve", yq, yq, 0.25, None, ALU.mult, None, K_, K_)
    tt("dve", u_, yq, yq, ALU.mult, K_, K_)
    ts("dve", q_, u_, 1.0 / 362880, None, ALU.mult, None, K_, K_)
    for c in (-1.0 / 5040, 1.0 / 120, -1.0 / 6):
        stt("dve", q_, q_, c, u_, ALU.add, ALU.mult, K_, K_)
    stt("dve", s_, q_, 1.0, yq, ALU.add, ALU.mult, K_, K_)
    ts("dve", q_, u_, -1.0 / 3628800, None, ALU.mult, None, K_, K_)
    for c in (1.0 / 40320, -1.0 / 720, 1.0 / 24, -0.5):
        stt("dve", q_, q_, c, u_, ALU.add, ALU.mult, K_, K_)
    ts("dve", c_, q_, 1.0, None, ALU.add, None, K_, K_)
    for _ in range(2):
        tt("dve", t_, s_, c_, ALU.mult, K_, K_)
        tt("dve", q_, s_, s_, ALU.mult, K_, K_)
        ts("dve", s_, t_, 2.0, None, ALU.mult, None, K_, K_)
        ts("dve", c_, q_, -2.0, 1.0, ALU.mult, ALU.add, K_, K_)
    pwr = [smn() for _ in range(9)]
    pwi = [smn() for _ in range(9)]
    memset("dve", pwr[0], 1.0, K_)
    memset("dve", pwi[0], 0.0, K_)
    tt("dve", pwr[1], mag, c_, ALU.mult, K_, K_)
    tt("dve", pwi[1], mag, s_, ALU.mult, K_, K_)

    def cmul(or_, oi_, ar, ai, br, bi, t1, t2, keys, eng="dve", wkeys=None):
        wk = keys if wkeys is None else wkeys
        tt(eng, t1, ar, br, ALU.mult, keys, wk)
        tt(eng, t2, ai, bi, ALU.mult, keys, wk)
        tt(eng, or_, t1, t2, ALU.subtract, keys + wk, wk)
        tt(eng, t1, ar, bi, ALU.mult, keys, wk)
        tt(eng, t2, ai, br, ALU.mult, keys, wk)
        tt(eng, oi_, t1, t2, ALU.add, keys + wk, wk)

    ta = smn(); tb_ = smn()
    for k in range(2, 9):
        cmul(pwr[k], pwi[k], pwr[k - 1], pwi[k - 1], pwr[1], pwi[1], ta, tb_, K_)
    den = smn(); nr = smn(); zr = smn(); zi = smn()
    tt("dve", den, lr, lr, ALU.mult, ["s5p"] + K_, K_)
    tt("dve", ta, li, li, ALU.mult, ["s5p"] + K_, K_)
    tt("dve", den, den, ta, ALU.add, K_, K_)
    E("dve", K_, K_, lambda e, den=den: e.reciprocal(out=den, in_=den))
    ts("dve", nr, pwr[1], -1.0, None, ALU.add, None, K_, K_)
    tt("dve", ta, nr, lr, ALU.mult, ["s5p"] + K_, K_)
    tt("dve", tb_, pwi[1], li, ALU.mult, ["s5p"] + K_, K_)
    tt("dve", ta, ta, tb_, ALU.add, K_, K_)
    tt("dve", zr, ta, den, ALU.mult, K_, K_)
    tt("dve", ta, pwi[1], lr, ALU.mult, ["s5p"] + K_, K_)
    tt("dve", tb_, nr, li, ALU.mult, ["s5p"] + K_, K_)
    tt("dve", ta, ta, tb_, ALU.subtract, K_, K_)
    tt("dve", zi, ta, den, ALU.mult, K_, K_)
    i7r = smn(); i7i = smn()
    tt("dve", ta, pwr[7], pwr[7], ALU.mult, K_, K_)
    tt("dve", tb_, pwi[7], pwi[7], ALU.mult, K_, K_)
    tt("dve", ta, ta, tb_, ALU.add, K_, K_)
    E("dve", K_, K_, lambda e, ta=ta: e.reciprocal(out=ta, in_=ta))
    tt("dve", i7r, pwr[7], ta, ALU.mult, K_, K_)
    tt("dve", i7i, pwi[7], ta, ALU.mult, K_, K_)
    ts("dve", i7i, i7i, -1.0, None, ALU.mult, None, K_, K_)
    KA = ["Achain"]
    cp("dve", A1[:, :, 0], pwr[8], K_, KA)
    cp("dve", A1[:, :, 1], pwr[8], K_, KA)
    cp("dve", A2p[:], pwi[8], K_, KA)
    ts("dve", A2n[:], pwi[8], -1.0, None, ALU.mult, None, K_, KA)
    KP = ["wstage"]
    zpr = smn(); zpi = smn()
    for i in range(8):
        cp("dve", pws[:, 0, :, i], pwr[7 - i], K_, KP)
        cp("dve", pws[:, 1, :, i], pwi[7 - i], K_, KP)
        cmul(zpr, zpi, pwr[i], pwi[i], i7r, i7i, ta, tb_, K_)
        cp("dve", pws[:, 2, :, i], zpr, K_, KP)
        cp("dve", pws[:, 3, :, i], zpi, K_, KP)
        cp("dve", pws[:, 4, :, i], pwr[i + 1], K_, KP)
        cp("dve", pws[:, 5, :, i], pwi[i + 1], K_, KP)
        bre = s5p[:, 96:608].rearrange("p (g h) -> p g h", h=16)
    bim = s5p[:, 608:1120].rearrange("p (g h) -> p g h", h=16)
    cre = s5p[:, 1120:1632].rearrange("p (g h) -> p g h", h=16)
    cim = s5p[:, 1632:2144].rearrange("p (g h) -> p g h", h=16)
    BBr = bb0.rearrange("p (g h) -> p g h", h=16)
    BBi = bb1.rearrange("p (g h) -> p g h", h=16)
    zrb = zr.unsqueeze(2).broadcast_to([64, 32, 16])
    zib = zi.unsqueeze(2).broadcast_to([64, 32, 16])
    m2f = m2T[0:64].rearrange("p a b -> p (a b)").rearrange("p (a b) -> p a b", a=4)
    xTf = xT[0:64].rearrange("p a b -> p (a b)").rearrange("p (a b) -> p a b", a=4)

    def bgv(i):
        return m2f[:, i, :] if i < 4 else xTf[:, i - 4, :]

    t512a = bgv(6).rearrange("p (g h) -> p g h", h=16)
    t512b = bgv(7).rearrange("p (g h) -> p g h", h=16)
    cmul(BBr, BBi, zrb, zib, bre, bim, t512a, t512b, ["s5p", "stg", "wstage", "m2T", "xT", "Sall"])

    def v4(ap2):
        return ap2.rearrange("p (g i h) -> p g i h", g=4, i=8)

    KG = ["m2T", "xT", "wstage", "stg", "s5p"]
    for e8 in range(8):
        g0 = 4 * e8
        XR, XI, ZR, ZI, PR, PI, TA, TB = [v4(bgv(i)) for i in range(8)]

        def pwb(idx):
            return pws[:, idx, g0:g0 + 4, :].unsqueeze(3).broadcast_to([64, 4, 8, 16])

        def vb(ap3):
            return ap3[:, g0:g0 + 4, :].unsqueeze(2).broadcast_to([64, 4, 8, 16])

        cmul(XR, XI, pwb(0), pwb(1), vb(BBr), vb(BBi), TA, TB, KG)
        cmul(ZR, ZI, pwb(2), pwb(3), vb(cre), vb(cim), TA, TB, KG)
        cmul(PR, PI, pwb(4), pwb(5), vb(cre), vb(cim), TA, TB, KG)
        cp("dve", Pm[:, 0, g0:g0 + 4, :], bgv(4).rearrange("p (g x) -> p g x", g=4), KG, ["Pm"])
        ts("dve", Pm[:, 1, g0:g0 + 4, :], bgv(5).rearrange("p (g x) -> p g x", g=4), -1.0, None,
           ALU.mult, None, KG, ["Pm"])
        ts("dve", bgv(3), bgv(3), -1.0, None, ALU.mult, None, KG, KG)
        pq, pqk = pmR.next()
        for gl in range(4):
            for ri in range(2):
                src = bgv(ri)[:, gl * 128:(gl + 1) * 128]
                tr(pq[:, (gl * 2 + ri) * 64:(gl * 2 + ri + 1) * 64], src, ident[0:64, 0:64], KG + ["cst"], [pqk])
        cp("act", Qm[:, :, g0:g0 + 4, :].rearrange("p r g n -> p g r n"),
           pq[:, :].rearrange("p (g r n) -> p g r n", g=4, r=2), [pqk], ["Qm"])
        pT_, pTk = pmR.next()
        for gl in range(4):
            sl = slice(gl * 128, (gl + 1) * 128)
            mm(pT_[:, sl], bgv(0)[:, sl], bgv(2)[:, sl], True, False, KG, [pTk])
            mm(pT_[:, sl], bgv(1)[:, sl], bgv(3)[:, sl], False, True, KG, [pTk])
        tt("dve", tmpT[:], pT_[:, :].rearrange("p (g x) -> p g x", g=4),
           mask.unsqueeze(1).broadcast_to([128, 4, 128]), ALU.mult, [pTk, "cst"], ["tmpT"])
        for gl in range(4):
            g = g0 + gl
            stt("dve", Tm[:, g, :], ident, prm[:, P_DD + g:P_DD + g + 1], tmpT[:, gl, :], ALU.mult, ALU.add,
                ["cst", "prm", "tmpT"], ["Tm"])

    Pre = s5p[:, 0:1024].rearrange("p (c g) -> p c g", c=32)
    Pim = s5p[:, 1024:2048].rearrange("p (c g) -> p c g", c=32)
    cur_r = smn(); cur_i = smn(); nx_r = smn(); nx_i = smn()
    KT_ = ["Sall", "s5p"]
    memset("pool", cur_r, 1.0, K_)
    memset("pool", cur_i, 0.0, K_)
    ta2 = smn(); tb2 = smn()
    for k in range(32):
        cp("pool", Pre[:, 31 - k, :], cur_r, K_, ["s5p"])
        cp("pool", Pim[:, 31 - k, :], cur_i, K_, ["s5p"])
        cmul(nx_r, nx_i, cur_r, cur_i, pwr[8], pwi[8], ta2, tb2, K_, eng="pool")
        cur_r, cur_i, nx_r, nx_i = nx_r, nx_i, cur_r, cur_i
    cp("pool", A32[:, 0], cur_r, K_, ["A32"])
    cp("pool", A32[:, 1], cur_i, K_, ["A32"])
    ts("pool", A32[:, 2], cur_i, -1.0, None, ALU.mult, None, K_, ["A32"])
    stage(2)
    stage(3)
    wuse = []
    wstate = {"next_load": 0, "slot_of": {}}

    def plan_pass(kind, last_prefix=False):
        if kind == "prefix":
            return ["in0"] + (["in1"] if last_prefix else [])
        lst = ["in0", "in1", "gate10", "gate11", "brp", "gate20", "gate21", "brm", "glu", "gate00", "gate01", "brs0", "brs1"]
        lst += ["out0", "out1"] + [f"up{q}" for q in range(11)] + [f"dn{o}" for o in range(8)]
        return lst

    def issue_loads(upto):
        while wstate["next_load"] < min(upto, len(wuse)):
            i = wstate["next_load"]
            name = wuse[i]
            ui = UIDX[name]
            u = UNITS[ui]
            tot = u[3] * sum(n for _, n in u[4])
            s = i % 3
            dma(wsl[s][:, 0:tot], wbf[ui, :, 0:tot], f"wld{s}", [f"wbf{ui}"], [f"wsl{s}"])
            wstate["next_load"] += 1

    wptr = [0]

    PLAN = [False]

    def wget(name):
        if PLAN[0]:
            wuse.append(name)
            return wsl[0], "wsl0"
        i = wptr[0]
        assert wuse[i] == name, (wuse[i], name, i)
        issue_loads(i + 3)
        wptr[0] += 1
        return wsl[i % 3], f"wsl{i % 3}"

    def norm_stats(src, srck, n):
        for kc in range(8):
            s_, sk = sqR.next()
            act(s_[:, 0:n], src[:, kc, 0:n], AF.Square, [srck], [sk])
            mm(pst[:, 0:n], onesb[:], s_[:, 0:n], kc == 0, kc == 7, ["onesb", sk], ["pst"])
        act(rstd[:, 0:n], pst[:, 0:n], AF.Sqrt, ["pst", "epsc"], ["rstd"], bias=epsc, scale=1.0 / D)
        E("dve", ["rstd"], ["rstd"], lambda e, n=n: e.reciprocal(out=rstd[:, 0:n], in_=rstd[:, 0:n]))

    HB = {"h": hT, "hk": "hT", "g": gT, "gk": "gT"}

    def use_h(par):
        if par == 0:
            HB.update(h=hT, hk="hT", g=gT, gk="gT")
        else:
            HB.update(h=gT, hk="gT", g=hT, gk="hT")

    def make_h(src, srck, n, gi):
        hT, hk_ = HB["h"], HB["hk"]
        for kc in range(8):
            gcol = prm[:, P_GV + gi * 8 + kc:P_GV + gi * 8 + kc + 1]
            stt("dve", hT[:, kc, 0:n], src[:, kc, 0:n], gcol, rstd[:, 0:n], ALU.mult, ALU.mult, [srck, "rstd", "prm"], [hk_])

    def proj(wt, wk, kcs, colsel, rhs_fn, rhs_keys, n, part=128):
        p_, pk = pmR.next()
        for i, kc in enumerate(kcs):
            mm(p_[:, 0:n], wt(kc, colsel), rhs_fn(kc), i == 0, i == len(kcs) - 1, [wk] + rhs_keys, [pk],
               sig=(i == len(kcs) - 1))
        return p_, pk

    def wview(wt, kcn, ncol, P_=128):
        v = wt[0:P_, 0:kcn * ncol].rearrange("p (k c) -> p k c", k=kcn)
        return lambda kc, cs: v[:, kc, cs]

    CH = "pool"

    def s5_front(segs, n, prefix=False):
        ncn = n // L
        evn = (lambda: "act") if prefix else evR.next
        for i0 in range(0, 8, 2):
            p_, pk = ptR.next()
            for ii in range(2):
                for f in range(4):
                    src = zs[:, f, 0:n].rearrange("p (c i) -> p i c", i=8)[:, i0 + ii, :]
                    tr(p_[0:ncn, (ii * 4 + f) * 128:(ii * 4 + f + 1) * 128], src, identb[:, :], ["zs", "identb"], [pk])
            cp(evn(), Zc[0:ncn, :].rearrange("p (g i h) -> p i g h", g=32, i=8)[:, i0:i0 + 2, :, :],
               p_[0:ncn, :].rearrange("p (i g h) -> p i g h", i=2, g=32), [pk], ["Zc"])
        for gh in range(2):
            p_, pk = ptR.next()
            for gl in range(16):
                g = gh * 16 + gl
                src = Zc[0:ncn, g * 128:(g + 1) * 128]
                tr(p_[:, gl * 64:gl * 64 + ncn], src, identb[0:ncn, 0:ncn], ["Zc", "identb"], [pk])
            cp(evn(), U[:, gh * 16:(gh + 1) * 16, 0:ncn],
               p_[:, :].rearrange("p (g c) -> p g c", g=16)[:, :, 0:ncn], [pk], ["U"])
        GB = 8 if ncn <= 32 else 4
        cpad = 512 // (GB * 2)
        for g0 in range(0, 32, GB):
            p_, pk = pmR.next()
            for gl in range(GB):
                for ri in range(2):
                    mm(p_[0:64, (gl * 2 + ri) * cpad:(gl * 2 + ri) * cpad + ncn], Qm[:, ri, g0 + gl, :], U[:, g0 + gl, 0:ncn],
                       True, True, ["Qm", "U"], [pk])
            pv = p_[0:64, :].rearrange("p (g r c) -> p c g r", g=GB, r=2)
            for sg in segs:
                c0 = sg["c0"] // L
                cn = sg["n"] // L
                cp(evn(), Sall[:, sg["slot"] + 1:sg["slot"] + 1 + cn, g0:g0 + GB, :], pv[:, c0:c0 + cn, :, :],
                   [pk, ], ["Sall"])
        if prefix:
            return
        for sg in segs:
            b = sg["slot"]
            cn = sg["n"] // L
            cp(CH, Sall[:, b], Sst[:, sg["sid"]], ["Sst"], ["Sall"])
            for c in range(cn):
                s0 = Sall[:, b + c]
                s1 = Sall[:, b + c + 1]
                tt(CH, ct1[:], s0, A1[:], ALU.mult, ["Sall", "Achain"], ["ct1"])
                tt(CH, ct2[:, :, 0], s0[:, :, 1], A2n[:], ALU.mult, ["Sall", "Achain"], ["ct2"])
                tt(CH, ct2[:, :, 1], s0[:, :, 0], A2p[:], ALU.mult, ["Sall", "Achain"], ["ct2"])
                tt(CH, s1, s1, ct1[:], ALU.add, ["Sall", "ct1"], ["Sall"])
                tt(CH, s1, s1, ct2[:], ALU.add, ["Sall", "ct2"], ["Sall"])
            cp(CH, Sst[:, sg["sid"]], Sall[:, b + cn], ["Sall"], ["Sst"])

    def prefix_wsum():
        Wv = Sall[:, 1:33]
        t1 = m2T[0:64].rearrange("p a b -> p (a b)").rearrange("p (c g r) -> p c g r", c=32, g=32)
        S0 = Sst[:, 0]
        for tab, acc in ((Pre, accA), (Pim, accB)):
            tt("dve", t1, Wv, tab.unsqueeze(3).broadcast_to([64, 32, 32, 2]), ALU.mult, ["Sall", "s5p"], ["m2T"])
            E("dve", ["m2T"], ["acc"], lambda e, acc=acc, t1=t1: e.tensor_reduce(
                out=acc[:], in_=t1.rearrange("p c g r -> p g r c"), axis=mybir.AxisListType.X, op=ALU.add))
        tt("dve", ct1[:], S0, A32[:, 0].unsqueeze(2).broadcast_to([64, 32, 2]), ALU.mult, ["Sst", "A32"], ["ct1"])
        tt("dve", ct2[:, :, 0], S0[:, :, 1], A32[:, 2], ALU.mult, ["Sst", "A32"], ["ct2"])
        tt("dve", ct2[:, :, 1], S0[:, :, 0], A32[:, 1], ALU.mult, ["Sst", "A32"], ["ct2"])
        tt("dve", S0, ct1[:], ct2[:], ALU.add, ["ct1", "ct2"], ["Sst"])
        tt("dve", S0, S0, accA[:], ALU.add, ["Sst", "acc"], ["Sst"])
        tt("dve", S0[:, :, 0], S0[:, :, 0], accB[:, :, 1], ALU.subtract, ["Sst", "acc"], ["Sst"])
        tt("dve", S0[:, :, 1], S0[:, :, 1], accB[:, :, 0], ALU.add, ["Sst", "acc"], ["Sst"])

    def s5_back(segs, n, wglu, wgk):
        ncn = n // L
        nslot = segs[-1]["slot"] + segs[-1]["n"] // L + 1
        cp("act", Sbf[:, 0:nslot], Sall[:, 0:nslot], ["Sall"], ["Sbf"])
        for g0 in range(0, 32, 8):
            p_, pk = pmR.next()
            for gl in range(8):
                g = g0 + gl
                mm(p_[:, gl * 64:gl * 64 + ncn], Tm[:, g, :], U[:, g, 0:ncn], True, False, ["Tm", "U"], [pk])
                for si, sg in enumerate(segs):
                    c0 = sg["c0"] // L
                    cn = sg["n"] // L
                    for ri in range(2):
                        last = (si == len(segs) - 1) and ri == 1
                        mm(p_[:, gl * 64 + c0:gl * 64 + c0 + cn], Pm[:, ri, g, :],
                           Sbf[:, sg["slot"]:sg["slot"] + cn, g, ri], False, last, ["Pm", "Sbf"], [pk])
            act(Yg[:, g0:g0 + 8, 0:ncn], p_[:, :].rearrange("p (g c) -> p g c", g=8)[:, :, 0:ncn], AF.Gelu, [pk], ["Yg"])
        for g0 in range(0, 32, 8):
            p_, pk = ptR.next()
            for gl in range(8):
                tr(p_[0:ncn, gl * 128:(gl + 1) * 128], Yg[:, g0 + gl, 0:ncn], identb[:, :], ["Yg", "identb"], [pk])
            cp(evR.next(), Zc[0:ncn, :].rearrange("p (j g h) -> p g j h", j=8, g=32)[:, g0:g0 + 8, :, :],
               p_[0:ncn, :].rearrange("p (g j h) -> p g j h", g=8, j=8), [pk], ["Zc"])
        for j0 in range(0, 8, 2):
            p_, pk = ptR.next()
            for jj in range(2):
                for f in range(4):
                    tr(p_[:, (jj * 4 + f) * 64:(jj * 4 + f) * 64 + ncn], Zc[0:ncn, (j0 + jj) * 512 + f * 128:(j0 + jj) * 512 + (f + 1) * 128],
                       identb[0:ncn, 0:ncn], ["Zc", "identb"], [pk])
            src = p_[:, 0:512].rearrange("p (j f c) -> p f c j", j=2, f=4)[:, :, 0:ncn, :]
            dst = ysT[:, :, 0:n].rearrange("p f (c j) -> p f c j", j=8)[:, :, :, j0:j0 + 2]
            cp(evR.next(), dst, src, [pk], ["ysT"])
        wv = wview(wglu, 4, 512)
        for ot in range(4):
            p_, pk = proj(wv, wgk, range(4), slice(ot * 128, (ot + 1) * 128), lambda kc: ysT[:, kc, 0:n], ["ysT"], n)
            t_, tk = tbR.next()
            act(t_[:, 0:n], p_[:, 0:n], AF.Sigmoid, [pk, "prm"], [tk], bias=prm[:, P_BGLU + ot:P_BGLU + ot + 1])
            tt("dve", yss[:, ot, 0:n], ysT[:, ot, 0:n], t_[:, 0:n], ALU.mult, ["ysT", tk], ["yss"])

    def pooling(segs, n, wtot, first_main):
        W = wtot
        for sg in segs:
            cp("pool", up[:, :, sg["po"] - HP:sg["po"]], phist[:, :, sg["sid"], :], ["phist"], ["up"])
        tt("dve", b2[:, :, 1:W], up[:, :, 1:W], up[:, :, 0:W - 1], ALU.add, ["up"], ["b2"])
        tt("pool", b4[:, :, 3:W], b2[:, :, 3:W], b2[:, :, 1:W - 2], ALU.add, ["b2"], ["b4"])
        tt("dve", b2[:, 1, 7:W], b4[:, 1, 7:W], b4[:, 1, 3:W - 4], ALU.add, ["b4"], ["b2"])
        tt("pool", b4[:, 1, 15:W], b2[:, 1, 15:W], b2[:, 1, 7:W - 8], ALU.add, ["b2"], ["b4"])
        srcs = [(b2, 0, 0, 64, 2), (b4, 0, 64, 128, 4), (b2, 1, 0, 64, 8), (b4, 1, 64, 128, 16)]
        for sg in segs:
            po, c0, sn = sg["po"], sg["c0"], sg["n"]
            for (bt, tl, p0, p1, w) in srcs:
                stt("dve", dfT[p0:p1, tl, c0:c0 + sn], bt[p0:p1, tl, po:po + sn], 1.0 / w, up[p0:p1, tl, po:po + sn],
                    ALU.mult, ALU.subtract, ["b2", "b4", "up"], ["dfT"])
            if first_main and sg["sid"] == 0:
                for (bt, tl, p0, p1, w) in srcs:
                    t_, tk = tfR.next()
                    tt("dve", t_[p0:p1, 0:16], bt[p0:p1, tl, po + 8:po + 24], invc[p0:p1, tl * 16:(tl + 1) * 16], ALU.mult,
                       ["b2", "b4", "cst"], [tk])
                    tt("dve", dfT[p0:p1, tl, c0 + 8:c0 + 24], t_[p0:p1, 0:16], up[p0:p1, tl, po + 8:po + 24], ALU.subtract,
                       [tk, "up"], ["dfT"])
            cp("pool", phist[:, :, sg["sid"], :], up[:, :, po + sn - HP:po + sn], ["up"], ["phist"])
        for tl in range(2):
            p_, pk = pmR.next()
            mm(p_[:, 0:n], wpl[:, tl, :], dfT[:, tl, 0:n], True, True, ["wpl", "dfT"], [pk])
            ts("dve", ypT[:, tl, 0:n], p_[:, 0:n], prm[:, P_PSC + tl:P_PSC + tl + 1], None, ALU.mult, None, [pk, "prm"], ["ypT"])

    def attention(segs, n):
        def s1(sg, h):
            c0, sn, kv = sg["c0"], sg["n"], sg["kv"]
            hq, r0 = h // 2, (h % 2) * 64
            pts = []
            for mc in range(2):
                p_, pk = pmR.next()
                mm(p_[:, 0:sn], KT[r0:r0 + 64, kv, hq, mc * 128:(mc + 1) * 128], qT[r0:r0 + 64, hq, c0:c0 + sn], True, True,
                   ["KT", "qT"], [pk])
                t_, tk = PTR.next()
                act(t_[:, 0:sn], p_[:, 0:sn], AF.Exp, [pk], [tk], scale=0.125)
                pts.append((t_, tk))
            return pts

        def s2(sg, h, pts):
            c0, sn, kv = sg["c0"], sg["n"], sg["kv"]
            po_, pok = pmR.next()
            pd_, pdk = pmR.next()
            for mc in range(2):
                mm(po_[0:64, 0:sn], Vb[:, kv, mc, h * 64:(h + 1) * 64], pts[mc][0][:, 0:sn], mc == 0, mc == 1,
                   ["Vb", pts[mc][1]], [pok])
            for mc in range(2):
                mm(pd_[0:64, 0:sn], onesb[:, 0:64], pts[mc][0][:, 0:sn], mc == 0, mc == 1, ["onesb", pts[mc][1]], [pdk])
            E("dve", [pdk], ["rden"], lambda e, pd_=pd_, sn=sn: e.reciprocal(out=rden[:, 0:sn], in_=pd_[0:64, 0:sn]))
            tt("dve", ymT[:, h, c0:c0 + sn], po_[0:64, 0:sn], rden[:, 0:sn], ALU.mult, [pok, "rden"], ["ymT"])

        items = [(sg, h) for sg in segs for h in range(4)]
        pend = None
        for (sg, h) in items:
            pts = s1(sg, h)
            if pend is not None:
                s2(*pend)
            pend = (sg, h, pts)
        s2(*pend)

    def merge_branch(b, role, n):
        brdefs = {
            0: (["brs0", "brs1"], 4, 512, 128, lambda kc: yss[:, kc, 0:n], ["yss"]),
            1: (["brp"], 2, 1024, 128, lambda kc: ypT[:, kc, 0:n], ["ypT"]),
            2: (["brm"], 4, 1024, 64, lambda kc: ymT[:, kc, 0:n], ["ymT"]),
        }
        brn, kcn, bcols, bp, rfn, rkeys = brdefs[b]
        hT, hk_, gT, gk_ = HB["h"], HB["hk"], HB["g"], HB["gk"]
        for hf in range(2):
            gwt, gwk = wget(f"gate{b}{hf}")
            gv = wview(gwt, 8, 512)
            for o4 in range(4):
                ot = hf * 4 + o4
                pg, pgk = proj(gv, gwk, range(8), slice(o4 * 128, (o4 + 1) * 128), lambda kc: hT[:, kc, 0:n], [hk_], n)
                act(gT[:, ot, 0:n], pg[:, 0:n], AF.Sigmoid, [pgk, "prm"], [gk_],
                    bias=prm[:, P_BGATE + b * 8 + ot:P_BGATE + b * 8 + ot + 1])
        for bi, nm in enumerate(brn):
            bwt, bwk = wget(nm)
            bv = wview(bwt, kcn, bcols, bp)
            ots = range(bi * 4, bi * 4 + 4) if len(brn) == 2 else range(8)
            for ot in ots:
                csel = slice((ot % 4) * 128, (ot % 4 + 1) * 128) if len(brn) == 2 else slice(ot * 128, (ot + 1) * 128)
                pb, pbk = proj(bv, bwk, range(kcn), csel, rfn, rkeys, n)
                if role == "first":
                    tt("dve", m2T[:, ot, 0:n], pb[:, 0:n], gT[:, ot, 0:n], ALU.mult, [pbk, gk_], ["m2T"])
                else:
                    f_, fk = tfR.next()
                    tt("dve", f_[:, 0:n], pb[:, 0:n], gT[:, ot, 0:n], ALU.mult, [pbk, gk_], [fk])
                    if role == "mid":
                        tt("dve", m2T[:, ot, 0:n], m2T[:, ot, 0:n], f_[:, 0:n], ALU.add, ["m2T", fk], ["m2T"])
                    else:
                        tt("dve", mgT[:, ot, 0:n], m2T[:, ot, 0:n], f_[:, 0:n], ALU.add, ["m2T", fk], ["mgT"])

    def merge_out(n):
        for hf in range(2):
            wt_, wk_ = wget(f"out{hf}")
            wv = wview(wt_, 8, 512)
            for o4 in range(4):
                ot = hf * 4 + o4
                p_, pk = proj(wv, wk_, range(8), slice(o4 * 128, (o4 + 1) * 128), lambda kc: mgT[:, kc, 0:n], ["mgT"], n)
                cp("act", m2T[:, ot, 0:n], p_[:, 0:n], [pk], ["m2T"])

    def resid_norm(n, gofs, xb, xk):
        norm_stats(m2T, "m2T", n)
        for ot in range(8):
            f_, fk = tfR.next()
            stt("dve", f_[:, 0:n], m2T[:, ot, 0:n], prm[:, gofs + ot:gofs + ot + 1], rstd[:, 0:n], ALU.mult, ALU.mult,
                ["m2T", "prm", "rstd"], [fk])
            tt("dve", xb[:, ot, 0:n], xb[:, ot, 0:n], f_[:, 0:n], ALU.add, [xk, fk], [xk])

    def ffn_up(segs, n, first_main, xb, xk):
        norm_stats(xb, xk, n)
        make_h(xb, xk, n, 1)
        hT, hk_ = HB["h"], HB["hk"]
        pend = []

        def stage_b(j, pg, pgk, cb, cbk):
            t_, tk = tbR.next()
            for sg in segs:
                ao, c0, sn = sg["ao"], sg["c0"], sg["n"]
                act(t_[:, c0:c0 + sn], cb[:, ao:ao + sn], AF.Gelu, [cbk], [tk])
            tt("dve", actT[:, j, 0:n], pg[:, 0:n], t_[:, 0:n], ALU.mult, [pgk, tk], ["actT"])

        for q in range(11):
            wt_, wk_ = wget(f"up{q}")
            wv = wview(wt_, 8, 512)
            for jj in range(2):
                j = 2 * q + jj
                pa, pak = proj(wv, wk_, range(8), slice(jj * 128, (jj + 1) * 128), lambda kc: hT[:, kc, 0:n], [hk_], n)
                pg, pgk = proj(wv, wk_, range(8), slice(256 + jj * 128, 256 + (jj + 1) * 128), lambda kc: hT[:, kc, 0:n], [hk_], n)
                ab, cb, abk, cbk = abR.next()
                wtot = segs[-1]["ao"] + segs[-1]["n"]
                for sg in segs:
                    ao, c0, sn = sg["ao"], sg["c0"], sg["n"]
                    cp("act", ab[:, ao:ao + sn], pa[:, c0:c0 + sn], [pak], [abk])
                    cp("act", ab[:, ao - HC:ao], ahist[:, j, sg["sid"], :], ["ahist"], [abk])
                    if first_main and sg["sid"] == 0:
                        act(ab[:, ao:ao + 8], ab[:, ao:ao + 8], AF.Copy, [abk, "cst"], [abk], scale=hflag)
                    cp("act", ahist[:, j, sg["sid"], :], ab[:, ao + sn - HC:ao + sn], [abk], ["ahist"])
                w0 = prm[:, P_WDW + j * 3 + 0:P_WDW + j * 3 + 1]
                w1 = prm[:, P_WDW + j * 3 + 1:P_WDW + j * 3 + 2]
                w2 = prm[:, P_WDW + j * 3 + 2:P_WDW + j * 3 + 3]
                bd = prm[:, P_BDW + j:P_BDW + j + 1]
                act(cb[:, HC:wtot], ab[:, HC:wtot], AF.Identity, [abk, "prm"], [cbk], bias=bd, scale=w2)
                stt("dve", cb[:, HC:wtot], ab[:, HC - 1:wtot - 1], w1, cb[:, HC:wtot], ALU.mult, ALU.add, [abk, cbk, "prm"], [cbk])
                stt("dve", cb[:, HC:wtot], ab[:, HC - 2:wtot - 2], w0, cb[:, HC:wtot], ALU.mult, ALU.add, [abk, cbk, "prm"], [cbk])
                if pend:
                    stage_b(*pend.pop())
                pend.append((j, pg, pgk, cb, cbk))
        stage_b(*pend.pop())

    def ffn_down(n, xb, xk):
        for ot in range(8):
            wt_, wk_ = wget(f"dn{ot}")
            wv = wview(wt_, NFF, 128)
            p_, pk = proj(wv, wk_, range(NFF), slice(0, 128), lambda kc: actT[:, kc, 0:n], ["actT"], n)
            cp("act", m2T[:, ot, 0:n], p_[:, 0:n], [pk], ["m2T"])
        resid_norm(n, P_GP2, xb, xk)

    xbufs = [(xT, "xT"), (wstage[:, :].rearrange("p (a b) -> p a b", a=8), "wstage")]
    loaded = set()

    def load_x(d):
        if d["id"] in loaded:
            return
        loaded.add(d["id"])
        xb, xk = xbufs[d["par"]]
        for (src_ap, c0, sn) in d["xsrc"]:
            dma(xb[:, :, c0:c0 + sn], src_ap, "xld", [], [xk])

    def front(d, nxt=None, part="all"):
        kind, segs, n = d["kind"], d["segs"], d["n"]
        xb, xk = xbufs[d["par"]]
        use_h(d["par"])
        hT, hk_ = HB["h"], HB["hk"]
        if part in ("all", "A1"):
            load_x(d)
            if nxt is not None and kind == "prefix":
                load_x(nxt)
            norm_stats(xb, xk, n)
            make_h(xb, xk, n, 0)
            if part == "A1":
                return
        wt_, wk_ = wget("in0")
        wv = wview(wt_, 8, 512)
        for ot in range(4):
            p_, pk = proj(wv, wk_, range(8), slice(ot * 128, (ot + 1) * 128), lambda kc: hT[:, kc, 0:n], [hk_], n)
            cp("act", zs[:, ot, 0:n], p_[:, 0:n], [pk], ["zs"])
        if kind != "prefix" or d["last_prefix"]:
            wt_, wk_ = wget("in1")
            wv = wview(wt_, 8, 512)
            for ot in range(4):
                if kind == "prefix" and ot >= 2:
                    break
                p_, pk = proj(wv, wk_, range(8), slice(ot * 128, (ot + 1) * 128), lambda kc: hT[:, kc, 0:n], [hk_], n)
                if ot < 2:
                    for sg in segs:
                        cp("dve", up[:, ot, sg["po"]:sg["po"] + sg["n"]], p_[:, sg["c0"]:sg["c0"] + sg["n"]], [pk], ["up"])
                else:
                    cp("dve", qT[:, ot - 2, 0:n], p_[:, 0:n], [pk], ["qT"])
        if kind != "prefix":
            pooling(segs, n, segs[-1]["po"] + segs[-1]["n"], d["first_main"])
        s5_front(segs, n, prefix=(kind == "prefix"))
        if kind == "prefix" and d["last_prefix"]:
            sg = segs[0]
            cp("pool", phist[:, :, 0, :], up[:, :, sg["po"] + n - HP:sg["po"] + n], ["up"], ["phist"])

    def back_a1(d, nxt=None):
        segs, n = d["segs"], d["n"]
        use_h(d["par"])
        if nxt is not None:
            load_x(nxt)
        attention(segs, n)
        merge_branch(1, "first", n)
        merge_branch(2, "mid", n)
        wg_, wgk_ = wget("glu")
        s5_back(segs, n, wg_, wgk_)
        merge_branch(0, "last", n)
        merge_out(n)

    def back_a2(d):
        segs, n = d["segs"], d["n"]
        xb, xk = xbufs[d["par"]]
        use_h(d["par"])
        resid_norm(n, P_GP1, xb, xk)
        ffn_up(segs, n, d["first_main"], xb, xk)

    def back_b(d):
        n = d["n"]
        xb, xk = xbufs[d["par"]]
        ffn_down(n, xb, xk)
        for (dst_ap, c0, sn) in d["ydst"]:
            dma(dst_ap, xb[:, :, c0:c0 + sn], "yst", [xk], [])

    def drive(descs):
        loaded.clear()
        pre = [d for d in descs if d["kind"] == "prefix"]
        mains = [d for d in descs if d["kind"] != "prefix"]
        npre = len(pre)

        def a1(i):
            front(pre[i], pre[i + 1] if i + 1 < npre else mains[0], part="A1")

        if npre:
            a1(0)
            if npre > 1:
                a1(1)
            front(pre[0], part="A2")
            for i in range(1, npre):
                if i + 1 < npre:
                    a1(i + 1)
                prefix_wsum()
                front(pre[i], part="A2")
            prefix_wsum()
        front(mains[0])
        for j, d in enumerate(mains):
            nxt = mains[j + 1] if j + 1 < len(mains) else None
            back_a1(d, nxt)
            if nxt is not None:
                front(nxt)
            back_a2(d)
            back_b(d)

    identb = sb("identb", [128, 128], BF16)
    cp("dve", identb[:], ident, ["cst"], ["identb"])
    descs = []
    col = 0
    ycol = 0
    for p in range(NPRE + M):
        kind = "prefix" if p < NPRE else "main"
        descs.append(dict(id=p, kind=kind, par=p % 2, n=N, last_prefix=(p == NPRE - 1), first_main=(p == NPRE),
                          segs=[dict(sid=0, kv=0, c0=0, n=N, slot=0, po=HP, ao=HC)],
                          xsrc=[(xT_p[:, :, col:col + N], 0, N)],
                          ydst=[(yT_p[:, :, ycol:ycol + N], 0, N)] if kind == "main" else []))
        col += N
        if kind == "main":
            ycol += N
    fsegs = []
    c0 = slot = po = ao = 0
    for sid, sn in ((0, L), (1, 16), (2, 16)):
        po += HP
        ao += HC
        fsegs.append(dict(sid=sid, kv=sid, c0=c0, n=sn, slot=slot, po=po, ao=ao))
        c0 += sn
        slot += sn // L + 1
        po += sn
        ao += sn
    descs.append(dict(id=NPRE + M, kind="main", par=(NPRE + M) % 2, n=NF, last_prefix=False, first_main=False, segs=fsegs,
                      xsrc=[(xT_p[:, :, col:col + L], 0, L), (xT_s, L, 32)],
                      ydst=[(yT_p[:, :, ycol:ycol + L], 0, L), (yT_s, L, 32)]))
    wuse.append("kv")
    PLAN[0] = True
    R.enabled = False
    drive(descs)
    R.enabled = True
    PLAN[0] = False


    stage(4)
    dma(xT[:, :, 0:256], memT, "xld", [], ["xT"])
    norm_stats(xT, "xT", 256)
    stage(4.1)
    make_h(xT, "xT", 256, 2)
    stage(4.2)
    wt_, wk_ = wget("kv")
    stage(4.3)
    wv = wview(wt_, 8, 512)
    for hq in range(2):
        p_, pk = proj(wv, wk_, range(8), slice(hq * 128, (hq + 1) * 128), lambda kc: hT[:, kc, 0:256], ["hT"], 256)
        stage(4.31)
        cp("act", KT[:, 0, hq, :], p_[:, 0:256], [pk], ["KT"])
        stage(4.32)
        cp("dve", stg[:, hq, :], p_[:, 0:256], [pk], ["stg"])
        stage(4.33)
    dma(kout, stg[:], "o_k", ["stg"], [])
    stage(4.4)
    for mc in range(2):
        p_, pk = pmR.next()
        for kc in range(8):
            mm(p_[:, 0:256], hT[:, kc, mc * 128:(mc + 1) * 128], wv(kc, slice(256, 512)), kc == 0, kc == 7, ["hT", wk_], [pk])
        cp("act", Vb[:, 0, mc, :], p_[:, 0:256], [pk], ["Vb"])
        cp("dve", stg[:, mc, :], p_[:, 0:256], [pk], ["stg"])
    dma(vout, stg[:], "o_v", ["stg"], [])
    stage(4.5)
    for s in range(2):
        dma(stg[:], kTs[s], "c7", [], ["stg"])
        cp("dve", KT[:, 1 + s].rearrange("p a b -> p (a b)"), stg[:].rearrange("p a b -> p (a b)"), ["stg"], ["KT"])
        dma(stg[:], vs_in[s], "c7", [], ["stg"])
        cp("dve", Vb[:, 1 + s].rearrange("p a b -> p (a b)"), stg[:].rearrange("p a b -> p (a b)"), ["stg"], ["Vb"])

    stage(5)
    drive(descs)
    R.enabled = True
    dma(sfin, Sst[:], "o_s", ["Sst"], [])
    dma(poolo, phist[:], "o_p", ["phist"], [])
    dma(convo, ahist[:], "o_c", ["ahist"], [])
    R.final_wait("sp")

    sems = {name: st.enter_context(nc.semaphore(name)) for name in sorted(R.sems)}
    with nc.Block() as block:
        def mk(stream):
            def body(eng):
                for it in stream:
                    if it[0] == "w":
                        eng.wait_ge(sems[it[1]], it[2])
                    else:
                        ins = it[1](eng)
                        if it[2] is not None:
                            ins.then_inc(sems[it[2]], it[3])
            return body

        block.tensor(mk(R.streams["pe"]))
        block.scalar(mk(R.streams["act"]))
        block.vector(mk(R.streams["dve"]))
        block.gpsimd(mk(R.streams["pool"]))
        block.sync(mk(R.streams["sp"]))
    st.close()
    return nc


def fm(a):
    T, F = a.shape
    return np.ascontiguousarray(a.T.reshape(F // 128, 128, T).transpose(1, 0, 2))


def unfm(a):
    P, KC, T = a.shape
    return np.ascontiguousarray(a.transpose(1, 0, 2).reshape(KC * P, T).T)


def prep_inputs(inp, M):
    OWN = N * M
    SEQ = OWN * CPS
    f32 = np.float32
    g = lambda k: np.asarray(inp[k], f32)[0]
    W = {k: g(k) for k in ("w_in", "w_glu", "w_gate", "w_br_ssm", "w_br_pool", "w_br_mem", "w_out", "w_ffn_up", "w_ffn_down")}
    W["w_kv"] = np.concatenate([g("w_mem_k"), g("w_mem_v")], axis=1)
    wsrc = np.stack([unit_host(u, W) for u in UNITS])
    wp = g("w_pool")
    wpool = np.zeros((128, 2, 128), f32)
    for gi in range(4):
        tl, hf = gi // 2, gi % 2
        wpool[hf * 64:(hf + 1) * 64, tl, hf * 64:(hf + 1) * 64] = wp[gi]

    def pp(v):
        return v.reshape(-1, 128).T

    prm = np.concatenate([
        pp(g("b_gate")), pp(g("b_glu")), pp(g("pool_scale")), pp(g("g_post1")), pp(g("g_post2")),
        g("w_dw").T.reshape(NFF, 128, 3).transpose(1, 0, 2).reshape(128, 66), pp(g("b_dw")),
        np.tile(g("ssm_d").reshape(32, 16).T, (8, 1)),
        pp(g("g_pre1")), pp(g("g_pre2")), pp(g("g_mem")),
    ], axis=1).astype(f32)
    s5p = np.concatenate([
        g("ssm_lam_re").T, g("ssm_lam_im").T, np.tile(g("ssm_log_dt")[None, :], (64, 1)),
        g("ssm_b_re").transpose(1, 0, 2).reshape(64, 512), g("ssm_b_im").transpose(1, 0, 2).reshape(64, 512),
        g("ssm_c_re").transpose(2, 0, 1).reshape(64, 512), g("ssm_c_im").transpose(2, 0, 1).reshape(64, 512),
    ], axis=1).astype(f32)
    ident = np.eye(128, dtype=f32)
    ii = np.arange(128) // 16
    mask = (ii[None, :] >= ii[:, None]).astype(f32)
    xp = np.asarray(inp["x_prompt"], f32)
    xs = np.asarray(inp["x_sample"], f32)
    memp = np.asarray(inp["mem_prompt"], f32)
    ck = np.asarray(inp["cache_mem_k"], f32)[0]
    cv = np.asarray(inp["cache_mem_v"], f32)[0]
    sre = np.asarray(inp["state_ssm_re"], f32)[0]
    sim = np.asarray(inp["state_ssm_im"], f32)[0]
    spool = np.asarray(inp["state_pool"], f32)[0]
    sconv = np.asarray(inp["state_conv"], f32)[0]
    NPREC = (CPS - 1) * OWN
    maps = []
    for c in range(NCORES):
        b, k = c // CPS, c % CPS
        t0 = k * OWN
        idx = np.arange(t0 - L - NPREC, t0 + OWN)
        stream = np.zeros((idx.size, D), f32)
        v = idx >= 0
        stream[v] = xp[b, idx[v]]
        wins = np.array([2, 4, 8, 16], f32)
        pos = t0 + np.arange(16)
        invc = np.zeros((128, 2, 16), f32)
        for gi in range(4):
            tl, hf = gi // 2, gi % 2
            invc[hf * 64:(hf + 1) * 64, tl, :] = 1.0 / np.minimum(pos + 1, wins[gi])
        cst = np.concatenate([ident, mask, invc.reshape(128, 32), np.full((128, 1), 0.0 if k == 0 else 1.0, f32)], axis=1)
        ss = [2 * c, 2 * c + 1]
        m = {
            "xT_p": fm(stream),
            "xT_s": fm(xs[ss].reshape(32, D)),
            "memT": fm(memp[b]),
            "kTs": np.stack([ck[s].reshape(256, 256).T.reshape(2, 128, 256).transpose(1, 0, 2) for s in ss]),
            "vs": np.stack([cv[s].reshape(2, 128, 256).transpose(1, 0, 2) for s in ss]),
            "sst": np.stack([np.stack([sre[s].T, sim[s].T], axis=-1) for s in ss], axis=1),
            "ph": np.stack([spool[s].T.reshape(2, 128, HP).transpose(1, 0, 2) for s in ss], axis=2),
            "ch": np.stack([sconv[s].T.reshape(NFF, 128, HC).transpose(1, 0, 2) for s in ss], axis=2),
            "wsrc": wsrc, "wpool": wpool, "prm": prm, "s5p": s5p, "cst": cst.astype(f32),
        }
        maps.append({kk: np.ascontiguousarray(vv, dtype=f32) for kk, vv in m.items()})
    return maps


def assemble(res, M):
    OWN = N * M
    SEQ = OWN * CPS
    f32 = np.float32
    yp = np.zeros((2, SEQ, D), f32)
    ys = np.zeros((16, 16, D), f32)
    mk = np.zeros((1, 2, 256, 4, 64), f32)
    mv = np.zeros((1, 2, 256, 4, 64), f32)
    srp = np.zeros((1, 2, 32, 64), f32); sip = np.zeros((1, 2, 32, 64), f32)
    srs = np.zeros((1, 16, 32, 64), f32); sis = np.zeros((1, 16, 32, 64), f32)
    pp_ = np.zeros((1, 2, HP, 256), f32); ps_ = np.zeros((1, 16, HP, 256), f32)
    cp_ = np.zeros((1, 2, HC, DFF), f32); cs_ = np.zeros((1, 16, HC, DFF), f32)
    for c in range(NCORES):
        r = res[c]
        b, k = c // CPS, c % CPS
        y = unfm(np.asarray(r["yT_p"], f32))
        yp[b, k * OWN:(k + 1) * OWN] = y[L:L + OWN]
        ysm = unfm(np.asarray(r["yT_s"], f32))
        ys[2 * c] = ysm[0:16]
        ys[2 * c + 1] = ysm[16:32]
        sf = np.asarray(r["sfin"], f32)
        po = np.asarray(r["poolo"], f32)
        co = np.asarray(r["convo"], f32)

        def pool_of(sid):
            return po[:, :, sid, :].transpose(1, 0, 2).reshape(256, HP).T

        def conv_of(sid):
            return co[:, :, sid, :].transpose(1, 0, 2).reshape(DFF, HC).T

        if k == 0:
            kt = np.asarray(r["kout"], f32)
            mk[0, b] = kt.transpose(1, 0, 2).reshape(256, 256).T.reshape(256, 4, 64)
            vt = np.asarray(r["vout"], f32)
            mv[0, b] = vt.transpose(1, 0, 2).reshape(256, 4, 64)
        if k == CPS - 1:
            srp[0, b] = sf[:, 0, :, 0].T
            sip[0, b] = sf[:, 0, :, 1].T
            pp_[0, b] = pool_of(0)
            cp_[0, b] = conv_of(0)
        for j, s in enumerate((2 * c, 2 * c + 1)):
            srs[0, s] = sf[:, 1 + j, :, 0].T
            sis[0, s] = sf[:, 1 + j, :, 1].T
            ps_[0, s] = pool_of(1 + j)
            cs_[0, s] = conv_of(1 + j)
    return (yp, ys, mk, mv, srp, sip, srs, sis, pp_, ps_, cp_, cs_)


_CACHE = {}


def kernel(**inputs):
    M = 8
    maps = prep_inputs(inputs, M)
    if M not in _CACHE:
        _CACHE[M] = build(M)
    res = run_bass_kernel_spmd(_CACHE[M], maps, core_ids=list(range(NCORES)))
    return assemble(res.results, M)
```
